# Optimizing a Trainium2 kernel written in Bass

```python
import jax, jax.numpy as jnp
from jax import lax
import numpy as np

D_MODEL = 1024
BATCH = 2
SEQ = 8192
DEPTH = 2

CHUNK = 64
D_MIX = D_MODEL
D_RET = D_MIX // 2
D_SB = D_MIX - D_RET
RET_HEADS = 4
RET_HEAD_DIM = D_RET // RET_HEADS
SB_HEADS = 8
SB_HEAD_DIM = D_SB // SB_HEADS
Q_BLOCK = 128
ROPE_BASE = 10000.0
EPS = 1e-6
ADA_SCALE = 0.2
SPLIT_SIZES = [D_RET] * 4 + [D_SB] * 4
D_IN = sum(SPLIT_SIZES)

kernel_name = "hybrid_retention_stickbreaking_block"


def rms_norm(x, g):
    xf = x.astype(jnp.float32)
    return xf * lax.rsqrt(jnp.mean(xf * xf, axis=-1, keepdims=True) + EPS) * g.astype(jnp.float32)


def split_heads(x, n_heads):
    b, s, _ = x.shape
    return x.reshape(b, s, n_heads, -1).transpose(0, 2, 1, 3)


def merge_heads(x):
    b, h, s, d = x.shape
    return x.transpose(0, 2, 1, 3).reshape(b, s, h * d)


def rotary(x, pos):
    half = x.shape[-1] // 2
    inv = ROPE_BASE ** (-jnp.arange(half, dtype=jnp.float32) / half)
    ang = pos[:, None] * inv[None, :]
    cos, sin = jnp.cos(ang), jnp.sin(ang)
    x1, x2 = x[..., :half], x[..., half:]
    return jnp.concatenate([x1 * cos - x2 * sin, x1 * sin + x2 * cos], axis=-1)


def retention(q, k, v):
    b, h, s, d = q.shape
    n = s // CHUNK
    log_gamma = jnp.log1p(-(2.0 ** (-5.0 - jnp.arange(h, dtype=jnp.float32))))
    idx = jnp.arange(CHUNK, dtype=jnp.float32)
    dmask = jnp.exp(jnp.abs(idx[:, None] - idx[None, :])[None] * log_gamma[:, None, None])
    q_dec = jnp.exp((idx + 1.0)[None, :] * log_gamma[:, None])
    k_dec = jnp.exp((CHUNK - 1.0 - idx)[None, :] * log_gamma[:, None])
    chunk_dec = jnp.exp(CHUNK * log_gamma)

    k = k * (d ** -0.5)
    qc = q.reshape(b, h, n, CHUNK, d)
    kc = k.reshape(b, h, n, CHUNK, d)
    vc = v.reshape(b, h, n, CHUNK, v.shape[-1])

    scores = jnp.einsum('bhncd,bhnmd->bhncm', qc, kc) * dmask[None, :, None]
    o_inner = jnp.einsum('bhncm,bhnme->bhnce', scores, vc)

    kv = jnp.einsum('bhnmd,bhnme->nbhde', kc * k_dec[None, :, None, :, None], vc)

    def step(state, kv_i):
        return state * chunk_dec[None, :, None, None] + kv_i, state

    init = jnp.zeros((b, h, d, v.shape[-1]), dtype=kv.dtype)
    _, prev = lax.scan(step, init, kv)
    o_cross = jnp.einsum('bhncd,nbhde->bhnce', qc, prev) * q_dec[None, :, None, :, None]

    o = (o_inner + o_cross).reshape(b, h, s, v.shape[-1])
    mu = jnp.mean(o, axis=-1, keepdims=True)
    var = jnp.mean(jnp.square(o - mu), axis=-1, keepdims=True)
    return (o - mu) * lax.rsqrt(var + EPS)


def stick_breaking(q, k, v):
    b, h, s, d = q.shape
    nb = s // Q_BLOCK
    scale = d ** -0.5
    q_blocks = q.reshape(b, h, nb, Q_BLOCK, d).transpose(2, 0, 1, 3, 4)
    key_pos = jnp.arange(s)

    def block(args):
        qi, bi = args
        t = bi * Q_BLOCK + jnp.arange(Q_BLOCK)
        mask = key_pos[None, :] < t[:, None]
        z = jnp.einsum('bhqd,bhsd->bhqs', qi, k) * scale
        log_beta = jax.nn.log_sigmoid(z)
        log_keep = jnp.where(mask, jax.nn.log_sigmoid(-z), 0.0)
        suffix = lax.cumsum(log_keep, axis=3, reverse=True) - log_keep
        a = jnp.where(mask, jnp.exp(log_beta + suffix), 0.0)
        return jnp.einsum('bhqs,bhse->bhqe', a, v)

    out = lax.map(block, (q_blocks, jnp.arange(nb)))
    return out.transpose(1, 2, 0, 3, 4).reshape(b, h, s, v.shape[-1])


def hybrid_layer(x, c_act, norm_g, w_ada, b_ada, w_in, w_out, pos):
    mod = c_act @ w_ada + b_ada
    shift, scale, gate = jnp.split(mod, 3, axis=-1)
    h = rms_norm(x, norm_g) * (1.0 + scale[:, None, :]) + shift[:, None, :]
    proj = h @ w_in
    split_at = [int(i) for i in np.cumsum(SPLIT_SIZES)[:-1]]
    rq, rk, rv, rg, sq, sk, sv, sg = jnp.split(proj, split_at, axis=-1)

    y_ret = retention(rotary(split_heads(rq, RET_HEADS), pos),
                      rotary(split_heads(rk, RET_HEADS), pos),
                      split_heads(rv, RET_HEADS))
    y_ret = merge_heads(y_ret) * jax.nn.silu(rg)

    y_sb = stick_breaking(split_heads(sq, SB_HEADS), split_heads(sk, SB_HEADS),
                          split_heads(sv, SB_HEADS))
    y_sb = merge_heads(y_sb) * jax.nn.silu(sg)

    y = jnp.concatenate([y_ret, y_sb], axis=-1) @ w_out
    return x + gate[:, None, :] * y


def setup_inputs(seed: int = 0) -> dict:
    key = jax.random.key(seed)
    ks = jax.random.split(key, 8)
    f32 = jnp.float32
    x = jax.random.normal(ks[0], (BATCH, SEQ, D_MODEL), f32)
    c = jax.random.normal(ks[1], (BATCH, D_MODEL), f32)
    norm_g = 1.0 + 0.02 * jax.random.normal(ks[2], (DEPTH, D_MODEL), f32)
    w_ada = jax.random.normal(ks[3], (DEPTH, D_MODEL, 3 * D_MODEL), f32) * (ADA_SCALE * D_MODEL ** -0.5)
    b_ada = 0.02 * jax.random.normal(ks[4], (DEPTH, 3 * D_MODEL), f32)
    w_in = jax.random.normal(ks[5], (DEPTH, D_MODEL, D_IN), f32) * (D_MODEL ** -0.5)
    w_out = jax.random.normal(ks[6], (DEPTH, D_MIX, D_MODEL), f32) * (D_MIX ** -0.5)
    final_g = 1.0 + 0.02 * jax.random.normal(ks[7], (D_MODEL,), f32)
    return {"x": x, "c": c, "norm_g": norm_g, "w_ada": w_ada, "b_ada": b_ada,
            "w_in": w_in, "w_out": w_out, "final_g": final_g}


def reference(x, c, norm_g, w_ada, b_ada, w_in, w_out, final_g):
    pos = jnp.arange(x.shape[1], dtype=jnp.float32)
    c_act = jax.nn.silu(c.astype(jnp.float32))
    h = x.astype(jnp.float32)
    for layer in range(DEPTH):
        h = hybrid_layer(h, c_act, norm_g[layer], w_ada[layer], b_ada[layer],
                         w_in[layer], w_out[layer], pos)
    return rms_norm(h, final_g).astype(x.dtype)
```

```python
import contextlib
import numpy as np
import ml_dtypes
import concourse.bass as bass
import concourse.mybir as mybir
from concourse.bass_utils import run_bass_kernel_spmd

F32 = mybir.dt.float32
BF16 = mybir.dt.bfloat16
AF = mybir.ActivationFunctionType
ALU = mybir.AluOpType

D = 1024
B = 2
S = 8192
DEPTH = 2
NCORES = 8
TQ = S // 4
NT = 512
EPS = 1e-6
BIG = 32768.0
SAME_ENGINE_WAITS = True
MODE = "FUSED"
DBG = dict(ntile2a=None, do2c=True, nsteps=None, ret=True, stop=99)
POOL = "dve"


class Src:
    def __init__(self, sem, name):
        self.sem = sem
        self.count = 0
        self.name = name


class Buf:
    __slots__ = ("w", "r", "name", "excl")

    def __init__(self, name="", excl=False):
        self.w = None
        self.r = {}
        self.name = name
        self.excl = excl


class KB:
    ENGS = ("pe", "act", "dve", "pool", "sp")

    def __init__(self, nc, es):
        self.nc = nc
        self.es = es
        self.q = {n: [] for n in self.ENGS}
        self.src = {}
        for n in self.ENGS:
            self.src[n] = Src(es.enter_context(nc.semaphore("sem_" + n)), n)
        self.waited = {n: {} for n in self.ENGS}
        self.streams = []
        self.nops = 0

    def stream(self, name):
        s = Src(self.es.enter_context(self.nc.semaphore("dq_" + name)), name)
        self.streams.append(s)
        return s

    def _waits(self, eng, reads, writes):
        need = {}
        for b in reads:
            if b.w is not None:
                s, c = b.w
                if need.get(s, 0) < c:
                    need[s] = c
        for b in writes:
            if b.w is not None:
                s, c = b.w
                if need.get(s, 0) < c:
                    need[s] = c
            for s, c in b.r.items():
                if need.get(s, 0) < c:
                    need[s] = c
        out = []
        me = self.src[eng]
        wd = self.waited[eng]
        for s, c in need.items():
            if s is me and (eng == "pe" or eng == "sp" or not SAME_ENGINE_WAITS):
                continue
            if wd.get(s, 0) < c:
                wd[s] = c
                out.append((s.sem, c))
        return out

    def op(self, eng, meth, reads=(), writes=(), args=(), **kw):
        excl = [b for b in reads if b.excl]
        if excl:
            reads = [b for b in reads if not b.excl]
            writes = list(writes) + excl
        ws = self._waits(eng, reads, writes)
        me = self.src[eng]
        me.count += 1
        cnt = me.count
        sem = me.sem

        def thunk(h):
            for s, c in ws:
                h.wait_ge(s, c)
            getattr(h, meth)(*args, **kw).then_inc(sem, 1)

        self.q[eng].append(thunk)
        self.nops += 1
        for b in reads:
            if b.r.get(me, 0) < cnt:
                b.r[me] = cnt
        for b in writes:
            b.w = (me, cnt)
            b.r = {}

    def mm(self, out, lhsT, rhs, start, stop, reads, writes):
        self.op("pe", "matmul", reads, writes, args=(out,), lhsT=lhsT, rhs=rhs, start=start, stop=stop)

    def act(self, out, in_, func, reads, writes, **kw):
        self.op("act", "activation", reads, writes, out=out, in_=in_, func=func, **kw)

    def dma(self, out, in_, reads=(), writes=(), stream=None, eng="sp"):
        ws = self._waits(eng, reads, writes)
        stream.count += 16
        cnt = stream.count
        sem = stream.sem

        def thunk(h):
            for s, c in ws:
                h.wait_ge(s, c)
            src_ap = in_(h) if callable(in_) else in_
            h.dma_start(out=out, in_=src_ap).then_inc(sem, 16)

        self.q[eng].append(thunk)
        for b in reads:
            if b.r.get(stream, 0) < cnt:
                b.r[stream] = cnt
        for b in writes:
            b.w = (stream, cnt)
            b.r = {}

    def cc(self, kind, groups, in_ap, out_ap, reads=(), writes=()):
        ws = self._waits("pool", reads, writes)
        st = Src(self.es.enter_context(self.nc.semaphore("cc%d" % len(self.streams))), "cc")
        self.streams.append(st)
        st.count = 1
        sem = st.sem

        def thunk(h):
            for s, c in ws:
                h.wait_ge(s, c)
            h.collective_compute(kind, ALU.bypass, replica_groups=groups, ins=[in_ap], outs=[out_ap]).then_inc(sem, 1)

        self.q["pool"].append(thunk)
        for b in reads:
            b.r[st] = 1
        for b in writes:
            b.w = (st, 1)
            b.r = {}

    def barrier(self):
        allsrc = [self.src[n] for n in self.ENGS] + self.streams
        for eng in self.ENGS:
            ws = []
            wd = self.waited[eng]
            for s in allsrc:
                if s is self.src[eng] and eng == "sp":
                    continue
                if s.count > 0 and wd.get(s, 0) < s.count:
                    wd[s] = s.count
                    ws.append((s.sem, s.count))
            if ws:
                def thunk(h, ws=ws):
                    for s, c in ws:
                        h.wait_ge(s, c)
                self.q[eng].append(thunk)

    def final_wait(self, streams):
        ws = [(s.sem, s.count) for s in streams if s.count > 0]

        def thunk(h):
            for s, c in ws:
                h.wait_ge(s, c)
        self.q["sp"].append(thunk)

    def replay(self):
        nc = self.nc
        with nc.Block() as block:
            @block.tensor
            def _(h):
                for t in self.q["pe"]:
                    t(h)

            @block.scalar
            def _(h):
                for t in self.q["act"]:
                    t(h)

            @block.vector
            def _(h):
                for t in self.q["dve"]:
                    t(h)

            @block.gpsimd
            def _(h):
                for t in self.q["pool"]:
                    t(h)

            @block.sync
            def _(h):
                for t in self.q["sp"]:
                    t(h)


class Arena:
    def __init__(self, handle, nbytes):
        self.h = handle
        self.n = nbytes
        self.off = 0

    def alloc(self, dtype, *free):
        esz = 2 if dtype == BF16 else 4
        n = 1
        for f in free:
            n *= f
        nb = (n * esz + 31) // 32 * 32
        assert self.off + nb <= self.n, ("SBUF arena overflow", self.off, nb, self.n)
        w0 = self.off // 4
        ap = self.h[:, w0:w0 + nb // 4]
        if dtype == BF16:
            ap = ap.bitcast(BF16)
        ap = ap[:, 0:n]
        self.off += nb
        if len(free) == 2:
            ap = ap.rearrange("p (a b) -> p a b", a=free[0])
        elif len(free) == 3:
            ap = ap.rearrange("p (a b c) -> p a b c", a=free[0], b=free[1])
        return ap


def build(prog):
    fused = prog == "FUSED"
    nc = bass.Bass("TRN2", target_bir_lowering=False)
    es = contextlib.ExitStack()
    kb = KB(nc, es)

    def din(name, shape, dt):
        return nc.dram_tensor(name, list(shape), dt, kind="ExternalInput").ap()

    def dout(name, shape, dt):
        return nc.dram_tensor(name, list(shape), dt, kind="ExternalOutput").ap()

    need_p1 = {"P1_0": [0], "P3_0": [1], "FUSED": [0, 1]}.get(prog, [])
    need_p2 = {"P2_0": [0], "P2_1": [1], "FUSED": [0, 1]}.get(prog, [])
    need_p3 = {"P3_0": [0], "P3_1": [1], "FUSED": [0, 1]}.get(prog, [])
    need_ada = {"P1_0": [0], "P3_0": [0, 1], "P3_1": [1], "FUSED": [0, 1]}.get(prog, [])

    cbf_d = din("cbf", [128, 5, 128], BF16)
    dr = {}
    if need_ada:
        dr["cvec"] = din("cvec", [128, 8], F32)
        dr["normg"] = din("normg", [128, DEPTH, 8], F32)
        dr["finalg"] = din("finalg", [128, 8], F32)
        dr["bada"] = din("bada", [128, DEPTH, 24], F32)
        dr["wada"] = din("wada", [DEPTH, D, 3 * D], F32)
    if need_p3:
        dr["wout"] = din("wout", [DEPTH, D, D], F32)
    if need_p2:
        dr["win"] = din("win", [DEPTH, D, 1024], F32)
        dr["cosT"] = din("cosT", [128, S], F32)
        dr["sinT"] = din("sinT", [128, S], F32)
        dr["negmask"] = din("negmask", [128, 4, NT], BF16)
        dr["retc"] = din("retc", [128, 128 + 2 + NT], F32)
    if prog in ("P1_0", "P3_0", "FUSED"):
        dr["xT"] = din("xT", [D, TQ], F32)
    if prog == "P3_1":
        dr["x1T_in"] = din("x1T_in", [D, TQ], F32)
    if prog == "P3_0":
        dr["x1T_out"] = dout("x1T_out", [D, TQ], F32)
    if prog in ("P1_0", "P3_0"):
        dr["hTq"] = dout("hTq", [D, TQ], BF16)
    if prog in ("P2_0", "P2_1"):
        dr["hTfull"] = din("hTfull", [4 * D, TQ], BF16)
        dr["ymT"] = dout("ymT", [256, S], BF16)
    if prog in ("P3_0", "P3_1"):
        dr["ymTfull"] = din("ymTfull", [4 * 256, TQ], BF16)
    if prog in ("P3_1", "FUSED"):
        dr["outT"] = dout("outT", [D, TQ], F32)

    ARENA_BYTES = 200 * 1024
    arena_h = es.enter_context(nc.sbuf_tensor("arena", [128, ARENA_BYTES // 4], F32))
    ar = Arena(arena_h, ARENA_BYTES)
    banks = [es.enter_context(nc.psum_tensor("bank%d" % i, [128, 512], F32)) for i in range(7)]
    bankT = es.enter_context(nc.psum_tensor("bankT", [128, 1024], BF16))
    BK = [Buf("bank%d" % i, excl=True) for i in range(7)]
    BKT = Buf("bankT", excl=True)

    st_const = kb.stream("const")
    cbf = ar.alloc(BF16, 5, 128)
    CBF = Buf("cbf")
    SMALL = Buf("small")
    kb.dma(cbf, cbf_d, writes=[CBF], stream=st_const)
    ident, negU, negones, ones_bf, perm = (cbf[:, i, :] for i in range(5))

    if need_ada:
        cvec = ar.alloc(F32, 8)
        normg = ar.alloc(F32, DEPTH, 8)
        finalg = ar.alloc(F32, 8)
        bada = ar.alloc(F32, DEPTH, 24)
        for a, d_ in ((cvec, dr["cvec"]), (normg, dr["normg"]), (finalg, dr["finalg"]), (bada, dr["bada"])):
            kb.dma(a, d_, writes=[SMALL], stream=st_const)
        cact = ar.alloc(F32, 8)
        ctmp = ar.alloc(F32, 8)
        mod = ar.alloc(F32, DEPTH, 24)
        gs = ar.alloc(F32, DEPTH, 8)
        MOD = Buf("mod")
        CACT = Buf("cact")
    CBF.w = (st_const, st_const.count)
    SMALL.w = (st_const, st_const.count)

    arena_mark = ar.off
    xt = XT = st_x = sq = SQ = lnv = LNV = rstd = RSTD = ntmp = NTMP = hto = HTO = st_ho = st_out = None
    wout_bf = WOUT = wst = WST = st_w = ymt = YMT = st_ym = None

    def alloc_rl(tag):
        nonlocal xt, XT, st_x, sq, SQ, lnv, LNV, rstd, RSTD, ntmp, NTMP, hto, HTO, st_ho, st_out
        nonlocal wout_bf, WOUT, wst, WST, st_w, ymt, YMT, st_ym
        ar.off = arena_mark
        xt = [ar.alloc(F32, 8, NT) for _ in range(2)]
        XT = [Buf("xt0"), Buf("xt1")]
        st_x = [kb.stream("x0" + tag), kb.stream("x1" + tag)]
        sq = ar.alloc(BF16, 8, NT)
        SQ = Buf("sq")
        lnv = ar.alloc(F32, NT)
        LNV = Buf("lnv")
        rstd = ar.alloc(F32, NT)
        RSTD = Buf("rstd")
        ntmp = [ar.alloc(F32, NT) for _ in range(2)]
        NTMP = [Buf("ntmp0"), Buf("ntmp1")]
        hto = [ar.alloc(BF16, 8, NT) for _ in range(2)]
        HTO = [Buf("hto0"), Buf("hto1")]
        st_ho = [kb.stream("ho0" + tag), kb.stream("ho1" + tag)]
        st_out = [kb.stream("out0" + tag), kb.stream("out1" + tag)]
        if need_p3:
            wout_bf = ar.alloc(BF16, 8, D)
            WOUT = Buf("wout")
            wst = [ar.alloc(F32, D) for _ in range(2)]
            WST = [Buf("wst0"), Buf("wst1")]
            st_w = [kb.stream("w0" + tag), kb.stream("w1" + tag)]
            ymt = [ar.alloc(BF16, 8, NT) for _ in range(2)]
            YMT = [Buf("ymt0"), Buf("ymt1")]
            st_ym = [kb.stream("ym0" + tag), kb.stream("ym1" + tag)]

    if need_ada:
        alloc_rl("a")
    final_streams = []

    def emit_silu_c():
        kb.act(ctmp, cvec, AF.Exp, [SMALL], [CACT], scale=-1.0)
        kb.act(ctmp, ctmp, AF.Ln, [CACT], [CACT], bias=1.0)
        kb.act(ctmp, ctmp, AF.Exp, [CACT], [CACT], scale=-1.0)
        kb.op("dve", "tensor_tensor", [CACT, SMALL], [CACT], out=cact, in0=ctmp, in1=cvec, op=ALU.mult)

    def emit_adaln(l):
        wv = dr["wada"][l].rearrange("(k p) f -> p k f", p=128)
        for grp in range(6):
            sl = grp % 2
            kb.dma(xt[sl], wv[:, :, grp * 512:(grp + 1) * 512], writes=[XT[sl]], stream=st_x[sl])
            for j in range(4):
                fb = grp * 4 + j
                for kc in range(8):
                    kb.mm(banks[0][:, fb:fb + 1], xt[sl][:, kc, j * 128:(j + 1) * 128], cact[:, kc:kc + 1],
                          kc == 0, kc == 7, [XT[sl], CACT], [BK[0]])
        kb.op("dve", "tensor_tensor", [BK[0], SMALL], [MOD], out=mod[:, l, :], in0=banks[0][:, 0:24], in1=bada[:, l, :], op=ALU.add)
        kb.op("dve", "scalar_tensor_tensor", [MOD, SMALL], [MOD], out=gs[:, l, :], in0=mod[:, l, 8:16], scalar=1.0,
              in1=normg[:, l, :], op0=ALU.add, op1=ALU.mult)

    def emit_norm_tile(sl, scale_ap, shift_ap, dst_view, final, after=None, DST=()):
        kb.act(sq, xt[sl], AF.Square, [XT[sl]], [SQ])
        for kc in range(8):
            kb.mm(banks[0][:, :], ones_bf, sq[:, kc, :], kc == 0, kc == 7, [SQ, CBF], [BK[0]])
        kb.act(lnv, banks[0][:, :], AF.Ln, [BK[0]], [LNV], scale=1.0 / D, bias=EPS)
        kb.act(rstd, lnv, AF.Exp, [LNV], [RSTD], scale=-0.5)
        if final:
            for kc in range(8):
                kb.op("dve", "scalar_tensor_tensor", [XT[sl], RSTD, MOD, SMALL], [XT[sl]], out=xt[sl][:, kc, :], in0=xt[sl][:, kc, :],
                      scalar=scale_ap[:, kc:kc + 1], in1=rstd, op0=ALU.mult, op1=ALU.mult)
            kb.dma(dst_view, xt[sl], reads=[XT[sl]], stream=st_out[sl])
        else:
            for kc in range(8):
                tm = kc % 2
                kb.op("dve", "scalar_tensor_tensor", [XT[sl], RSTD, MOD], [NTMP[tm]], out=ntmp[tm], in0=xt[sl][:, kc, :],
                      scalar=scale_ap[:, kc:kc + 1], in1=rstd, op0=ALU.mult, op1=ALU.mult)
                kb.op("pool", "tensor_scalar", [NTMP[tm], MOD], [HTO[sl]], out=hto[sl][:, kc, :], in0=ntmp[tm],
                      scalar1=shift_ap[:, kc:kc + 1], scalar2=None, op0=ALU.add)
            kb.dma(dst_view, hto[sl], reads=[HTO[sl]] + list(DST), stream=st_ho[sl])
            if after is not None:
                after()

    def pkt(dram_ap):
        return dram_ap.rearrange("(k p) t -> p k t", p=128)

    def emit_phase1(l, x_src, h_view, h_after=None, H_DST=None):
        xv = pkt(x_src)
        for tt in range(TQ // NT):
            sl = tt % 2
            kb.dma(xt[sl], xv[:, :, tt * NT:(tt + 1) * NT], writes=[XT[sl]], stream=st_x[sl])
            emit_norm_tile(sl, gs[:, l, :], mod[:, l, 0:8], h_view(tt), final=False,
                           after=(None if h_after is None else (lambda tt=tt: h_after(tt))),
                           DST=([] if H_DST is None else [H_DST[tt]]))

    def emit_phase3(l, x_src, ym_view, x_dst, h_view, out_dst, h_after=None, H_DST=None, YM_SRC=()):
        for j in range(8):
            g_, half = j // 2, j % 2
            r0 = g_ * 128 if half == 0 else 512 + g_ * 128
            sl = j % 2
            kb.dma(wst[sl], dr["wout"][l, r0:r0 + 128, :], writes=[WST[sl]], stream=st_w[sl])
            kb.op("pool", "tensor_copy", [WST[sl]], [WOUT], out=wout_bf[:, j, :], in_=wst[sl])
        xv = pkt(x_src)
        for tt in range(TQ // NT):
            sl = tt % 2
            kb.dma(xt[sl], xv[:, :, tt * NT:(tt + 1) * NT], writes=[XT[sl]], stream=st_x[sl])
            kb.dma(ymt[sl], ym_view(tt), reads=list(YM_SRC), writes=[YMT[sl]], stream=st_ym[sl])
            for fo in range(8):
                bk = 1 + (fo % 2)
                for j in range(8):
                    kb.mm(banks[bk][:, :], wout_bf[:, j, fo * 128:(fo + 1) * 128], ymt[sl][:, j, :], j == 0, j == 7,
                          [WOUT, YMT[sl]], [BK[bk]])
                kb.op("dve", "scalar_tensor_tensor", [BK[bk], MOD, XT[sl]], [XT[sl]], out=xt[sl][:, fo, :], in0=banks[bk][:, :],
                      scalar=mod[:, l, 16 + fo:17 + fo], in1=xt[sl][:, fo, :], op0=ALU.mult, op1=ALU.add)
            if l + 1 < DEPTH:
                kb.dma(pkt(x_dst)[:, :, tt * NT:(tt + 1) * NT], xt[sl], reads=[XT[sl]], stream=st_out[sl])
                emit_norm_tile(sl, gs[:, l + 1, :], mod[:, l + 1, 0:8], h_view(tt), final=False,
                               after=(None if h_after is None else (lambda tt=tt: h_after(tt))),
                               DST=([] if H_DST is None else [H_DST[tt]]))
            else:
                emit_norm_tile(sl, finalg, None, pkt(out_dst)[:, :, tt * NT:(tt + 1) * NT], final=True)

    def emit_phase2(l, h_view, ym_view, H_SRC=None, Y_DST=None, y_after=None):
        ar.off = arena_mark
        win_bf = ar.alloc(BF16, 8, 1024)
        WIN = Buf("win")
        wst2 = [ar.alloc(F32, 1024) for _ in range(2)]
        WST2 = [Buf("wst2_0"), Buf("wst2_1")]
        st_w2 = [kb.stream("w2_0_%d" % l), kb.stream("w2_1_%d" % l)]
        negmask = ar.alloc(BF16, 4, NT)
        retc = ar.alloc(F32, 128 + 2 + NT)
        P2C = Buf("p2c")
        st_c2 = kb.stream("c2_%d" % l)
        kb.dma(negmask, dr["negmask"], writes=[P2C], stream=st_c2)
        kb.dma(retc, dr["retc"], writes=[P2C], stream=st_c2)
        dmaskT = retc[:, 0:128]
        kdec = retc[:, 128:129]
        cdec = retc[:, 129:130]
        qdec = retc[:, 130:130 + NT]

        qT = ar.alloc(BF16, S)
        kTA = ar.alloc(BF16, S)
        kTB = ar.alloc(BF16, S)
        sgT = ar.alloc(BF16, S)
        svA = ar.alloc(BF16, S // 128, 128)
        svB = ar.alloc(BF16, S // 128, 128)
        NTILE = S // NT
        QT = [Buf("qT%d" % i) for i in range(NTILE)]
        KTb = [Buf("kT%d" % i) for i in range(NTILE)]
        SG = [Buf("sg%d" % i) for i in range(NTILE)]
        SV = [Buf("sv%d" % i) for i in range(NTILE)]
        ZERO = Buf("zero")
        kb.op("dve", "memset", [], [ZERO], args=(kTA, 0.0))
        kb.op("dve", "memset", [], [ZERO], args=(kTB, 0.0))
        kb.op("dve", "memset", [], [ZERO], args=(svA, 0.0))
        kb.op("dve", "memset", [], [ZERO], args=(svB, 0.0))

        ht = [ar.alloc(BF16, 8, NT) for _ in range(2)]
        HT = [Buf("ht0"), Buf("ht1")]
        st_h = [kb.stream("h0_%d" % l), kb.stream("h1_%d" % l)]
        cs = [ar.alloc(F32, 2, NT) for _ in range(2)]
        CS = [Buf("cs0"), Buf("cs1")]
        st_cs = [kb.stream("cs0_%d" % l), kb.stream("cs1_%d" % l)]

        xbf = [ar.alloc(BF16, NT) for _ in range(2)]
        XBF = [Buf("xbf0"), Buf("xbf1")]
        t1 = [ar.alloc(F32, NT) for _ in range(2)]
        T1 = [Buf("t1_0"), Buf("t1_1")]
        t2 = [ar.alloc(F32, NT) for _ in range(2)]
        T2 = [Buf("t2_0"), Buf("t2_1")]
        rqT = ar.alloc(BF16, NT)
        rkT = ar.alloc(BF16, NT)
        qdT = ar.alloc(BF16, NT)
        RQ, RK, QD = Buf("rq"), Buf("rk"), Buf("qd")
        sil = ar.alloc(F32, NT)
        SIL = Buf("sil")
        vret = ar.alloc(BF16, 4, 128)
        rgs = ar.alloc(BF16, 4, 128)
        VRET = [Buf("vret%d" % i) for i in range(4)]
        RGS = [Buf("rgs%d" % i) for i in range(4)]
        sil2 = ar.alloc(F32, 128)
        SIL2 = Buf("sil2")
        kd = ar.alloc(BF16, 128)
        KD = Buf("kd")
        Sm = ar.alloc(BF16, 128)
        SM = Buf("Sm")
        NP = 3
        Pf = [ar.alloc(F32, 128) for _ in range(NP)]
        Pb = [ar.alloc(BF16, 128) for _ in range(NP)]
        PF = [Buf("Pf%d" % i) for i in range(NP)]
        PB = [Buf("Pb%d" % i) for i in range(NP)]
        stats = ar.alloc(F32, 6)
        mv = ar.alloc(F32, 2)
        gsm = ar.alloc(F32, 4)
        GN = Buf("gn")
        on = ar.alloc(F32, 128)
        ON = Buf("on")
        ybf = ar.alloc(BF16, 128)
        YBF = Buf("ybf")
        yrT = [ar.alloc(BF16, NT) for _ in range(2)]
        YRT = [Buf("yrT0"), Buf("yrT1")]
        st_yr = [kb.stream("yr0_%d" % l), kb.stream("yr1_%d" % l)]

        for kc in range(8):
            sl = kc % 2
            kb.dma(wst2[sl], dr["win"][l, kc * 128:(kc + 1) * 128, :], writes=[WST2[sl]], stream=st_w2[sl])
            kb.op(POOL, "tensor_copy", [WST2[sl]], [WIN], out=win_bf[:, kc, :], in_=wst2[sl])

        kb.op(POOL, "memset", [], [PF[0]], args=(Pf[0], 0.0))
        kb.op(POOL, "memset", [], [PB[0]], args=(Pb[0], 0.0))
        pst = 0


        def silu_from_psum(src_ap, SRC, tmp, TMP, dst_ap, DST):
            kb.act(tmp, src_ap, AF.Exp, [SRC], [TMP], scale=-1.0)
            kb.act(tmp, tmp, AF.Ln, [TMP], [TMP], bias=1.0)
            kb.act(tmp, tmp, AF.Exp, [TMP], [TMP], scale=-1.0)
            kb.op("dve", "tensor_tensor", [SRC, TMP], [DST], out=dst_ap, in0=src_ap, in1=tmp, op=ALU.mult)

        for tt in range(NTILE if DBG['ntile2a'] is None else DBG['ntile2a']):
            sl = tt % 2
            r, tq = tt // 4, (tt % 4) * NT
            c0 = tt * NT
            kb.dma(ht[sl], h_view(tt), reads=([] if H_SRC is None else [H_SRC[tt % 4]]), writes=[HT[sl]], stream=st_h[sl])
            kb.dma(cs[sl][:, 0, :], dr["cosT"][:, c0:c0 + NT], writes=[CS[sl]], stream=st_cs[sl])
            kb.dma(cs[sl][:, 1, :], dr["sinT"][:, c0:c0 + NT], writes=[CS[sl]], stream=st_cs[sl])

            def proj_fm(blk, bk):
                for kc in range(8):
                    kb.mm(banks[bk][:, :], win_bf[:, kc, blk * 128:(blk + 1) * 128], ht[sl][:, kc, :], kc == 0, kc == 7,
                          [WIN, HT[sl]], [BK[bk]])

            for which, (dstT, DST) in enumerate(((rqT, RQ), (rkT, RK))):
                bk = which
                proj_fm(which, bk)
                kb.act(xbf[which], banks[bk][:, :], AF.Identity, [BK[bk]], [XBF[which]])
                kb.mm(banks[2][:, :], perm, xbf[which], True, True, [XBF[which], CBF], [BK[2]])
                kb.op("dve", "tensor_tensor", [BK[bk], CS[sl]], [T1[which]], out=t1[which], in0=banks[bk][:, :], in1=cs[sl][:, 0, :], op=ALU.mult)
                kb.op("dve", "tensor_tensor", [BK[2], CS[sl]], [T2[which]], out=t2[which], in0=banks[2][:, :], in1=cs[sl][:, 1, :], op=ALU.mult)
                kb.op(POOL, "tensor_tensor", [T1[which], T2[which]], [DST], out=dstT, in0=t1[which], in1=t2[which], op=ALU.add)
                if which == 0:
                    kb.op("dve", "tensor_tensor", [T1[0], T2[0]], [T1[0]], out=t1[0], in0=t1[0], in1=t2[0], op=ALU.add)
                    kb.op(POOL, "tensor_tensor", [T1[0], P2C], [QD], out=qdT, in0=t1[0], in1=qdec, op=ALU.mult)
            proj_fm(2, 0)
            kb.act(qT[:, c0:c0 + NT], banks[0][:, :], AF.Identity, [BK[0]], [QT[tt]])
            proj_fm(3, 1)
            kb.act(kTA[0:64, c0:c0 + NT], banks[1][0:64, :], AF.Identity, [BK[1], ZERO], [KTb[tt]], scale=0.125)
            kb.act(kTB[64:128, c0:c0 + NT], banks[1][64:128, :], AF.Identity, [BK[1], ZERO], [KTb[tt]], scale=0.125)
            proj_fm(4, 0)
            silu_from_psum(banks[0][:, :], BK[0], sil, SIL, sgT[:, c0:c0 + NT], SG[tt])

            for st in range(4):
                gt = tt * 4 + st
                for kc in range(8):
                    kb.mm(banks[3][:, 0:384], ht[sl][:, kc, st * 128:(st + 1) * 128], win_bf[:, kc, 640:1024], kc == 0, kc == 7,
                          [WIN, HT[sl]], [BK[3]])
                kb.op("dve", "tensor_copy", [BK[3]], [VRET[st]], out=vret[:, st, :], in_=banks[3][:, 0:128])
                kb.op("dve", "tensor_copy", [BK[3], ZERO], [SV[tt]], out=svA[:, gt, 0:64], in_=banks[3][:, 256:320])
                kb.op("dve", "tensor_copy", [BK[3], ZERO], [SV[tt]], out=svB[:, gt, 64:128], in_=banks[3][:, 320:384])
                silu_from_psum(banks[3][:, 128:256], BK[3], sil2, SIL2, rgs[:, st, :], RGS[st])

                cs0 = st * 128
                kb.op("pe", "transpose", [RK, CBF], [BKT], args=(bankT[:, 0:128], rkT[:, cs0:cs0 + 128], ident))
                kb.op("dve", "tensor_scalar", [BKT, P2C], [KD], out=kd, in0=bankT[:, 0:128], scalar1=kdec, scalar2=None, op0=ALU.mult)
                kb.mm(banks[4][:, 0:128], rkT[:, cs0:cs0 + 128], rqT[:, cs0:cs0 + 128], True, True, [RK, RQ], [BK[4]])
                kb.op("dve", "tensor_tensor", [BK[4], P2C], [SM], out=Sm, in0=banks[4][:, 0:128], in1=dmaskT, op=ALU.mult)
                kb.mm(banks[5][:, 0:128], kd[0:64, :], vret[0:64, st, :], True, True, [KD, VRET[st]], [BK[5]])
                kb.mm(banks[6][:, 0:128], kd[64:128, :], vret[64:128, st, :], True, True, [KD, VRET[st]], [BK[6]])
                p0 = pst
                p1 = (p0 + 1) % NP
                p2 = (p0 + 2) % NP
                kb.op("dve", "scalar_tensor_tensor", [PF[p0], BK[5], P2C], [PF[p1]], out=Pf[p1], in0=Pf[p0], scalar=cdec,
                      in1=banks[5][:, 0:128], op0=ALU.mult, op1=ALU.add)
                kb.op(POOL, "tensor_copy", [PF[p1]], [PB[p1]], out=Pb[p1], in_=Pf[p1])
                kb.op("dve", "scalar_tensor_tensor", [PF[p1], BK[6], P2C], [PF[p2]], out=Pf[p2], in0=Pf[p1], scalar=cdec,
                      in1=banks[6][:, 0:128], op0=ALU.mult, op1=ALU.add)
                kb.op(POOL, "tensor_copy", [PF[p2]], [PB[p2]], out=Pb[p2], in_=Pf[p2])
                pst = p2
                kb.mm(banks[4][:, 128:256], Sm, vret[:, st, :], True, False, [SM, VRET[st]], [BK[4]])
                kb.mm(banks[4][0:64, 128:256], qdT[:, cs0:cs0 + 64], Pb[p0], False, True, [QD, PB[p0]], [BK[4]])
                kb.mm(banks[4][64:128, 128:256], qdT[:, cs0 + 64:cs0 + 128], Pb[p1], False, True, [QD, PB[p1]], [BK[4]])
                kb.op("dve", "bn_stats", [BK[4]], [GN], out=stats, in_=banks[4][:, 128:256])
                kb.op("dve", "bn_aggr", [GN], [GN], out=mv, in_=stats)
                kb.act(gsm[:, 0:1], mv[:, 1:2], AF.Ln, [GN], [GN], bias=EPS)
                kb.act(gsm[:, 1:2], gsm[:, 0:1], AF.Exp, [GN], [GN], scale=-0.5)
                kb.op("dve", "scalar_tensor_tensor", [GN], [GN], out=gsm[:, 2:3], in0=mv[:, 0:1], scalar=-1.0, in1=gsm[:, 1:2],
                      op0=ALU.mult, op1=ALU.mult)
                kb.act(on, banks[4][:, 128:256], AF.Identity, [BK[4], GN], [ON], bias=gsm[:, 2:3], scale=gsm[:, 1:2])
                kb.op(POOL, "tensor_tensor", [ON, RGS[st]], [YBF], out=ybf, in0=on, in1=rgs[:, st, :], op=ALU.mult)
                kb.op("pe", "transpose", [YBF, CBF], [BKT], args=(bankT[:, 128:256], ybf, ident))
                kb.op("dve", "tensor_copy", [BKT], [YRT[sl]], out=yrT[sl][:, cs0:cs0 + 128], in_=bankT[:, 128:256])
            kb.dma(ym_view(tt, 0), yrT[sl], reads=[YRT[sl]] + ([] if Y_DST is None else [Y_DST[tt // 4]]), stream=st_yr[sl])

        NE, NSP, NA = 4, 4, 3
        Eb = [ar.alloc(F32, NT) for _ in range(NE)]
        EB = [Buf("E%d" % i) for i in range(NE)]
        SPb = [ar.alloc(BF16, NT) for _ in range(NSP)]
        SPB = [Buf("SP%d" % i) for i in range(NSP)]
        Ab = [ar.alloc(BF16, NT) for _ in range(NA)]
        AB = [Buf("A%d" % i) for i in range(NA)]
        Rb = [[ar.alloc(BF16, NT) for _ in range(2)] for _ in range(2)]
        RB = [[Buf("R%d%d" % (i, j)) for j in range(2)] for i in range(2)]
        ysb = [ar.alloc(BF16, NT) for _ in range(2)]
        YSB = [Buf("ysb0"), Buf("ysb1")]
        st_ys = [kb.stream("ys0_%d" % l), kb.stream("ys1_%d" % l)]
        kTs = (kTA, kTB)
        svs = (svA, svB)

        steps = []
        for qi in range(NTILE):
            blocks = [(4 * qi + o4, o4) for o4 in (3, 2, 1, 0)] + [(kbk, None) for kbk in range(4 * qi - 1, -1, -1)]
            nb = len(blocks)
            for k, (kbk, o4) in enumerate(blocks):
                for hd in range(2):
                    steps.append(dict(qi=qi, hd=hd, k=k, kbk=kbk, o4=o4, first=(k == 0), last=(k == nb - 1)))
        n = len(steps) if DBG['nsteps'] is None else DBG['nsteps']
        if not DBG['do2c']:
            return st_yr

        def emit_Z(s):
            stp = steps[s]
            zb = s % 2
            q0 = stp["qi"] * NT
            kT_ = kTs[stp["hd"]]
            kbk = stp["kbk"]
            diag = stp["o4"] is not None
            kb.mm(banks[zb][:, :], kT_[:, kbk * 128:(kbk + 1) * 128], qT[:, q0:q0 + NT], True, not diag,
                  [KTb[kbk // 4], QT[stp["qi"]], ZERO], [BK[zb]])
            if diag:
                kb.mm(banks[zb][:, :], ident, negmask[:, stp["o4"], :], False, True, [CBF, P2C], [BK[zb]])

        def emit_E(s):
            zb = s % 2
            kb.act(Eb[s % NE], banks[zb][:, :], AF.Exp, [BK[zb]], [EB[s % NE]])

        def emit_SP(s):
            kb.act(SPb[s % NSP], Eb[s % NE], AF.Ln, [EB[s % NE]], [SPB[s % NSP]], bias=1.0)

        def emit_R(s):
            stp = steps[s]
            if stp["last"]:
                return
            hd, k = stp["hd"], stp["k"]
            if stp["first"]:
                kb.op(POOL, "tensor_copy", [SPB[s % NSP]], [RB[hd][1]], out=Rb[hd][1], in_=SPb[s % NSP])
            else:
                kb.op(POOL, "tensor_tensor", [RB[hd][k % 2], SPB[s % NSP]], [RB[hd][(k + 1) % 2]], out=Rb[hd][(k + 1) % 2],
                      in0=Rb[hd][k % 2], in1=SPb[s % NSP], op=ALU.add)

        def emit_PA(s):
            stp = steps[s]
            pb = 2 + (s % 2)
            hd, k = stp["hd"], stp["k"]
            kb.mm(banks[pb][:, :], negU, SPb[s % NSP], True, stp["first"], [CBF, SPB[s % NSP]], [BK[pb]])
            if not stp["first"]:
                kb.mm(banks[pb][:, :], negones, Rb[hd][k % 2], False, True, [CBF, RB[hd][k % 2]], [BK[pb]])

        def emit_A(s):
            pb = 2 + (s % 2)
            kb.act(banks[pb][:, :], banks[pb][:, :], AF.Exp, [BK[pb]], [BK[pb]])
            kb.op("dve", "tensor_tensor", [BK[pb], EB[s % NE]], [AB[s % NA]], out=Ab[s % NA], in0=banks[pb][:, :], in1=Eb[s % NE], op=ALU.mult)

        def emit_AV(s):
            stp = steps[s]
            qi, hd, kbk = stp["qi"], stp["hd"], stp["kbk"]
            ob = 4 + (qi % 2)
            first = stp["first"] and hd == 0
            last = stp["last"] and hd == 1
            kb.mm(banks[ob][:, :], svs[hd][:, kbk, :], Ab[s % NA], first, last, [SV[kbk // 4], ZERO, AB[s % NA]], [BK[ob]])
            if last:
                ys = qi % 2
                q0 = qi * NT
                kb.op("dve", "tensor_tensor", [BK[ob], SG[qi]], [YSB[ys]], out=ysb[ys], in0=banks[ob][:, :], in1=sgT[:, q0:q0 + NT], op=ALU.mult)
                kb.dma(ym_view(qi, 1), ysb[ys], reads=[YSB[ys]] + ([] if Y_DST is None else [Y_DST[qi // 4]]), stream=st_ys[ys])
                if y_after is not None and qi % 4 == 3:
                    y_after(qi // 4)

        emit_Z(0)
        emit_E(0)
        emit_Z(1)
        for s in range(0, n + 1):
            if s < n:
                emit_SP(s)
                emit_R(s)
                emit_PA(s)
            if 0 <= s - 1 < n:
                emit_A(s - 1)
                emit_AV(s - 1)
            if s + 1 < n:
                emit_E(s + 1)
            if s + 2 < n:
                emit_Z(s + 2)
        return st_yr + st_ys

    def hview_unfused(dram):
        hv = dram.rearrange("(r k p) t -> p r k t", p=128, k=8)
        return lambda tt: hv[:, tt // 4, :, (tt % 4) * NT:(tt % 4 + 1) * NT]

    def yview_unfused(dram):
        return lambda tile, half: dram[half * 128:(half + 1) * 128, tile * NT:(tile + 1) * NT]

    if prog == "P1_0":
        emit_silu_c()
        emit_adaln(0)
        emit_phase1(0, dr["xT"], lambda tt: pkt(dr["hTq"])[:, :, tt * NT:(tt + 1) * NT])
        final_streams += st_ho
    elif prog in ("P2_0", "P2_1"):
        l = int(prog[-1])
        final_streams += emit_phase2(l, hview_unfused(dr["hTfull"]), yview_unfused(dr["ymT"]))
    elif prog == "P3_0":
        emit_silu_c()
        emit_adaln(0)
        emit_adaln(1)
        ymv = dr["ymTfull"].rearrange("(j p) t -> p j t", p=128)
        emit_phase3(0, dr["xT"], lambda tt: ymv[:, :, tt * NT:(tt + 1) * NT], dr["x1T_out"],
                    lambda tt: pkt(dr["hTq"])[:, :, tt * NT:(tt + 1) * NT], None)
        final_streams += st_ho + st_out
    elif prog == "P3_1":
        emit_silu_c()
        emit_adaln(1)
        ymv = dr["ymTfull"].rearrange("(j p) t -> p j t", p=128)
        emit_phase3(1, dr["x1T_in"], lambda tt: ymv[:, :, tt * NT:(tt + 1) * NT], None, None, dr["outT"])
        final_streams += st_out
    elif fused:
        groups = [[0, 1, 2, 3], [4, 5, 6, 7]]
        x1T = nc.dram_tensor("x1T_scr", [D, TQ], F32).ap()
        hsrc = [[nc.dram_tensor("hsrc_%d_%d" % (l, t), [D, NT], BF16) for t in range(4)] for l in range(DEPTH)]
        hgat = [[nc.dram_tensor("hgat_%d_%d" % (l, t), [4 * D, NT], BF16) for t in range(4)] for l in range(DEPTH)]
        ysrc = [[nc.dram_tensor("ysrc_%d_%d" % (l, q), [256, TQ], BF16) for q in range(4)] for l in range(DEPTH)]
        ygat = [nc.dram_tensor("ygat_%d" % l, [4 * 1024, TQ], BF16) for l in range(DEPTH)]
        HS = [[Buf("hs") for _ in range(4)] for _ in range(DEPTH)]
        HG = [[Buf("hg") for _ in range(4)] for _ in range(DEPTH)]
        YS = [[Buf("ys") for _ in range(4)] for _ in range(DEPTH)]
        YG = [Buf("yg") for _ in range(DEPTH)]
        rank_cache = {}

        def myrank(h):
            if id(h) not in rank_cache:
                rank_cache[id(h)] = h.partition_id() % 4
            return rank_cache[id(h)]

        def h_view_dst(l):
            return lambda tt: pkt(hsrc[l][tt].ap())

        def h_after(l):
            def f(tt):
                kb.cc("AllGather", groups, hsrc[l][tt].ap().opt(), hgat[l][tt].ap().opt(), reads=[], writes=[HS[l][tt], HG[l][tt]])
            return f

        def h_view_src(l):
            def f(tt):
                hv = hgat[l][tt % 4].ap().rearrange("(r k p) t -> p r k t", p=128, k=8)
                return hv[:, tt // 4, :, :]
            return f

        def y_view_dst(l):
            return lambda tile, half: ysrc[l][tile // 4].ap()[half * 128:(half + 1) * 128, (tile % 4) * NT:(tile % 4 + 1) * NT]

        def y_after(l):
            def f(q):
                kb.cc("AllGather", groups, ysrc[l][q].ap().opt(), ygat[l].ap()[q * 1024:(q + 1) * 1024, :].opt(),
                      reads=[], writes=[YS[l][q], YG[l]])
            return f

        def y_view_src(l):
            def f(tt):
                def g(h):
                    v = ygat[l].ap().rearrange("(qj p) t -> p qj t", p=128)
                    return v[:, bass.ds(myrank(h) * 8, 8), tt * NT:(tt + 1) * NT]
                return g
            return f

        emit_silu_c()
        emit_adaln(0)
        emit_adaln(1)
        emit_phase1(0, dr["xT"], h_view_dst(0), h_after(0), HS[0])
        for l in range(DEPTH):
            kb.barrier()
            emit_phase2(l, h_view_src(l), y_view_dst(l), H_SRC=HG[l], Y_DST=YS[l], y_after=y_after(l))
            kb.barrier()
            alloc_rl("p3_%d" % l)
            if l + 1 < DEPTH:
                emit_phase3(l, dr["xT"] if l == 0 else x1T, y_view_src(l), x1T, h_view_dst(l + 1), None,
                            h_after=h_after(l + 1), H_DST=HS[l + 1], YM_SRC=[YG[l]])
            else:
                emit_phase3(l, x1T, y_view_src(l), None, None, dr["outT"], YM_SRC=[YG[l]])
        final_streams += st_out
    else:
        raise NotImplementedError(prog)

    kb.final_wait(final_streams)
    kb.replay()
    return nc, es


def _bf(a):
    return np.asarray(a, dtype=np.float32).astype(ml_dtypes.bfloat16)


def _consts():
    ident = np.eye(128, dtype=np.float32)
    jj, ss = np.meshgrid(np.arange(128), np.arange(128), indexing="ij")
    negU = np.where(jj >= ss, -1.0, 0.0).astype(np.float32)
    negones = -np.ones((128, 128), np.float32)
    ones = np.ones((128, 128), np.float32)
    perm = np.zeros((128, 128), np.float32)
    for d in range(128):
        perm[(d + 64) % 128, d] = 1.0
    cbf = _bf(np.stack([ident, negU, negones, ones, perm], axis=1))
    i = np.arange(128)[:, None, None]
    o4 = np.arange(4)[None, :, None]
    j = np.arange(NT)[None, None, :]
    negmask = _bf(np.where(j > o4 * 128 + i, 0.0, -BIG))
    half = 64
    inv = (10000.0 ** (-(np.arange(half, dtype=np.float32) / np.float32(half)))).astype(np.float32)
    pos = np.arange(S, dtype=np.float32)
    ang = (pos[None, :] * inv[:, None]).astype(np.float32)
    cos = np.cos(ang).astype(np.float32)
    sin = np.sin(ang).astype(np.float32)
    cosT = np.concatenate([cos, cos], axis=0)
    sinT = np.concatenate([-sin, sin], axis=0)
    return cbf, negmask, np.ascontiguousarray(cosT), np.ascontiguousarray(sinT)


def _retc(g):
    lg = np.log1p(-(2.0 ** (-5.0 - g)))
    m = np.arange(128)
    same = (m[:, None] // 64) == (m[None, :] // 64)
    dm = np.where(same, np.exp(np.abs(m[:, None] - m[None, :]) * lg), 0.0) * (128.0 ** -0.5)
    kdec = np.exp((63.0 - (m % 64)) * lg) * (128.0 ** -0.5)
    cdec = np.full(128, np.exp(64.0 * lg))
    c = np.arange(NT)
    qdec = np.broadcast_to(np.exp(((c % 64) + 1.0) * lg)[None, :], (128, NT))
    return np.ascontiguousarray(np.concatenate([dm, kdec[:, None], cdec[:, None], qdec], axis=1).astype(np.float32))


def _vec(v):
    return np.ascontiguousarray(np.asarray(v, np.float32).reshape(-1, 128).T)


_NC_CACHE = {}


def _get(prog):
    if prog not in _NC_CACHE:
        _NC_CACHE[prog] = build(prog)
    return _NC_CACHE[prog][0]


def _run(prog, in_maps):
    nc = _get(prog)
    res = run_bass_kernel_spmd(nc, in_maps, core_ids=list(range(NCORES)))
    return res.results


def kernel(x, c, norm_g, w_ada, b_ada, w_in, w_out, final_g):
    x = np.asarray(x, np.float32)
    c = np.asarray(c, np.float32)
    norm_g = np.asarray(norm_g, np.float32)
    w_ada = np.ascontiguousarray(np.asarray(w_ada, np.float32))
    b_ada = np.asarray(b_ada, np.float32)
    w_in = np.asarray(w_in, np.float32)
    w_out = np.ascontiguousarray(np.asarray(w_out, np.float32))
    final_g = np.asarray(final_g, np.float32)

    cbf, negmask, cosT, sinT = _consts()
    normg_l = np.ascontiguousarray(np.stack([_vec(norm_g[l]) for l in range(DEPTH)], axis=1))
    bada_l = np.ascontiguousarray(np.stack([np.asarray(b_ada[l]).reshape(24, 128).T for l in range(DEPTH)], axis=1))
    finalg_l = _vec(final_g)

    base = []
    for core in range(NCORES):
        b, g = core // 4, core % 4
        cols = np.concatenate([
            np.arange(g * 128, (g + 1) * 128),
            512 + np.arange(g * 128, (g + 1) * 128),
            2048 + np.arange(g * 128, (g + 1) * 128),
            2560 + np.arange(g * 128, (g + 1) * 128),
            3584 + np.arange(g * 128, (g + 1) * 128),
            1024 + np.arange(g * 128, (g + 1) * 128),
            1536 + np.arange(g * 128, (g + 1) * 128),
            3072 + np.arange(g * 128, (g + 1) * 128),
        ])
        base.append(dict(
            b=b, g=g,
            cbf=cbf, negmask=negmask, cosT=cosT, sinT=sinT, retc=_retc(g),
            cvec=_vec(c[b]), normg=normg_l, finalg=finalg_l, bada=bada_l, wada=w_ada, wout=w_out,
            win=np.ascontiguousarray(w_in[:, :, cols]),
            xT=np.ascontiguousarray(x[b, g * TQ:(g + 1) * TQ, :].T),
        ))

    def pick(core, names):
        return {k: base[core][k] for k in names}

    P13 = ["cbf", "cvec", "normg", "finalg", "bada", "wada"]
    P2 = ["cbf", "win", "cosT", "sinT", "negmask", "retc"]

    def gather_h(res):
        out = []
        for core in range(NCORES):
            b = core // 4
            out.append(np.ascontiguousarray(np.concatenate([res[b * 4 + r]["hTq"] for r in range(4)], axis=0)))
        return out

    def gather_ym(res):
        out = []
        for core in range(NCORES):
            b, g = core // 4, core % 4
            full = np.concatenate([res[b * 4 + r]["ymT"] for r in range(4)], axis=0)
            out.append(np.ascontiguousarray(full[:, g * TQ:(g + 1) * TQ]))
        return out

    if MODE == "FUSED":
        names = ["cbf", "cvec", "normg", "finalg", "bada", "wada", "wout", "win", "cosT", "sinT", "negmask", "retc", "xT"]
        res = _run("FUSED", [pick(i, names) for i in range(NCORES)])
        out = np.empty((B, S, D), np.float32)
        for core in range(NCORES):
            b, g = core // 4, core % 4
            out[b, g * TQ:(g + 1) * TQ, :] = res[core]["outT"].T
        return out

    r1 = _run("P1_0", [dict(pick(i, P13 + ["xT"])) for i in range(NCORES)])
    hfull = gather_h(r1)
    r2 = _run("P2_0", [dict(pick(i, P2), hTfull=hfull[i]) for i in range(NCORES)])
    ymfull = gather_ym(r2)
    r3 = _run("P3_0", [dict(pick(i, P13 + ["xT", "wout"]), ymTfull=ymfull[i]) for i in range(NCORES)])
    hfull = gather_h(r3)
    r4 = _run("P2_1", [dict(pick(i, P2), hTfull=hfull[i]) for i in range(NCORES)])
    ymfull = gather_ym(r4)
    r5 = _run("P3_1", [dict(pick(i, P13 + ["wout"]), ymTfull=ymfull[i], x1T_in=r3[i]["x1T_out"]) for i in range(NCORES)])

    out = np.empty((B, S, D), np.float32)
    for core in range(NCORES):
        b, g = core // 4, core % 4
        out[b, g * TQ:(g + 1) * TQ, :] = r5[core]["outT"].T
    return out
```

```python
import contextlib
import numpy as np
import ml_dtypes
import concourse.bass as bass
import concourse.mybir as mybir
from concourse.bass_utils import run_bass_kernel_spmd

F32 = mybir.dt.float32
BF16 = mybir.dt.bfloat16
AF = mybir.ActivationFunctionType
ALU = mybir.AluOpType

D = 1024
B = 2
S = 8192
DEPTH = 2
NCORES = 8
TQ = S // 4
NT = 512
EPS = 1e-6
BIG = 32768.0
SAME_ENGINE_WAITS = True
INTERLEAVE = True
MODE = "FUSED"
DBG = dict(ntile2a=None, do2c=True, nsteps=None, ret=True, stop=99)
POOL = "dve"


class Src:
    def __init__(self, sem, name):
        self.sem = sem
        self.count = 0
        self.name = name


class Buf:
    __slots__ = ("w", "r", "name", "excl")

    def __init__(self, name="", excl=False):
        self.w = None
        self.r = {}
        self.name = name
        self.excl = excl


class KB:
    ENGS = ("pe", "act", "dve", "pool", "sp")

    def __init__(self, nc, es):
        self.nc = nc
        self.es = es
        self.q = {n: [] for n in self.ENGS}
        self.src = {}
        for n in self.ENGS:
            self.src[n] = Src(es.enter_context(nc.semaphore("sem_" + n)), n)
        self.waited = {n: {} for n in self.ENGS}
        self.streams = []
        self.nops = 0
        self.capture = None

    def stream(self, name):
        s = Src(self.es.enter_context(self.nc.semaphore("dq_" + name)), name)
        self.streams.append(s)
        return s

    def _waits(self, eng, reads, writes):
        need = {}
        for b in reads:
            if b.w is not None:
                s, c = b.w
                if need.get(s, 0) < c:
                    need[s] = c
        for b in writes:
            if b.w is not None:
                s, c = b.w
                if need.get(s, 0) < c:
                    need[s] = c
            for s, c in b.r.items():
                if need.get(s, 0) < c:
                    need[s] = c
        out = []
        me = self.src[eng]
        wd = self.waited[eng]
        for s, c in need.items():
            if s is me and (eng == "pe" or eng == "sp" or not SAME_ENGINE_WAITS):
                continue
            if wd.get(s, 0) < c:
                wd[s] = c
                out.append((s.sem, c))
        return out

    def op(self, eng, meth, reads=(), writes=(), args=(), **kw):
        if self.capture is not None:
            self.capture.append(("op", (eng, meth, list(reads), list(writes), args), kw))
            return
        excl = [b for b in reads if b.excl]
        if excl:
            reads = [b for b in reads if not b.excl]
            writes = list(writes) + excl
        ws = self._waits(eng, reads, writes)
        me = self.src[eng]
        me.count += 1
        cnt = me.count
        sem = me.sem

        def thunk(h):
            for s, c in ws:
                h.wait_ge(s, c)
            getattr(h, meth)(*args, **kw).then_inc(sem, 1)

        self.q[eng].append(thunk)
        self.nops += 1
        for b in reads:
            if b.r.get(me, 0) < cnt:
                b.r[me] = cnt
        for b in writes:
            b.w = (me, cnt)
            b.r = {}

    def mm(self, out, lhsT, rhs, start, stop, reads, writes):
        self.op("pe", "matmul", reads, writes, args=(out,), lhsT=lhsT, rhs=rhs, start=start, stop=stop)

    def act(self, out, in_, func, reads, writes, **kw):
        self.op("act", "activation", reads, writes, out=out, in_=in_, func=func, **kw)

    def dma(self, out, in_, reads=(), writes=(), stream=None, eng="sp"):
        if self.capture is not None:
            self.capture.append(("dma", (out, in_, list(reads), list(writes), stream, eng), {}))
            return
        ws = self._waits(eng, reads, writes)
        stream.count += 16
        cnt = stream.count
        sem = stream.sem

        def thunk(h):
            for s, c in ws:
                h.wait_ge(s, c)
            src_ap = in_(h) if callable(in_) else in_
            h.dma_start(out=out, in_=src_ap).then_inc(sem, 16)

        self.q[eng].append(thunk)
        for b in reads:
            if b.r.get(stream, 0) < cnt:
                b.r[stream] = cnt
        for b in writes:
            b.w = (stream, cnt)
            b.r = {}

    def play(self, item):
        kind, a, kw = item
        assert self.capture is None
        if kind == "op":
            self.op(a[0], a[1], a[2], a[3], a[4], **kw)
        else:
            self.dma(a[0], a[1], a[2], a[3], a[4], a[5])

    def cc(self, kind, groups, in_ap, out_ap, reads=(), writes=()):
        ws = self._waits("pool", reads, writes)
        st = Src(self.es.enter_context(self.nc.semaphore("cc%d" % len(self.streams))), "cc")
        self.streams.append(st)
        st.count = 1
        sem = st.sem

        def thunk(h):
            for s, c in ws:
                h.wait_ge(s, c)
            h.collective_compute(kind, ALU.bypass, replica_groups=groups, ins=[in_ap], outs=[out_ap]).then_inc(sem, 1)

        self.q["pool"].append(thunk)
        for b in reads:
            b.r[st] = 1
        for b in writes:
            b.w = (st, 1)
            b.r = {}

    def barrier(self):
        allsrc = [self.src[n] for n in self.ENGS] + self.streams
        for eng in self.ENGS:
            ws = []
            wd = self.waited[eng]
            for s in allsrc:
                if s is self.src[eng] and eng == "sp":
                    continue
                if s.count > 0 and wd.get(s, 0) < s.count:
                    wd[s] = s.count
                    ws.append((s.sem, s.count))
            if ws:
                def thunk(h, ws=ws):
                    for s, c in ws:
                        h.wait_ge(s, c)
                self.q[eng].append(thunk)

    def final_wait(self, streams):
        ws = [(s.sem, s.count) for s in streams if s.count > 0]

        def thunk(h):
            for s, c in ws:
                h.wait_ge(s, c)
        self.q["sp"].append(thunk)

    def replay(self):
        nc = self.nc
        with nc.Block() as block:
            @block.tensor
            def _(h):
                for t in self.q["pe"]:
                    t(h)

            @block.scalar
            def _(h):
                for t in self.q["act"]:
                    t(h)

            @block.vector
            def _(h):
                for t in self.q["dve"]:
                    t(h)

            @block.gpsimd
            def _(h):
                for t in self.q["pool"]:
                    t(h)

            @block.sync
            def _(h):
                for t in self.q["sp"]:
                    t(h)


class Arena:
    def __init__(self, handle, nbytes):
        self.h = handle
        self.n = nbytes
        self.off = 0

    def alloc(self, dtype, *free):
        esz = 2 if dtype == BF16 else 4
        n = 1
        for f in free:
            n *= f
        nb = (n * esz + 31) // 32 * 32
        assert self.off + nb <= self.n, ("SBUF arena overflow", self.off, nb, self.n)
        w0 = self.off // 4
        ap = self.h[:, w0:w0 + nb // 4]
        if dtype == BF16:
            ap = ap.bitcast(BF16)
        ap = ap[:, 0:n]
        self.off += nb
        if len(free) == 2:
            ap = ap.rearrange("p (a b) -> p a b", a=free[0])
        elif len(free) == 3:
            ap = ap.rearrange("p (a b c) -> p a b c", a=free[0], b=free[1])
        return ap


def build(prog):
    fused = prog == "FUSED"
    nc = bass.Bass("TRN2", target_bir_lowering=False)
    es = contextlib.ExitStack()
    kb = KB(nc, es)

    def din(name, shape, dt):
        return nc.dram_tensor(name, list(shape), dt, kind="ExternalInput").ap()

    def dout(name, shape, dt):
        return nc.dram_tensor(name, list(shape), dt, kind="ExternalOutput").ap()

    need_p1 = {"P1_0": [0], "P3_0": [1], "FUSED": [0, 1]}.get(prog, [])
    need_p2 = {"P2_0": [0], "P2_1": [1], "FUSED": [0, 1]}.get(prog, [])
    need_p3 = {"P3_0": [0], "P3_1": [1], "FUSED": [0, 1]}.get(prog, [])
    need_ada = {"P1_0": [0], "P3_0": [0, 1], "P3_1": [1], "FUSED": [0, 1]}.get(prog, [])

    cbf_d = din("cbf", [128, 5, 128], BF16)
    dr = {}
    if need_ada:
        dr["cvec"] = din("cvec", [128, 8], F32)
        dr["normg"] = din("normg", [128, DEPTH, 8], F32)
        dr["finalg"] = din("finalg", [128, 8], F32)
        dr["bada"] = din("bada", [128, DEPTH, 24], F32)
        dr["wada"] = din("wada", [DEPTH, D, 3 * D], F32)
    if need_p3:
        dr["wout"] = din("wout", [DEPTH, D, D], F32)
    if need_p2:
        dr["win"] = din("win", [DEPTH, D, 1024], F32)
        dr["cosT"] = din("cosT", [128, S], F32)
        dr["sinT"] = din("sinT", [128, S], F32)
        dr["negmask"] = din("negmask", [128, 4, NT], BF16)
        dr["retc"] = din("retc", [128, 128 + 2 + NT], F32)
    if prog in ("P1_0", "P3_0", "FUSED"):
        dr["xT"] = din("xT", [D, TQ], F32)
    if prog == "P3_1":
        dr["x1T_in"] = din("x1T_in", [D, TQ], F32)
    if prog == "P3_0":
        dr["x1T_out"] = dout("x1T_out", [D, TQ], F32)
    if prog in ("P1_0", "P3_0"):
        dr["hTq"] = dout("hTq", [D, TQ], BF16)
    if prog in ("P2_0", "P2_1"):
        dr["hTfull"] = din("hTfull", [4 * D, TQ], BF16)
        dr["ymT"] = dout("ymT", [256, S], BF16)
    if prog in ("P3_0", "P3_1"):
        dr["ymTfull"] = din("ymTfull", [4 * 256, TQ], BF16)
    if prog in ("P3_1", "FUSED"):
        dr["outT"] = dout("outT", [D, TQ], F32)

    ARENA_BYTES = 200 * 1024
    arena_h = es.enter_context(nc.sbuf_tensor("arena", [128, ARENA_BYTES // 4], F32))
    ar = Arena(arena_h, ARENA_BYTES)
    banks = [es.enter_context(nc.psum_tensor("bank%d" % i, [128, 512], F32)) for i in range(7)]
    bankT = es.enter_context(nc.psum_tensor("bankT", [128, 1024], BF16))
    BK = [Buf("bank%d" % i, excl=True) for i in range(7)]
    BKT = Buf("bankT", excl=True)

    st_const = kb.stream("const")
    cbf = ar.alloc(BF16, 5, 128)
    CBF = Buf("cbf")
    SMALL = Buf("small")
    kb.dma(cbf, cbf_d, writes=[CBF], stream=st_const)
    ident, negU, negones, ones_bf, perm = (cbf[:, i, :] for i in range(5))

    if need_ada:
        cvec = ar.alloc(F32, 8)
        normg = ar.alloc(F32, DEPTH, 8)
        finalg = ar.alloc(F32, 8)
        bada = ar.alloc(F32, DEPTH, 24)
        for a, d_ in ((cvec, dr["cvec"]), (normg, dr["normg"]), (finalg, dr["finalg"]), (bada, dr["bada"])):
            kb.dma(a, d_, writes=[SMALL], stream=st_const)
        cact = ar.alloc(F32, 8)
        ctmp = ar.alloc(F32, 8)
        mod = ar.alloc(F32, DEPTH, 24)
        gs = ar.alloc(F32, DEPTH, 8)
        MOD = Buf("mod")
        CACT = Buf("cact")
    CBF.w = (st_const, st_const.count)
    SMALL.w = (st_const, st_const.count)

    arena_mark = ar.off
    xt = XT = st_x = sq = SQ = lnv = LNV = rstd = RSTD = ntmp = NTMP = hto = HTO = st_ho = st_out = None
    wout_bf = WOUT = wst = WST = st_w = ymt = YMT = st_ym = None

    def alloc_rl(tag):
        nonlocal xt, XT, st_x, sq, SQ, lnv, LNV, rstd, RSTD, ntmp, NTMP, hto, HTO, st_ho, st_out
        nonlocal wout_bf, WOUT, wst, WST, st_w, ymt, YMT, st_ym
        ar.off = arena_mark
        xt = [ar.alloc(F32, 8, NT) for _ in range(2)]
        XT = [Buf("xt0"), Buf("xt1")]
        st_x = [kb.stream("x0" + tag), kb.stream("x1" + tag)]
        sq = ar.alloc(BF16, 8, NT)
        SQ = Buf("sq")
        lnv = ar.alloc(F32, NT)
        LNV = Buf("lnv")
        rstd = ar.alloc(F32, NT)
        RSTD = Buf("rstd")
        ntmp = [ar.alloc(F32, NT) for _ in range(2)]
        NTMP = [Buf("ntmp0"), Buf("ntmp1")]
        hto = [ar.alloc(BF16, 8, NT) for _ in range(2)]
        HTO = [Buf("hto0"), Buf("hto1")]
        st_ho = [kb.stream("ho0" + tag), kb.stream("ho1" + tag)]
        st_out = [kb.stream("out0" + tag), kb.stream("out1" + tag)]
        if need_p3:
            wout_bf = ar.alloc(BF16, 8, D)
            WOUT = Buf("wout")
            wst = [ar.alloc(F32, D) for _ in range(2)]
            WST = [Buf("wst0"), Buf("wst1")]
            st_w = [kb.stream("w0" + tag), kb.stream("w1" + tag)]
            ymt = [ar.alloc(BF16, 8, NT) for _ in range(2)]
            YMT = [Buf("ymt0"), Buf("ymt1")]
            st_ym = [kb.stream("ym0" + tag), kb.stream("ym1" + tag)]

    if need_ada:
        alloc_rl("a")
    final_streams = []

    def emit_silu_c():
        kb.act(ctmp, cvec, AF.Exp, [SMALL], [CACT], scale=-1.0)
        kb.act(ctmp, ctmp, AF.Ln, [CACT], [CACT], bias=1.0)
        kb.act(ctmp, ctmp, AF.Exp, [CACT], [CACT], scale=-1.0)
        kb.op("dve", "tensor_tensor", [CACT, SMALL], [CACT], out=cact, in0=ctmp, in1=cvec, op=ALU.mult)

    def emit_adaln(l):
        wv = dr["wada"][l].rearrange("(k p) f -> p k f", p=128)
        for grp in range(6):
            sl = grp % 2
            kb.dma(xt[sl], wv[:, :, grp * 512:(grp + 1) * 512], writes=[XT[sl]], stream=st_x[sl])
            for j in range(4):
                fb = grp * 4 + j
                for kc in range(8):
                    kb.mm(banks[0][:, fb:fb + 1], xt[sl][:, kc, j * 128:(j + 1) * 128], cact[:, kc:kc + 1],
                          kc == 0, kc == 7, [XT[sl], CACT], [BK[0]])
        kb.op("dve", "tensor_tensor", [BK[0], SMALL], [MOD], out=mod[:, l, :], in0=banks[0][:, 0:24], in1=bada[:, l, :], op=ALU.add)
        kb.op("dve", "scalar_tensor_tensor", [MOD, SMALL], [MOD], out=gs[:, l, :], in0=mod[:, l, 8:16], scalar=1.0,
              in1=normg[:, l, :], op0=ALU.add, op1=ALU.mult)

    def emit_norm_tile(sl, scale_ap, shift_ap, dst_view, final, after=None, DST=()):
        kb.act(sq, xt[sl], AF.Square, [XT[sl]], [SQ])
        for kc in range(8):
            kb.mm(banks[0][:, :], ones_bf, sq[:, kc, :], kc == 0, kc == 7, [SQ, CBF], [BK[0]])
        kb.act(lnv, banks[0][:, :], AF.Ln, [BK[0]], [LNV], scale=1.0 / D, bias=EPS)
        kb.act(rstd, lnv, AF.Exp, [LNV], [RSTD], scale=-0.5)
        if final:
            for kc in range(8):
                kb.op("dve", "scalar_tensor_tensor", [XT[sl], RSTD, MOD, SMALL], [XT[sl]], out=xt[sl][:, kc, :], in0=xt[sl][:, kc, :],
                      scalar=scale_ap[:, kc:kc + 1], in1=rstd, op0=ALU.mult, op1=ALU.mult)
            kb.dma(dst_view, xt[sl], reads=[XT[sl]], stream=st_out[sl])
        else:
            for kc in range(8):
                tm = kc % 2
                kb.op("dve", "scalar_tensor_tensor", [XT[sl], RSTD, MOD], [NTMP[tm]], out=ntmp[tm], in0=xt[sl][:, kc, :],
                      scalar=scale_ap[:, kc:kc + 1], in1=rstd, op0=ALU.mult, op1=ALU.mult)
                kb.act(hto[sl][:, kc, :], ntmp[tm], AF.Identity, [NTMP[tm], MOD], [HTO[sl]], bias=shift_ap[:, kc:kc + 1])
            kb.dma(dst_view, hto[sl], reads=[HTO[sl]] + list(DST), stream=st_ho[sl])
            if after is not None:
                after()

    def pkt(dram_ap):
        return dram_ap.rearrange("(k p) t -> p k t", p=128)

    def emit_phase1(l, x_src, h_view, h_after=None, H_DST=None):
        xv = pkt(x_src)
        for tt in range(TQ // NT):
            sl = tt % 2
            kb.dma(xt[sl], xv[:, :, tt * NT:(tt + 1) * NT], writes=[XT[sl]], stream=st_x[sl])
            emit_norm_tile(sl, gs[:, l, :], mod[:, l, 0:8], h_view(tt), final=False,
                           after=(None if h_after is None else (lambda tt=tt: h_after(tt))),
                           DST=([] if H_DST is None else [H_DST[tt]]))

    def emit_phase3(l, x_src, ym_view, x_dst, h_view, out_dst, h_after=None, H_DST=None, YM_SRC=()):
        for j in range(8):
            g_, half = j // 2, j % 2
            r0 = g_ * 128 if half == 0 else 512 + g_ * 128
            sl = j % 2
            kb.dma(wst[sl], dr["wout"][l, r0:r0 + 128, :], writes=[WST[sl]], stream=st_w[sl])
            kb.op("pool", "tensor_copy", [WST[sl]], [WOUT], out=wout_bf[:, j, :], in_=wst[sl])
        xv = pkt(x_src)
        for tt in range(TQ // NT):
            sl = tt % 2
            kb.dma(xt[sl], xv[:, :, tt * NT:(tt + 1) * NT], writes=[XT[sl]], stream=st_x[sl])
            kb.dma(ymt[sl], ym_view(tt), reads=list(YM_SRC), writes=[YMT[sl]], stream=st_ym[sl])
            for fo in range(8):
                bk = 1 + (fo % 2)
                for j in range(8):
                    kb.mm(banks[bk][:, :], wout_bf[:, j, fo * 128:(fo + 1) * 128], ymt[sl][:, j, :], j == 0, j == 7,
                          [WOUT, YMT[sl]], [BK[bk]])
                kb.op("dve", "scalar_tensor_tensor", [BK[bk], MOD, XT[sl]], [XT[sl]], out=xt[sl][:, fo, :], in0=banks[bk][:, :],
                      scalar=mod[:, l, 16 + fo:17 + fo], in1=xt[sl][:, fo, :], op0=ALU.mult, op1=ALU.add)
            if l + 1 < DEPTH:
                kb.dma(pkt(x_dst)[:, :, tt * NT:(tt + 1) * NT], xt[sl], reads=[XT[sl]], stream=st_out[sl])
                emit_norm_tile(sl, gs[:, l + 1, :], mod[:, l + 1, 0:8], h_view(tt), final=False,
                               after=(None if h_after is None else (lambda tt=tt: h_after(tt))),
                               DST=([] if H_DST is None else [H_DST[tt]]))
            else:
                emit_norm_tile(sl, finalg, None, pkt(out_dst)[:, :, tt * NT:(tt + 1) * NT], final=True)

    def emit_phase2(l, h_view, ym_view, H_SRC=None, Y_DST=None, y_after=None):
        ar.off = arena_mark
        win_bf = ar.alloc(BF16, 8, 1024)
        WIN = Buf("win")
        wst2 = [ar.alloc(F32, 1024) for _ in range(2)]
        WST2 = [Buf("wst2_0"), Buf("wst2_1")]
        st_w2 = [kb.stream("w2_0_%d" % l), kb.stream("w2_1_%d" % l)]
        negmask = ar.alloc(BF16, 4, NT)
        retc = ar.alloc(F32, 128 + 2 + NT)
        P2C = Buf("p2c")
        st_c2 = kb.stream("c2_%d" % l)
        kb.dma(negmask, dr["negmask"], writes=[P2C], stream=st_c2)
        kb.dma(retc, dr["retc"], writes=[P2C], stream=st_c2)
        dmaskT = retc[:, 0:128]
        kdec = retc[:, 128:129]
        cdec = retc[:, 129:130]
        qdec = retc[:, 130:130 + NT]

        qT = ar.alloc(BF16, S)
        kTA = ar.alloc(BF16, S)
        kTB = ar.alloc(BF16, S)
        sgT = ar.alloc(BF16, S)
        svA = ar.alloc(BF16, S // 128, 128)
        svB = ar.alloc(BF16, S // 128, 128)
        NTILE = S // NT
        QT = [Buf("qT%d" % i) for i in range(NTILE)]
        KTb = [Buf("kT%d" % i) for i in range(NTILE)]
        SG = [Buf("sg%d" % i) for i in range(NTILE)]
        SV = [Buf("sv%d" % i) for i in range(NTILE)]
        ZERO = Buf("zero")
        kb.op("dve", "memset", [], [ZERO], args=(kTA, 0.0))
        kb.op("dve", "memset", [], [ZERO], args=(kTB, 0.0))
        kb.op("dve", "memset", [], [ZERO], args=(svA, 0.0))
        kb.op("dve", "memset", [], [ZERO], args=(svB, 0.0))

        ht = [ar.alloc(BF16, 8, NT) for _ in range(2)]
        HT = [Buf("ht0"), Buf("ht1")]
        st_h = [kb.stream("h0_%d" % l), kb.stream("h1_%d" % l)]
        cs = [ar.alloc(F32, 2, NT) for _ in range(2)]
        CS = [Buf("cs0"), Buf("cs1")]
        st_cs = [kb.stream("cs0_%d" % l), kb.stream("cs1_%d" % l)]

        xbf = [ar.alloc(BF16, NT) for _ in range(2)]
        XBF = [Buf("xbf0"), Buf("xbf1")]
        t1 = [ar.alloc(F32, NT) for _ in range(2)]
        T1 = [Buf("t1_0"), Buf("t1_1")]
        t2 = [ar.alloc(F32, NT) for _ in range(2)]
        T2 = [Buf("t2_0"), Buf("t2_1")]
        rqT = ar.alloc(BF16, NT)
        rkT = ar.alloc(BF16, NT)
        qdT = ar.alloc(BF16, NT)
        RQ, RK, QD = Buf("rq"), Buf("rk"), Buf("qd")
        sil = ar.alloc(F32, NT)
        SIL = Buf("sil")
        vret = ar.alloc(BF16, 4, 128)
        rgs = ar.alloc(BF16, 4, 128)
        VRET = [Buf("vret%d" % i) for i in range(4)]
        RGS = [Buf("rgs%d" % i) for i in range(4)]
        sil2 = ar.alloc(F32, 128)
        SIL2 = Buf("sil2")
        kd = ar.alloc(BF16, 128)
        KD = Buf("kd")
        Sm = ar.alloc(BF16, 128)
        SM = Buf("Sm")
        NP = 3
        Pf = [ar.alloc(F32, 128) for _ in range(NP)]
        Pb = [ar.alloc(BF16, 128) for _ in range(NP)]
        PF = [Buf("Pf%d" % i) for i in range(NP)]
        PB = [Buf("Pb%d" % i) for i in range(NP)]
        stats = ar.alloc(F32, 6)
        mv = ar.alloc(F32, 2)
        gsm = ar.alloc(F32, 4)
        GN = Buf("gn")
        on = ar.alloc(F32, 128)
        ON = Buf("on")
        ybf = ar.alloc(BF16, 128)
        YBF = Buf("ybf")
        yrT = [ar.alloc(BF16, NT) for _ in range(2)]
        YRT = [Buf("yrT0"), Buf("yrT1")]
        st_yr = [kb.stream("yr0_%d" % l), kb.stream("yr1_%d" % l)]

        for kc in range(8):
            sl = kc % 2
            kb.dma(wst2[sl], dr["win"][l, kc * 128:(kc + 1) * 128, :], writes=[WST2[sl]], stream=st_w2[sl])
            kb.op(POOL, "tensor_copy", [WST2[sl]], [WIN], out=win_bf[:, kc, :], in_=wst2[sl])

        kb.op(POOL, "memset", [], [PF[0]], args=(Pf[0], 0.0))
        kb.op(POOL, "memset", [], [PB[0]], args=(Pb[0], 0.0))
        pst = 0


        def silu_from_psum(src_ap, SRC, tmp, TMP, dst_ap, DST):
            kb.act(tmp, src_ap, AF.Exp, [SRC], [TMP], scale=-1.0)
            kb.act(tmp, tmp, AF.Ln, [TMP], [TMP], bias=1.0)
            kb.act(tmp, tmp, AF.Exp, [TMP], [TMP], scale=-1.0)
            kb.op("dve", "tensor_tensor", [SRC, TMP], [DST], out=dst_ap, in0=src_ap, in1=tmp, op=ALU.mult)

        def emit_2a_tile(tt):
            nonlocal pst
            sl = tt % 2
            r, tq = tt // 4, (tt % 4) * NT
            c0 = tt * NT
            kb.dma(ht[sl], h_view(tt), reads=([] if H_SRC is None else [H_SRC[tt % 4]]), writes=[HT[sl]], stream=st_h[sl])
            kb.dma(cs[sl][:, 0, :], dr["cosT"][:, c0:c0 + NT], writes=[CS[sl]], stream=st_cs[sl])
            kb.dma(cs[sl][:, 1, :], dr["sinT"][:, c0:c0 + NT], writes=[CS[sl]], stream=st_cs[sl])

            def proj_fm(blk, bk):
                for kc in range(8):
                    kb.mm(banks[bk][:, :], win_bf[:, kc, blk * 128:(blk + 1) * 128], ht[sl][:, kc, :], kc == 0, kc == 7,
                          [WIN, HT[sl]], [BK[bk]])

            for which, (dstT, DST) in enumerate(((rqT, RQ), (rkT, RK))):
                bk = 5 + which
                pk = 6 - which
                proj_fm(which, bk)
                kb.act(xbf[which], banks[bk][:, :], AF.Identity, [BK[bk]], [XBF[which]])
                kb.mm(banks[pk][:, :], perm, xbf[which], True, True, [XBF[which], CBF], [BK[pk]])
                kb.op("dve", "tensor_tensor", [BK[bk], CS[sl]], [T1[which]], out=t1[which], in0=banks[bk][:, :], in1=cs[sl][:, 0, :], op=ALU.mult)
                kb.op("dve", "tensor_tensor", [BK[pk], CS[sl]], [T2[which]], out=t2[which], in0=banks[pk][:, :], in1=cs[sl][:, 1, :], op=ALU.mult)
                kb.op(POOL, "tensor_tensor", [T1[which], T2[which]], [DST], out=dstT, in0=t1[which], in1=t2[which], op=ALU.add)
                if which == 0:
                    kb.op("dve", "tensor_tensor", [T1[0], T2[0]], [T1[0]], out=t1[0], in0=t1[0], in1=t2[0], op=ALU.add)
                    kb.op(POOL, "tensor_tensor", [T1[0], P2C], [QD], out=qdT, in0=t1[0], in1=qdec, op=ALU.mult)
            proj_fm(2, 5)
            kb.act(qT[:, c0:c0 + NT], banks[5][:, :], AF.Identity, [BK[5]], [QT[tt]])
            proj_fm(3, 6)
            kb.act(kTA[0:64, c0:c0 + NT], banks[6][0:64, :], AF.Identity, [BK[6], ZERO], [KTb[tt]], scale=0.125)
            kb.act(kTB[64:128, c0:c0 + NT], banks[6][64:128, :], AF.Identity, [BK[6], ZERO], [KTb[tt]], scale=0.125)
            proj_fm(4, 5)
            silu_from_psum(banks[5][:, :], BK[5], sil, SIL, sgT[:, c0:c0 + NT], SG[tt])

            for st in range(4):
                gt = tt * 4 + st
                for kc in range(8):
                    kb.mm(banks[5][:, 0:384], ht[sl][:, kc, st * 128:(st + 1) * 128], win_bf[:, kc, 640:1024], kc == 0, kc == 7,
                          [WIN, HT[sl]], [BK[5]])
                kb.op("dve", "tensor_copy", [BK[5]], [VRET[st]], out=vret[:, st, :], in_=banks[5][:, 0:128])
                kb.op("dve", "tensor_copy", [BK[5], ZERO], [SV[tt]], out=svA[:, gt, 0:64], in_=banks[5][:, 256:320])
                kb.op("dve", "tensor_copy", [BK[5], ZERO], [SV[tt]], out=svB[:, gt, 64:128], in_=banks[5][:, 320:384])
                silu_from_psum(banks[5][:, 128:256], BK[5], sil2, SIL2, rgs[:, st, :], RGS[st])

                cs0 = st * 128
                kb.op("pe", "transpose", [RK, CBF], [BKT], args=(bankT[:, 0:128], rkT[:, cs0:cs0 + 128], ident))
                kb.op("dve", "tensor_scalar", [BKT, P2C], [KD], out=kd, in0=bankT[:, 0:128], scalar1=kdec, scalar2=None, op0=ALU.mult)
                kb.mm(banks[6][:, 0:128], rkT[:, cs0:cs0 + 128], rqT[:, cs0:cs0 + 128], True, True, [RK, RQ], [BK[6]])
                kb.op("dve", "tensor_tensor", [BK[6], P2C], [SM], out=Sm, in0=banks[6][:, 0:128], in1=dmaskT, op=ALU.mult)
                kb.mm(banks[5][:, 0:128], kd[0:64, :], vret[0:64, st, :], True, True, [KD, VRET[st]], [BK[5]])
                kb.mm(banks[6][:, 256:384], kd[64:128, :], vret[64:128, st, :], True, True, [KD, VRET[st]], [BK[6]])
                p0 = pst
                p1 = (p0 + 1) % NP
                p2 = (p0 + 2) % NP
                kb.op("dve", "scalar_tensor_tensor", [PF[p0], BK[5], P2C], [PF[p1]], out=Pf[p1], in0=Pf[p0], scalar=cdec,
                      in1=banks[5][:, 0:128], op0=ALU.mult, op1=ALU.add)
                kb.op(POOL, "tensor_copy", [PF[p1]], [PB[p1]], out=Pb[p1], in_=Pf[p1])
                kb.op("dve", "scalar_tensor_tensor", [PF[p1], BK[6], P2C], [PF[p2]], out=Pf[p2], in0=Pf[p1], scalar=cdec,
                      in1=banks[6][:, 256:384], op0=ALU.mult, op1=ALU.add)
                kb.op(POOL, "tensor_copy", [PF[p2]], [PB[p2]], out=Pb[p2], in_=Pf[p2])
                pst = p2
                kb.mm(banks[6][:, 128:256], Sm, vret[:, st, :], True, False, [SM, VRET[st]], [BK[6]])
                kb.mm(banks[6][0:64, 128:256], qdT[:, cs0:cs0 + 64], Pb[p0], False, True, [QD, PB[p0]], [BK[6]])
                kb.mm(banks[6][64:128, 128:256], qdT[:, cs0 + 64:cs0 + 128], Pb[p1], False, True, [QD, PB[p1]], [BK[6]])
                kb.op("dve", "bn_stats", [BK[6]], [GN], out=stats, in_=banks[6][:, 128:256])
                kb.op("dve", "bn_aggr", [GN], [GN], out=mv, in_=stats)
                kb.act(gsm[:, 0:1], mv[:, 1:2], AF.Ln, [GN], [GN], bias=EPS)
                kb.act(gsm[:, 1:2], gsm[:, 0:1], AF.Exp, [GN], [GN], scale=-0.5)
                kb.op("dve", "scalar_tensor_tensor", [GN], [GN], out=gsm[:, 2:3], in0=mv[:, 0:1], scalar=-1.0, in1=gsm[:, 1:2],
                      op0=ALU.mult, op1=ALU.mult)
                kb.act(on, banks[6][:, 128:256], AF.Identity, [BK[6], GN], [ON], bias=gsm[:, 2:3], scale=gsm[:, 1:2])
                kb.op(POOL, "tensor_tensor", [ON, RGS[st]], [YBF], out=ybf, in0=on, in1=rgs[:, st, :], op=ALU.mult)
                kb.op("pe", "transpose", [YBF, CBF], [BKT], args=(bankT[:, 128:256], ybf, ident))
                kb.op("dve", "tensor_copy", [BKT], [YRT[sl]], out=yrT[sl][:, cs0:cs0 + 128], in_=bankT[:, 128:256])
            kb.dma(ym_view(tt, 0), yrT[sl], reads=[YRT[sl]] + ([] if Y_DST is None else [Y_DST[tt // 4]]), stream=st_yr[sl])

        ntile2a = NTILE if DBG['ntile2a'] is None else DBG['ntile2a']

        def capture_tile(tt):
            kb.capture = []
            emit_2a_tile(tt)
            cap = kb.capture
            kb.capture = None
            return cap

        if INTERLEAVE and DBG['do2c']:
            emit_2a_tile(0)
        else:
            for tt in range(ntile2a):
                emit_2a_tile(tt)

        NE, NSP, NA = 4, 4, 3
        Eb = [ar.alloc(F32, NT) for _ in range(NE)]
        EB = [Buf("E%d" % i) for i in range(NE)]
        SPb = [ar.alloc(BF16, NT) for _ in range(NSP)]
        SPB = [Buf("SP%d" % i) for i in range(NSP)]
        Ab = [ar.alloc(BF16, NT) for _ in range(NA)]
        AB = [Buf("A%d" % i) for i in range(NA)]
        Rb = [[ar.alloc(BF16, NT) for _ in range(2)] for _ in range(2)]
        RB = [[Buf("R%d%d" % (i, j)) for j in range(2)] for i in range(2)]
        ysb = [ar.alloc(BF16, NT) for _ in range(2)]
        YSB = [Buf("ysb0"), Buf("ysb1")]
        st_ys = [kb.stream("ys0_%d" % l), kb.stream("ys1_%d" % l)]
        kTs = (kTA, kTB)
        svs = (svA, svB)

        steps = []
        for qi in range(NTILE):
            blocks = [(4 * qi + o4, o4) for o4 in (3, 2, 1, 0)] + [(kbk, None) for kbk in range(4 * qi - 1, -1, -1)]
            nb = len(blocks)
            for k, (kbk, o4) in enumerate(blocks):
                for hd in range(2):
                    steps.append(dict(qi=qi, hd=hd, k=k, kbk=kbk, o4=o4, first=(k == 0), last=(k == nb - 1)))
        n = len(steps) if DBG['nsteps'] is None else DBG['nsteps']
        if not DBG['do2c']:
            return st_yr

        def emit_Z(s):
            stp = steps[s]
            zb = s % 2
            q0 = stp["qi"] * NT
            kT_ = kTs[stp["hd"]]
            kbk = stp["kbk"]
            diag = stp["o4"] is not None
            kb.mm(banks[zb][:, :], kT_[:, kbk * 128:(kbk + 1) * 128], qT[:, q0:q0 + NT], True, not diag,
                  [KTb[kbk // 4], QT[stp["qi"]], ZERO], [BK[zb]])
            if diag:
                kb.mm(banks[zb][:, :], ident, negmask[:, stp["o4"], :], False, True, [CBF, P2C], [BK[zb]])

        def emit_E(s):
            zb = s % 2
            kb.act(Eb[s % NE], banks[zb][:, :], AF.Exp, [BK[zb]], [EB[s % NE]])

        def emit_SP(s):
            kb.act(SPb[s % NSP], Eb[s % NE], AF.Ln, [EB[s % NE]], [SPB[s % NSP]], bias=1.0)

        def emit_R(s):
            stp = steps[s]
            if stp["last"]:
                return
            hd, k = stp["hd"], stp["k"]
            if stp["first"]:
                kb.op(POOL, "tensor_copy", [SPB[s % NSP]], [RB[hd][1]], out=Rb[hd][1], in_=SPb[s % NSP])
            else:
                kb.op(POOL, "tensor_tensor", [RB[hd][k % 2], SPB[s % NSP]], [RB[hd][(k + 1) % 2]], out=Rb[hd][(k + 1) % 2],
                      in0=Rb[hd][k % 2], in1=SPb[s % NSP], op=ALU.add)

        def emit_PA(s):
            stp = steps[s]
            pb = 2 + (s % 2)
            hd, k = stp["hd"], stp["k"]
            kb.mm(banks[pb][:, :], negU, SPb[s % NSP], True, stp["first"], [CBF, SPB[s % NSP]], [BK[pb]])
            if not stp["first"]:
                kb.mm(banks[pb][:, :], negones, Rb[hd][k % 2], False, True, [CBF, RB[hd][k % 2]], [BK[pb]])

        def emit_A(s):
            pb = 2 + (s % 2)
            kb.act(banks[pb][:, :], banks[pb][:, :], AF.Exp, [BK[pb]], [BK[pb]])
            kb.op("dve", "tensor_tensor", [BK[pb], EB[s % NE]], [AB[s % NA]], out=Ab[s % NA], in0=banks[pb][:, :], in1=Eb[s % NE], op=ALU.mult)

        def emit_AV(s):
            stp = steps[s]
            qi, hd, kbk = stp["qi"], stp["hd"], stp["kbk"]
            ob = 4
            first = stp["first"] and hd == 0
            last = stp["last"] and hd == 1
            kb.mm(banks[ob][:, :], svs[hd][:, kbk, :], Ab[s % NA], first, last, [SV[kbk // 4], ZERO, AB[s % NA]], [BK[ob]])
            if last:
                ys = qi % 2
                q0 = qi * NT
                kb.op("dve", "tensor_tensor", [BK[ob], SG[qi]], [YSB[ys]], out=ysb[ys], in0=banks[ob][:, :], in1=sgT[:, q0:q0 + NT], op=ALU.mult)
                kb.dma(ym_view(qi, 1), ysb[ys], reads=[YSB[ys]] + ([] if Y_DST is None else [Y_DST[qi // 4]]), stream=st_ys[ys])
                if y_after is not None and qi % 4 == 3:
                    y_after(qi // 4)

        pend = []
        state = dict(done=0 if INTERLEAVE else ntile2a - 1, nxt=1, rate=1)
        last_of_tile = {}
        for i_, stp_ in enumerate(steps[:n]):
            last_of_tile[stp_["qi"]] = i_

        def ensure(qi):
            while state["done"] < min(qi, ntile2a - 1):
                if not pend:
                    pend.extend(capture_tile(state["nxt"]))
                for it in pend:
                    kb.play(it)
                del pend[:]
                state["done"] = state["nxt"]
                state["nxt"] += 1

        def feed(s):
            if not INTERLEAVE:
                return
            cur = steps[min(s, n - 1)]["qi"]
            if not pend and state["nxt"] < ntile2a and state["nxt"] <= cur + 1 and state["done"] < state["nxt"]:
                pend.extend(capture_tile(state["nxt"]))
                rem = last_of_tile[cur] - s - 3
                state["rate"] = -(-len(pend) // max(1, rem))
            k = min(state["rate"], len(pend))
            for it in pend[:k]:
                kb.play(it)
            del pend[:k]
            if k and not pend:
                state["done"] = state["nxt"]
                state["nxt"] += 1

        emit_Z(0)
        emit_E(0)
        if n > 1:
            emit_Z(1)
        for s in range(0, n + 1):
            ensure(steps[min(s + 2, n - 1)]["qi"])
            if s < n:
                emit_SP(s)
                emit_R(s)
                emit_PA(s)
            if 0 <= s - 1 < n:
                emit_A(s - 1)
                emit_AV(s - 1)
            if s + 1 < n:
                emit_E(s + 1)
            if s + 2 < n:
                emit_Z(s + 2)
            feed(s)
        ensure(ntile2a - 1)
        return st_yr + st_ys

    def hview_unfused(dram):
        hv = dram.rearrange("(r k p) t -> p r k t", p=128, k=8)
        return lambda tt: hv[:, tt // 4, :, (tt % 4) * NT:(tt % 4 + 1) * NT]

    def yview_unfused(dram):
        return lambda tile, half: dram[half * 128:(half + 1) * 128, tile * NT:(tile + 1) * NT]

    if prog == "P1_0":
        emit_silu_c()
        emit_adaln(0)
        emit_phase1(0, dr["xT"], lambda tt: pkt(dr["hTq"])[:, :, tt * NT:(tt + 1) * NT])
        final_streams += st_ho
    elif prog in ("P2_0", "P2_1"):
        l = int(prog[-1])
        final_streams += emit_phase2(l, hview_unfused(dr["hTfull"]), yview_unfused(dr["ymT"]))
    elif prog == "P3_0":
        emit_silu_c()
        emit_adaln(0)
        emit_adaln(1)
        ymv = dr["ymTfull"].rearrange("(j p) t -> p j t", p=128)
        emit_phase3(0, dr["xT"], lambda tt: ymv[:, :, tt * NT:(tt + 1) * NT], dr["x1T_out"],
                    lambda tt: pkt(dr["hTq"])[:, :, tt * NT:(tt + 1) * NT], None)
        final_streams += st_ho + st_out
    elif prog == "P3_1":
        emit_silu_c()
        emit_adaln(1)
        ymv = dr["ymTfull"].rearrange("(j p) t -> p j t", p=128)
        emit_phase3(1, dr["x1T_in"], lambda tt: ymv[:, :, tt * NT:(tt + 1) * NT], None, None, dr["outT"])
        final_streams += st_out
    elif fused:
        groups = [[0, 1, 2, 3], [4, 5, 6, 7]]
        x1T = nc.dram_tensor("x1T_scr", [D, TQ], F32).ap()
        hsrc = [[nc.dram_tensor("hsrc_%d_%d" % (l, t), [D, NT], BF16) for t in range(4)] for l in range(DEPTH)]
        hgat = [[nc.dram_tensor("hgat_%d_%d" % (l, t), [4 * D, NT], BF16) for t in range(4)] for l in range(DEPTH)]
        ysrc = [[nc.dram_tensor("ysrc_%d_%d" % (l, q), [256, TQ], BF16) for q in range(4)] for l in range(DEPTH)]
        ygat = [nc.dram_tensor("ygat_%d" % l, [4 * 1024, TQ], BF16) for l in range(DEPTH)]
        HS = [[Buf("hs") for _ in range(4)] for _ in range(DEPTH)]
        HG = [[Buf("hg") for _ in range(4)] for _ in range(DEPTH)]
        YS = [[Buf("ys") for _ in range(4)] for _ in range(DEPTH)]
        YG = [Buf("yg") for _ in range(DEPTH)]
        rank_cache = {}

        def myrank(h):
            if id(h) not in rank_cache:
                rank_cache[id(h)] = h.partition_id() % 4
            return rank_cache[id(h)]

        def h_view_dst(l):
            return lambda tt: pkt(hsrc[l][tt].ap())

        def h_after(l):
            def f(tt):
                kb.cc("AllGather", groups, hsrc[l][tt].ap().opt(), hgat[l][tt].ap().opt(), reads=[], writes=[HS[l][tt], HG[l][tt]])
            return f

        def h_view_src(l):
            def f(tt):
                hv = hgat[l][tt % 4].ap().rearrange("(r k p) t -> p r k t", p=128, k=8)
                return hv[:, tt // 4, :, :]
            return f

        def y_view_dst(l):
            return lambda tile, half: ysrc[l][tile // 4].ap()[half * 128:(half + 1) * 128, (tile % 4) * NT:(tile % 4 + 1) * NT]

        def y_after(l):
            def f(q):
                kb.cc("AllGather", groups, ysrc[l][q].ap().opt(), ygat[l].ap()[q * 1024:(q + 1) * 1024, :].opt(),
                      reads=[], writes=[YS[l][q], YG[l]])
            return f

        def y_view_src(l):
            def f(tt):
                def g(h):
                    v = ygat[l].ap().rearrange("(qj p) t -> p qj t", p=128)
                    return v[:, bass.ds(myrank(h) * 8, 8), tt * NT:(tt + 1) * NT]
                return g
            return f

        emit_silu_c()
        emit_adaln(0)
        emit_adaln(1)
        emit_phase1(0, dr["xT"], h_view_dst(0), h_after(0), HS[0])
        for l in range(DEPTH):
            kb.barrier()
            emit_phase2(l, h_view_src(l), y_view_dst(l), H_SRC=HG[l], Y_DST=YS[l], y_after=y_after(l))
            kb.barrier()
            alloc_rl("p3_%d" % l)
            if l + 1 < DEPTH:
                emit_phase3(l, dr["xT"] if l == 0 else x1T, y_view_src(l), x1T, h_view_dst(l + 1), None,
                            h_after=h_after(l + 1), H_DST=HS[l + 1], YM_SRC=[YG[l]])
            else:
                emit_phase3(l, x1T, y_view_src(l), None, None, dr["outT"], YM_SRC=[YG[l]])
        final_streams += st_out
    else:
        raise NotImplementedError(prog)

    kb.final_wait(final_streams)
    kb.replay()
    return nc, es


def _bf(a):
    return np.asarray(a, dtype=np.float32).astype(ml_dtypes.bfloat16)


def _consts():
    ident = np.eye(128, dtype=np.float32)
    jj, ss = np.meshgrid(np.arange(128), np.arange(128), indexing="ij")
    negU = np.where(jj >= ss, -1.0, 0.0).astype(np.float32)
    negones = -np.ones((128, 128), np.float32)
    ones = np.ones((128, 128), np.float32)
    perm = np.zeros((128, 128), np.float32)
    for d in range(128):
        perm[(d + 64) % 128, d] = 1.0
    cbf = _bf(np.stack([ident, negU, negones, ones, perm], axis=1))
    i = np.arange(128)[:, None, None]
    o4 = np.arange(4)[None, :, None]
    j = np.arange(NT)[None, None, :]
    negmask = _bf(np.where(j > o4 * 128 + i, 0.0, -BIG))
    half = 64
    inv = (10000.0 ** (-(np.arange(half, dtype=np.float32) / np.float32(half)))).astype(np.float32)
    pos = np.arange(S, dtype=np.float32)
    ang = (pos[None, :] * inv[:, None]).astype(np.float32)
    cos = np.cos(ang).astype(np.float32)
    sin = np.sin(ang).astype(np.float32)
    cosT = np.concatenate([cos, cos], axis=0)
    sinT = np.concatenate([-sin, sin], axis=0)
    return cbf, negmask, np.ascontiguousarray(cosT), np.ascontiguousarray(sinT)


def _retc(g):
    lg = np.log1p(-(2.0 ** (-5.0 - g)))
    m = np.arange(128)
    same = (m[:, None] // 64) == (m[None, :] // 64)
    dm = np.where(same, np.exp(np.abs(m[:, None] - m[None, :]) * lg), 0.0) * (128.0 ** -0.5)
    kdec = np.exp((63.0 - (m % 64)) * lg) * (128.0 ** -0.5)
    cdec = np.full(128, np.exp(64.0 * lg))
    c = np.arange(NT)
    qdec = np.broadcast_to(np.exp(((c % 64) + 1.0) * lg)[None, :], (128, NT))
    return np.ascontiguousarray(np.concatenate([dm, kdec[:, None], cdec[:, None], qdec], axis=1).astype(np.float32))


def _vec(v):
    return np.ascontiguousarray(np.asarray(v, np.float32).reshape(-1, 128).T)


_NC_CACHE = {}


def _get(prog):
    if prog not in _NC_CACHE:
        _NC_CACHE[prog] = build(prog)
    return _NC_CACHE[prog][0]


def _run(prog, in_maps):
    nc = _get(prog)
    res = run_bass_kernel_spmd(nc, in_maps, core_ids=list(range(NCORES)))
    return res.results


def kernel(x, c, norm_g, w_ada, b_ada, w_in, w_out, final_g):
    x = np.asarray(x, np.float32)
    c = np.asarray(c, np.float32)
    norm_g = np.asarray(norm_g, np.float32)
    w_ada = np.ascontiguousarray(np.asarray(w_ada, np.float32))
    b_ada = np.asarray(b_ada, np.float32)
    w_in = np.asarray(w_in, np.float32)
    w_out = np.ascontiguousarray(np.asarray(w_out, np.float32))
    final_g = np.asarray(final_g, np.float32)

    cbf, negmask, cosT, sinT = _consts()
    normg_l = np.ascontiguousarray(np.stack([_vec(norm_g[l]) for l in range(DEPTH)], axis=1))
    bada_l = np.ascontiguousarray(np.stack([np.asarray(b_ada[l]).reshape(24, 128).T for l in range(DEPTH)], axis=1))
    finalg_l = _vec(final_g)

    base = []
    for core in range(NCORES):
        b, g = core // 4, core % 4
        cols = np.concatenate([
            np.arange(g * 128, (g + 1) * 128),
            512 + np.arange(g * 128, (g + 1) * 128),
            2048 + np.arange(g * 128, (g + 1) * 128),
            2560 + np.arange(g * 128, (g + 1) * 128),
            3584 + np.arange(g * 128, (g + 1) * 128),
            1024 + np.arange(g * 128, (g + 1) * 128),
            1536 + np.arange(g * 128, (g + 1) * 128),
            3072 + np.arange(g * 128, (g + 1) * 128),
        ])
        base.append(dict(
            b=b, g=g,
            cbf=cbf, negmask=negmask, cosT=cosT, sinT=sinT, retc=_retc(g),
            cvec=_vec(c[b]), normg=normg_l, finalg=finalg_l, bada=bada_l, wada=w_ada, wout=w_out,
            win=np.ascontiguousarray(w_in[:, :, cols]),
            xT=np.ascontiguousarray(x[b, g * TQ:(g + 1) * TQ, :].T),
        ))

    def pick(core, names):
        return {k: base[core][k] for k in names}

    P13 = ["cbf", "cvec", "normg", "finalg", "bada", "wada"]
    P2 = ["cbf", "win", "cosT", "sinT", "negmask", "retc"]

    def gather_h(res):
        out = []
        for core in range(NCORES):
            b = core // 4
            out.append(np.ascontiguousarray(np.concatenate([res[b * 4 + r]["hTq"] for r in range(4)], axis=0)))
        return out

    def gather_ym(res):
        out = []
        for core in range(NCORES):
            b, g = core // 4, core % 4
            full = np.concatenate([res[b * 4 + r]["ymT"] for r in range(4)], axis=0)
            out.append(np.ascontiguousarray(full[:, g * TQ:(g + 1) * TQ]))
        return out

    if MODE == "FUSED":
        names = ["cbf", "cvec", "normg", "finalg", "bada", "wada", "wout", "win", "cosT", "sinT", "negmask", "retc", "xT"]
        res = _run("FUSED", [pick(i, names) for i in range(NCORES)])
        out = np.empty((B, S, D), np.float32)
        for core in range(NCORES):
            b, g = core // 4, core % 4
            out[b, g * TQ:(g + 1) * TQ, :] = res[core]["outT"].T
        return out

    r1 = _run("P1_0", [dict(pick(i, P13 + ["xT"])) for i in range(NCORES)])
    hfull = gather_h(r1)
    r2 = _run("P2_0", [dict(pick(i, P2), hTfull=hfull[i]) for i in range(NCORES)])
    ymfull = gather_ym(r2)
    r3 = _run("P3_0", [dict(pick(i, P13 + ["xT", "wout"]), ymTfull=ymfull[i]) for i in range(NCORES)])
    hfull = gather_h(r3)
    r4 = _run("P2_1", [dict(pick(i, P2), hTfull=hfull[i]) for i in range(NCORES)])
    ymfull = gather_ym(r4)
    r5 = _run("P3_1", [dict(pick(i, P13 + ["wout"]), ymTfull=ymfull[i], x1T_in=r3[i]["x1T_out"]) for i in range(NCORES)])

    out = np.empty((B, S, D), np.float32)
    for core in range(NCORES):
        b, g = core // 4, core % 4
        out[b, g * TQ:(g + 1) * TQ, :] = r5[core]["outT"].T
    return out
```

```python
import contextlib
import numpy as np
import ml_dtypes
import concourse.bass as bass
import concourse.mybir as mybir
from concourse.bass_utils import run_bass_kernel_spmd

F32 = mybir.dt.float32
BF16 = mybir.dt.bfloat16
AF = mybir.ActivationFunctionType
ALU = mybir.AluOpType

D = 1024
B = 2
S = 8192
DEPTH = 2
NCORES = 8
TQ = S // 4
NT = 512
EPS = 1e-6
BIG = 32768.0
SAME_ENGINE_WAITS = True
INTERLEAVE = True
MODE = "FUSED"
DBG = dict(ntile2a=None, do2c=True, nsteps=None, ret=True, stop=99)
POOL = "dve"


class Src:
    def __init__(self, sem, name):
        self.sem = sem
        self.count = 0
        self.name = name


class Buf:
    __slots__ = ("w", "r", "name", "excl")

    def __init__(self, name="", excl=False):
        self.w = None
        self.r = {}
        self.name = name
        self.excl = excl


class KB:
    ENGS = ("pe", "act", "dve", "pool", "sp")

    def __init__(self, nc, es):
        self.nc = nc
        self.es = es
        self.q = {n: [] for n in self.ENGS}
        self.src = {}
        for n in self.ENGS:
            self.src[n] = Src(es.enter_context(nc.semaphore("sem_" + n)), n)
        self.waited = {n: {} for n in self.ENGS}
        self.streams = []
        self.nops = 0
        self.capture = None

    def stream(self, name):
        s = Src(self.es.enter_context(self.nc.semaphore("dq_" + name)), name)
        self.streams.append(s)
        return s

    def _waits(self, eng, reads, writes):
        need = {}
        for b in reads:
            if b.w is not None:
                s, c = b.w
                if need.get(s, 0) < c:
                    need[s] = c
        for b in writes:
            if b.w is not None:
                s, c = b.w
                if need.get(s, 0) < c:
                    need[s] = c
            for s, c in b.r.items():
                if need.get(s, 0) < c:
                    need[s] = c
        out = []
        me = self.src[eng]
        wd = self.waited[eng]
        for s, c in need.items():
            if s is me and (eng == "pe" or eng == "sp" or not SAME_ENGINE_WAITS):
                continue
            if wd.get(s, 0) < c:
                wd[s] = c
                out.append((s.sem, c))
        return out

    def op(self, eng, meth, reads=(), writes=(), args=(), **kw):
        if self.capture is not None:
            self.capture.append(("op", (eng, meth, list(reads), list(writes), args), kw))
            return
        excl = [b for b in reads if b.excl]
        if excl:
            reads = [b for b in reads if not b.excl]
            writes = list(writes) + excl
        ws = self._waits(eng, reads, writes)
        me = self.src[eng]
        me.count += 1
        cnt = me.count
        sem = me.sem

        def thunk(h):
            for s, c in ws:
                h.wait_ge(s, c)
            getattr(h, meth)(*args, **kw).then_inc(sem, 1)

        self.q[eng].append(thunk)
        self.nops += 1
        for b in reads:
            if b.r.get(me, 0) < cnt:
                b.r[me] = cnt
        for b in writes:
            b.w = (me, cnt)
            b.r = {}

    def mm(self, out, lhsT, rhs, start, stop, reads, writes):
        self.op("pe", "matmul", reads, writes, args=(out,), lhsT=lhsT, rhs=rhs, start=start, stop=stop)

    def act(self, out, in_, func, reads, writes, **kw):
        self.op("act", "activation", reads, writes, out=out, in_=in_, func=func, **kw)

    def dma(self, out, in_, reads=(), writes=(), stream=None, eng="sp"):
        if self.capture is not None:
            self.capture.append(("dma", (out, in_, list(reads), list(writes), stream, eng), {}))
            return
        ws = self._waits(eng, reads, writes)
        stream.count += 16
        cnt = stream.count
        sem = stream.sem

        def thunk(h):
            for s, c in ws:
                h.wait_ge(s, c)
            src_ap = in_(h) if callable(in_) else in_
            h.dma_start(out=out, in_=src_ap).then_inc(sem, 16)

        self.q[eng].append(thunk)
        for b in reads:
            if b.r.get(stream, 0) < cnt:
                b.r[stream] = cnt
        for b in writes:
            b.w = (stream, cnt)
            b.r = {}

    def play(self, item):
        kind, a, kw = item
        assert self.capture is None
        if kind == "op":
            self.op(a[0], a[1], a[2], a[3], a[4], **kw)
        else:
            self.dma(a[0], a[1], a[2], a[3], a[4], a[5])

    def cc(self, kind, groups, in_ap, out_ap, reads=(), writes=()):
        ws = self._waits("pool", reads, writes)
        st = Src(self.es.enter_context(self.nc.semaphore("cc%d" % len(self.streams))), "cc")
        self.streams.append(st)
        st.count = 1
        sem = st.sem

        def thunk(h):
            for s, c in ws:
                h.wait_ge(s, c)
            h.collective_compute(kind, ALU.bypass, replica_groups=groups, ins=[in_ap], outs=[out_ap]).then_inc(sem, 1)

        self.q["pool"].append(thunk)
        for b in reads:
            b.r[st] = 1
        for b in writes:
            b.w = (st, 1)
            b.r = {}

    def barrier(self):
        allsrc = [self.src[n] for n in self.ENGS] + self.streams
        for eng in self.ENGS:
            ws = []
            wd = self.waited[eng]
            for s in allsrc:
                if s is self.src[eng] and eng == "sp":
                    continue
                if s.count > 0 and wd.get(s, 0) < s.count:
                    wd[s] = s.count
                    ws.append((s.sem, s.count))
            if ws:
                def thunk(h, ws=ws):
                    for s, c in ws:
                        h.wait_ge(s, c)
                self.q[eng].append(thunk)

    def final_wait(self, streams):
        ws = [(s.sem, s.count) for s in streams if s.count > 0]

        def thunk(h):
            for s, c in ws:
                h.wait_ge(s, c)
        self.q["sp"].append(thunk)

    def replay(self):
        nc = self.nc
        with nc.Block() as block:
            @block.tensor
            def _(h):
                for t in self.q["pe"]:
                    t(h)

            @block.scalar
            def _(h):
                for t in self.q["act"]:
                    t(h)

            @block.vector
            def _(h):
                for t in self.q["dve"]:
                    t(h)

            @block.gpsimd
            def _(h):
                for t in self.q["pool"]:
                    t(h)

            @block.sync
            def _(h):
                for t in self.q["sp"]:
                    t(h)


class Arena:
    def __init__(self, handle, nbytes):
        self.h = handle
        self.n = nbytes
        self.off = 0

    def alloc(self, dtype, *free):
        esz = 2 if dtype == BF16 else 4
        n = 1
        for f in free:
            n *= f
        nb = (n * esz + 31) // 32 * 32
        assert self.off + nb <= self.n, ("SBUF arena overflow", self.off, nb, self.n)
        w0 = self.off // 4
        ap = self.h[:, w0:w0 + nb // 4]
        if dtype == BF16:
            ap = ap.bitcast(BF16)
        ap = ap[:, 0:n]
        self.off += nb
        if len(free) == 2:
            ap = ap.rearrange("p (a b) -> p a b", a=free[0])
        elif len(free) == 3:
            ap = ap.rearrange("p (a b c) -> p a b c", a=free[0], b=free[1])
        return ap


def build(prog):
    fused = prog == "FUSED"
    nc = bass.Bass("TRN2", target_bir_lowering=False)
    es = contextlib.ExitStack()
    kb = KB(nc, es)

    def din(name, shape, dt):
        return nc.dram_tensor(name, list(shape), dt, kind="ExternalInput").ap()

    def dout(name, shape, dt):
        return nc.dram_tensor(name, list(shape), dt, kind="ExternalOutput").ap()

    need_p1 = {"P1_0": [0], "P3_0": [1], "FUSED": [0, 1]}.get(prog, [])
    need_p2 = {"P2_0": [0], "P2_1": [1], "FUSED": [0, 1]}.get(prog, [])
    need_p3 = {"P3_0": [0], "P3_1": [1], "FUSED": [0, 1]}.get(prog, [])
    need_ada = {"P1_0": [0], "P3_0": [0, 1], "P3_1": [1], "FUSED": [0, 1]}.get(prog, [])

    cbf_d = din("cbf", [128, 5, 128], BF16)
    dr = {}
    if need_ada:
        dr["cvec"] = din("cvec", [128, 8], F32)
        dr["normg"] = din("normg", [128, DEPTH, 8], F32)
        dr["finalg"] = din("finalg", [128, 8], F32)
        dr["bada"] = din("bada", [128, DEPTH, 24], F32)
        dr["wada"] = din("wada", [DEPTH, D, 3 * D], F32)
    if need_p3:
        dr["wout"] = din("wout", [DEPTH, D, D], F32)
    if need_p2:
        dr["win"] = din("win", [DEPTH, D, 1024], F32)
        dr["cosT"] = din("cosT", [128, S], F32)
        dr["sinT"] = din("sinT", [128, S], F32)
        dr["negmask"] = din("negmask", [128, 4, NT], BF16)
        dr["retc"] = din("retc", [128, 2 * NT + 2], F32)
    if prog in ("P1_0", "P3_0", "FUSED"):
        dr["xT"] = din("xT", [D, TQ], F32)
    if prog == "P3_1":
        dr["x1T_in"] = din("x1T_in", [D, TQ], F32)
    if prog == "P3_0":
        dr["x1T_out"] = dout("x1T_out", [D, TQ], F32)
    if prog in ("P1_0", "P3_0"):
        dr["hTq"] = dout("hTq", [D, TQ], BF16)
    if prog in ("P2_0", "P2_1"):
        dr["hTfull"] = din("hTfull", [4 * D, TQ], BF16)
        dr["ymT"] = dout("ymT", [256, S], BF16)
    if prog in ("P3_0", "P3_1"):
        dr["ymTfull"] = din("ymTfull", [4 * 256, TQ], BF16)
    if prog in ("P3_1", "FUSED"):
        dr["outT"] = dout("outT", [D, TQ], F32)

    ARENA_BYTES = 207 * 1024
    arena_h = es.enter_context(nc.sbuf_tensor("arena", [128, ARENA_BYTES // 4], F32))
    ar = Arena(arena_h, ARENA_BYTES)
    banks = [es.enter_context(nc.psum_tensor("bank%d" % i, [128, 512], F32)) for i in range(7)]
    bankT = es.enter_context(nc.psum_tensor("bankT", [128, 1024], BF16))
    BK = [Buf("bank%d" % i, excl=True) for i in range(7)]
    BKT = Buf("bankT", excl=True)

    st_const = kb.stream("const")
    cbf = ar.alloc(BF16, 5, 128)
    CBF = Buf("cbf")
    SMALL = Buf("small")
    kb.dma(cbf, cbf_d, writes=[CBF], stream=st_const)
    ident, negU, negones, ones_bf, perm = (cbf[:, i, :] for i in range(5))

    if need_ada:
        cvec = ar.alloc(F32, 8)
        normg = ar.alloc(F32, DEPTH, 8)
        finalg = ar.alloc(F32, 8)
        bada = ar.alloc(F32, DEPTH, 24)
        for a, d_ in ((cvec, dr["cvec"]), (normg, dr["normg"]), (finalg, dr["finalg"]), (bada, dr["bada"])):
            kb.dma(a, d_, writes=[SMALL], stream=st_const)
        cact = ar.alloc(F32, 8)
        ctmp = ar.alloc(F32, 8)
        mod = ar.alloc(F32, DEPTH, 24)
        gs = ar.alloc(F32, DEPTH, 8)
        MOD = Buf("mod")
        CACT = Buf("cact")
    CBF.w = (st_const, st_const.count)
    SMALL.w = (st_const, st_const.count)

    arena_mark = ar.off
    xt = XT = st_x = sq = SQ = lnv = LNV = rstd = RSTD = ntmp = NTMP = hto = HTO = st_ho = st_out = None
    wout_bf = WOUT = wst = WST = st_w = ymt = YMT = st_ym = None

    def alloc_rl(tag):
        nonlocal xt, XT, st_x, sq, SQ, lnv, LNV, rstd, RSTD, ntmp, NTMP, hto, HTO, st_ho, st_out
        nonlocal wout_bf, WOUT, wst, WST, st_w, ymt, YMT, st_ym
        ar.off = arena_mark
        xt = [ar.alloc(F32, 8, NT) for _ in range(2)]
        XT = [Buf("xt0"), Buf("xt1")]
        st_x = [kb.stream("x0" + tag), kb.stream("x1" + tag)]
        sq = ar.alloc(BF16, 8, NT)
        SQ = Buf("sq")
        lnv = ar.alloc(F32, NT)
        LNV = Buf("lnv")
        rstd = ar.alloc(F32, NT)
        RSTD = Buf("rstd")
        ntmp = [ar.alloc(F32, NT) for _ in range(2)]
        NTMP = [Buf("ntmp0"), Buf("ntmp1")]
        hto = [ar.alloc(BF16, 8, NT) for _ in range(2)]
        HTO = [Buf("hto0"), Buf("hto1")]
        st_ho = [kb.stream("ho0" + tag), kb.stream("ho1" + tag)]
        st_out = [kb.stream("out0" + tag), kb.stream("out1" + tag)]
        if need_p3:
            wout_bf = ar.alloc(BF16, 8, D)
            WOUT = Buf("wout")
            wst = [ar.alloc(F32, D) for _ in range(2)]
            WST = [Buf("wst0"), Buf("wst1")]
            st_w = [kb.stream("w0" + tag), kb.stream("w1" + tag)]
            ymt = [ar.alloc(BF16, 8, NT) for _ in range(2)]
            YMT = [Buf("ymt0"), Buf("ymt1")]
            st_ym = [kb.stream("ym0" + tag), kb.stream("ym1" + tag)]

    if need_ada:
        alloc_rl("a")
    final_streams = []

    def emit_silu_c():
        kb.act(ctmp, cvec, AF.Exp, [SMALL], [CACT], scale=-1.0)
        kb.act(ctmp, ctmp, AF.Ln, [CACT], [CACT], bias=1.0)
        kb.act(ctmp, ctmp, AF.Exp, [CACT], [CACT], scale=-1.0)
        kb.op("dve", "tensor_tensor", [CACT, SMALL], [CACT], out=cact, in0=ctmp, in1=cvec, op=ALU.mult)

    def emit_adaln(l):
        wv = dr["wada"][l].rearrange("(k p) f -> p k f", p=128)
        for grp in range(6):
            sl = grp % 2
            kb.dma(xt[sl], wv[:, :, grp * 512:(grp + 1) * 512], writes=[XT[sl]], stream=st_x[sl])
            for j in range(4):
                fb = grp * 4 + j
                for kc in range(8):
                    kb.mm(banks[0][:, fb:fb + 1], xt[sl][:, kc, j * 128:(j + 1) * 128], cact[:, kc:kc + 1],
                          kc == 0, kc == 7, [XT[sl], CACT], [BK[0]])
        kb.op("dve", "tensor_tensor", [BK[0], SMALL], [MOD], out=mod[:, l, :], in0=banks[0][:, 0:24], in1=bada[:, l, :], op=ALU.add)
        kb.op("dve", "scalar_tensor_tensor", [MOD, SMALL], [MOD], out=gs[:, l, :], in0=mod[:, l, 8:16], scalar=1.0,
              in1=normg[:, l, :], op0=ALU.add, op1=ALU.mult)

    def emit_norm_tile(sl, scale_ap, shift_ap, dst_view, final, after=None, DST=()):
        kb.act(sq, xt[sl], AF.Square, [XT[sl]], [SQ])
        for kc in range(8):
            kb.mm(banks[0][:, :], ones_bf, sq[:, kc, :], kc == 0, kc == 7, [SQ, CBF], [BK[0]])
        kb.act(lnv, banks[0][:, :], AF.Ln, [BK[0]], [LNV], scale=1.0 / D, bias=EPS)
        kb.act(rstd, lnv, AF.Exp, [LNV], [RSTD], scale=-0.5)
        if final:
            for kc in range(8):
                kb.op("dve", "scalar_tensor_tensor", [XT[sl], RSTD, MOD, SMALL], [XT[sl]], out=xt[sl][:, kc, :], in0=xt[sl][:, kc, :],
                      scalar=scale_ap[:, kc:kc + 1], in1=rstd, op0=ALU.mult, op1=ALU.mult)
            kb.dma(dst_view, xt[sl], reads=[XT[sl]], stream=st_out[sl])
        else:
            for kc in range(8):
                tm = kc % 2
                kb.op("dve", "scalar_tensor_tensor", [XT[sl], RSTD, MOD], [NTMP[tm]], out=ntmp[tm], in0=xt[sl][:, kc, :],
                      scalar=scale_ap[:, kc:kc + 1], in1=rstd, op0=ALU.mult, op1=ALU.mult)
                kb.act(hto[sl][:, kc, :], ntmp[tm], AF.Identity, [NTMP[tm], MOD], [HTO[sl]], bias=shift_ap[:, kc:kc + 1])
            kb.dma(dst_view, hto[sl], reads=[HTO[sl]] + list(DST), stream=st_ho[sl])
            if after is not None:
                after()

    def pkt(dram_ap):
        return dram_ap.rearrange("(k p) t -> p k t", p=128)

    def emit_phase1(l, x_src, h_view, h_after=None, H_DST=None):
        xv = pkt(x_src)
        for tt in range(TQ // NT):
            sl = tt % 2
            kb.dma(xt[sl], xv[:, :, tt * NT:(tt + 1) * NT], writes=[XT[sl]], stream=st_x[sl])
            emit_norm_tile(sl, gs[:, l, :], mod[:, l, 0:8], h_view(tt), final=False,
                           after=(None if h_after is None else (lambda tt=tt: h_after(tt))),
                           DST=([] if H_DST is None else [H_DST[tt]]))

    def emit_phase3(l, x_src, ym_view, x_dst, h_view, out_dst, h_after=None, H_DST=None, YM_SRC=()):
        for j in range(8):
            g_, half = j // 2, j % 2
            r0 = g_ * 128 if half == 0 else 512 + g_ * 128
            sl = j % 2
            kb.dma(wst[sl], dr["wout"][l, r0:r0 + 128, :], writes=[WST[sl]], stream=st_w[sl])
            kb.op("pool", "tensor_copy", [WST[sl]], [WOUT], out=wout_bf[:, j, :], in_=wst[sl])
        xv = pkt(x_src)
        for tt in range(TQ // NT):
            sl = tt % 2
            kb.dma(xt[sl], xv[:, :, tt * NT:(tt + 1) * NT], writes=[XT[sl]], stream=st_x[sl])
            kb.dma(ymt[sl], ym_view(tt), reads=list(YM_SRC), writes=[YMT[sl]], stream=st_ym[sl])
            for fo in range(8):
                bk = 1 + (fo % 2)
                for j in range(8):
                    kb.mm(banks[bk][:, :], wout_bf[:, j, fo * 128:(fo + 1) * 128], ymt[sl][:, j, :], j == 0, j == 7,
                          [WOUT, YMT[sl]], [BK[bk]])
                kb.op("dve", "scalar_tensor_tensor", [BK[bk], MOD, XT[sl]], [XT[sl]], out=xt[sl][:, fo, :], in0=banks[bk][:, :],
                      scalar=mod[:, l, 16 + fo:17 + fo], in1=xt[sl][:, fo, :], op0=ALU.mult, op1=ALU.add)
            if l + 1 < DEPTH:
                kb.dma(pkt(x_dst)[:, :, tt * NT:(tt + 1) * NT], xt[sl], reads=[XT[sl]], stream=st_out[sl])
                emit_norm_tile(sl, gs[:, l + 1, :], mod[:, l + 1, 0:8], h_view(tt), final=False,
                               after=(None if h_after is None else (lambda tt=tt: h_after(tt))),
                               DST=([] if H_DST is None else [H_DST[tt]]))
            else:
                emit_norm_tile(sl, finalg, None, pkt(out_dst)[:, :, tt * NT:(tt + 1) * NT], final=True)

    def emit_phase2(l, h_view, ym_view, H_SRC=None, Y_DST=None, y_after=None):
        ar.off = arena_mark
        win_bf = ar.alloc(BF16, 8, 1024)
        WIN = Buf("win")
        off_wst2 = ar.off
        wst2 = [ar.alloc(F32, 1024) for _ in range(2)]
        WST2 = [Buf("wst2_0"), Buf("wst2_1")]
        st_w2 = [kb.stream("w2_0_%d" % l), kb.stream("w2_1_%d" % l)]
        negmask = ar.alloc(BF16, 4, NT)
        retc = ar.alloc(F32, 2 * NT + 2)
        P2C = Buf("p2c")
        st_c2 = kb.stream("c2_%d" % l)
        kb.dma(negmask, dr["negmask"], writes=[P2C], stream=st_c2)
        kb.dma(retc, dr["retc"], writes=[P2C], stream=st_c2)
        dmask4 = retc[:, 0:NT]
        kdec = retc[:, NT:NT + 1]
        cdec = retc[:, NT + 1:NT + 2]
        qdec = retc[:, NT + 2:2 * NT + 2]

        qT = ar.alloc(BF16, S)
        kTA = ar.alloc(BF16, S)
        kTB = ar.alloc(BF16, S)
        sgT = ar.alloc(BF16, S)
        svA = ar.alloc(BF16, S // 128, 128)
        svB = ar.alloc(BF16, S // 128, 128)
        NTILE = S // NT
        QT = [Buf("qT%d" % i) for i in range(NTILE)]
        KTb = [Buf("kT%d" % i) for i in range(NTILE)]
        SG = [Buf("sg%d" % i) for i in range(NTILE)]
        SV = [Buf("sv%d" % i) for i in range(NTILE)]
        ZERO = Buf("zero")
        kb.op("dve", "memset", [], [ZERO], args=(kTA, 0.0))
        kb.op("dve", "memset", [], [ZERO], args=(kTB, 0.0))
        kb.op("dve", "memset", [], [ZERO], args=(svA, 0.0))
        kb.op("dve", "memset", [], [ZERO], args=(svB, 0.0))

        ht = [ar.alloc(BF16, 8, NT) for _ in range(2)]
        HT = [Buf("ht0"), Buf("ht1")]
        st_h = [kb.stream("h0_%d" % l), kb.stream("h1_%d" % l)]
        cs = [ar.alloc(F32, 2, NT) for _ in range(2)]
        CS = [Buf("cs0"), Buf("cs1")]
        st_cs = [kb.stream("cs0_%d" % l), kb.stream("cs1_%d" % l)]

        xbf = [ar.alloc(BF16, NT) for _ in range(2)]
        XBF = [Buf("xbf0"), Buf("xbf1")]
        t1_ = ar.alloc(F32, NT)
        t1 = [t1_, t1_]
        T1_ = Buf("t1")
        T1 = [T1_, T1_]
        t2_ = ar.alloc(F32, NT)
        t2 = [t2_, t2_]
        T2_ = Buf("t2")
        T2 = [T2_, T2_]
        rqT = ar.alloc(BF16, NT)
        rkT = ar.alloc(BF16, NT)
        qdT = ar.alloc(BF16, NT)
        RQ, RK, QD = Buf("rq"), Buf("rk"), Buf("qd")
        sil = ar.alloc(F32, NT)
        SIL = Buf("sil")
        vret = ar.alloc(BF16, 4, 128)
        rgs = ar.alloc(BF16, 4, 128)
        VRET = [Buf("vret%d" % i) for i in range(4)]
        RGS = [Buf("rgs%d" % i) for i in range(4)]
        rgraw = ar.alloc(F32, 4, 128)
        RGRAW = Buf("rgraw")
        kd4 = ar.alloc(BF16, 4, 128)
        KD4 = Buf("kd4")
        Sm4 = ar.alloc(BF16, 4, 128)
        SM4 = Buf("Sm4")
        PfX = [ar.alloc(F32, 9, 128) for _ in range(2)]
        PFX = [Buf("PfX0"), Buf("PfX1")]
        Pb8 = ar.alloc(BF16, 8, 128)
        PB8 = Buf("Pb8")
        stats4 = ar.alloc(F32, 4, 6)
        mv4 = ar.alloc(F32, 4, 2)
        gs4 = ar.alloc(F32, 3, 4)
        GN4 = Buf("gn4")
        on4 = ar.alloc(F32, 4, 128)
        ON4 = Buf("on4")
        ybf4 = ar.alloc(BF16, 4, 128)
        YBF4 = Buf("ybf4")
        yrT = [ar.alloc(BF16, NT) for _ in range(2)]
        YRT = [Buf("yrT0"), Buf("yrT1")]
        st_yr = [kb.stream("yr0_%d" % l), kb.stream("yr1_%d" % l)]

        for kc in range(8):
            sl = kc % 2
            kb.dma(wst2[sl], dr["win"][l, kc * 128:(kc + 1) * 128, :], writes=[WST2[sl]], stream=st_w2[sl])
            kb.op(POOL, "tensor_copy", [WST2[sl]], [WIN], out=win_bf[:, kc, :], in_=wst2[sl])

        kb.op("dve", "memset", [], [PFX[0]], args=(PfX[0][:, 0, :], 0.0))
        pst = 0


        def silu_from_psum(src_ap, SRC, tmp, TMP, dst_ap, DST):
            kb.act(tmp, src_ap, AF.Exp, [SRC], [TMP], scale=-1.0)
            kb.act(tmp, tmp, AF.Ln, [TMP], [TMP], bias=1.0)
            kb.act(tmp, tmp, AF.Exp, [TMP], [TMP], scale=-1.0)
            kb.op("dve", "tensor_tensor", [SRC, TMP], [DST], out=dst_ap, in0=src_ap, in1=tmp, op=ALU.mult)

        def emit_2a_tile(tt):
            nonlocal pst
            sl = tt % 2
            r, tq = tt // 4, (tt % 4) * NT
            c0 = tt * NT
            kb.dma(ht[sl], h_view(tt), reads=([] if H_SRC is None else [H_SRC[tt % 4]]), writes=[HT[sl]], stream=st_h[sl])
            kb.dma(cs[sl][:, 0, :], dr["cosT"][:, c0:c0 + NT], writes=[CS[sl]], stream=st_cs[sl])
            kb.dma(cs[sl][:, 1, :], dr["sinT"][:, c0:c0 + NT], writes=[CS[sl]], stream=st_cs[sl])

            def proj_fm(blk, bk):
                for kc in range(8):
                    kb.mm(banks[bk][:, :], win_bf[:, kc, blk * 128:(blk + 1) * 128], ht[sl][:, kc, :], kc == 0, kc == 7,
                          [WIN, HT[sl]], [BK[bk]])

            for which, (dstT, DST) in enumerate(((rqT, RQ), (rkT, RK))):
                bk = 5 + which
                pk = 6 - which
                proj_fm(which, bk)
                kb.act(xbf[which], banks[bk][:, :], AF.Identity, [BK[bk]], [XBF[which]])
                kb.mm(banks[pk][:, :], perm, xbf[which], True, True, [XBF[which], CBF], [BK[pk]])
                kb.op("dve", "tensor_tensor", [BK[bk], CS[sl]], [T1[which]], out=t1[which], in0=banks[bk][:, :], in1=cs[sl][:, 0, :], op=ALU.mult)
                kb.op("dve", "tensor_tensor", [BK[pk], CS[sl]], [T2[which]], out=t2[which], in0=banks[pk][:, :], in1=cs[sl][:, 1, :], op=ALU.mult)
                kb.op(POOL, "tensor_tensor", [T1[which], T2[which]], [DST], out=dstT, in0=t1[which], in1=t2[which], op=ALU.add)
                if which == 0:
                    kb.op("dve", "tensor_tensor", [T1[0], T2[0]], [T1[0]], out=t1[0], in0=t1[0], in1=t2[0], op=ALU.add)
                    kb.op(POOL, "tensor_tensor", [T1[0], P2C], [QD], out=qdT, in0=t1[0], in1=qdec, op=ALU.mult)
            proj_fm(2, 5)
            kb.act(qT[:, c0:c0 + NT], banks[5][:, :], AF.Identity, [BK[5]], [QT[tt]])
            proj_fm(3, 6)
            kb.act(kTA[0:64, c0:c0 + NT], banks[6][0:64, :], AF.Identity, [BK[6], ZERO], [KTb[tt]], scale=0.125)
            kb.act(kTB[64:128, c0:c0 + NT], banks[6][64:128, :], AF.Identity, [BK[6], ZERO], [KTb[tt]], scale=0.125)
            proj_fm(4, 5)
            silu_from_psum(banks[5][:, :], BK[5], sil, SIL, sgT[:, c0:c0 + NT], SG[tt])

            for st in range(4):
                gt = tt * 4 + st
                bt = 5 + (st % 2)
                for kc in range(8):
                    kb.mm(banks[bt][:, 0:384], ht[sl][:, kc, st * 128:(st + 1) * 128], win_bf[:, kc, 640:1024], kc == 0, kc == 7,
                          [WIN, HT[sl]], [BK[bt]])
                kb.op("dve", "tensor_copy", [BK[bt]], [VRET[st]], out=vret[:, st, :], in_=banks[bt][:, 0:128])
                kb.op("dve", "tensor_copy", [BK[bt]], [RGRAW], out=rgraw[:, st, :], in_=banks[bt][:, 128:256])
                kb.op("dve", "tensor_copy", [BK[bt], ZERO], [SV[tt]], out=svA[:, gt, 0:64], in_=banks[bt][:, 256:320])
                kb.op("dve", "tensor_copy", [BK[bt], ZERO], [SV[tt]], out=svB[:, gt, 64:128], in_=banks[bt][:, 320:384])
            kb.act(sil, rgraw, AF.Exp, [RGRAW], [SIL], scale=-1.0)
            kb.act(sil, sil, AF.Ln, [SIL], [SIL], bias=1.0)
            kb.act(sil, sil, AF.Exp, [SIL], [SIL], scale=-1.0)
            kb.op("dve", "tensor_tensor", [RGRAW, SIL], [RGS[0]], out=rgs, in0=rgraw, in1=sil, op=ALU.mult)
            for st in range(4):
                kb.op("pe", "transpose", [RK, CBF], [BKT], args=(bankT[:, st * 128:(st + 1) * 128], rkT[:, st * 128:(st + 1) * 128], ident))
            kb.op("dve", "tensor_scalar", [BKT, P2C], [KD4], out=kd4, in0=bankT[:, 0:512], scalar1=kdec, scalar2=None, op0=ALU.mult)
            for st in range(4):
                kb.mm(banks[5][:, st * 128:(st + 1) * 128], rkT[:, st * 128:(st + 1) * 128], rqT[:, st * 128:(st + 1) * 128], True, True,
                      [RK, RQ], [BK[5]])
            kb.op("dve", "tensor_tensor", [BK[5], P2C], [SM4], out=Sm4, in0=banks[5][:, :], in1=dmask4, op=ALU.mult)
            for st in range(4):
                kb.mm(banks[5][:, st * 128:(st + 1) * 128], kd4[0:64, st, :], vret[0:64, st, :], True, True, [KD4, VRET[st]], [BK[5]])
                kb.mm(banks[6][:, st * 128:(st + 1) * 128], kd4[64:128, st, :], vret[64:128, st, :], True, True, [KD4, VRET[st]], [BK[6]])
            pa = pst
            pbn = 1 - pst
            for c in range(8):
                bkv = 5 + (c % 2)
                src_kv = banks[bkv][:, (c // 2) * 128:(c // 2 + 1) * 128]
                if c < 7:
                    kb.op("dve", "scalar_tensor_tensor", [PFX[pa], BK[bkv], P2C], [PFX[pa]], out=PfX[pa][:, c + 1, :], in0=PfX[pa][:, c, :],
                          scalar=cdec, in1=src_kv, op0=ALU.mult, op1=ALU.add)
                else:
                    kb.op("dve", "scalar_tensor_tensor", [PFX[pa], BK[bkv], P2C], [PFX[pbn]], out=PfX[pbn][:, 0, :], in0=PfX[pa][:, c, :],
                          scalar=cdec, in1=src_kv, op0=ALU.mult, op1=ALU.add)
            kb.op("dve", "tensor_copy", [PFX[pa]], [PB8], out=Pb8, in_=PfX[pa][:, 0:8, :])
            pst = pbn
            for st in range(4):
                oc = slice(st * 128, (st + 1) * 128)
                kb.mm(banks[5][:, oc], Sm4[:, st, :], vret[:, st, :], True, False, [SM4, VRET[st]], [BK[5]])
                kb.mm(banks[5][0:64, oc], qdT[:, st * 128:st * 128 + 64], Pb8[:, 2 * st, :], False, True, [QD, PB8], [BK[5]])
                kb.mm(banks[5][64:128, oc], qdT[:, st * 128 + 64:(st + 1) * 128], Pb8[:, 2 * st + 1, :], False, True, [QD, PB8], [BK[5]])
            for st in range(4):
                kb.op("dve", "bn_stats", [BK[5]], [GN4], out=stats4[:, st, :], in_=banks[5][:, st * 128:(st + 1) * 128])
            for st in range(4):
                kb.op("dve", "bn_aggr", [GN4], [GN4], out=mv4[:, st, :], in_=stats4[:, st, :])
            kb.act(gs4[:, 0, :], mv4[:, :, 1], AF.Ln, [GN4], [GN4], bias=EPS)
            kb.act(gs4[:, 1, :], gs4[:, 0, :], AF.Exp, [GN4], [GN4], scale=-0.5)
            kb.op("dve", "scalar_tensor_tensor", [GN4], [GN4], out=gs4[:, 2, :], in0=mv4[:, :, 0], scalar=-1.0, in1=gs4[:, 1, :],
                  op0=ALU.mult, op1=ALU.mult)
            for st in range(4):
                kb.act(on4[:, st, :], banks[5][:, st * 128:(st + 1) * 128], AF.Identity, [BK[5], GN4], [ON4],
                       bias=gs4[:, 2, st:st + 1], scale=gs4[:, 1, st:st + 1])
            kb.op("dve", "tensor_tensor", [ON4, RGS[0]], [YBF4], out=ybf4, in0=on4, in1=rgs, op=ALU.mult)
            for st in range(4):
                kb.op("pe", "transpose", [YBF4, CBF], [BKT], args=(bankT[:, 512 + st * 128:512 + (st + 1) * 128], ybf4[:, st, :], ident))
            kb.op("dve", "tensor_copy", [BKT], [YRT[sl]], out=yrT[sl], in_=bankT[:, 512:1024])
            kb.dma(ym_view(tt, 0), yrT[sl], reads=[YRT[sl]] + ([] if Y_DST is None else [Y_DST[tt // 4]]), stream=st_yr[sl])

        ntile2a = NTILE if DBG['ntile2a'] is None else DBG['ntile2a']

        def capture_tile(tt):
            kb.capture = []
            emit_2a_tile(tt)
            cap = kb.capture
            kb.capture = None
            return cap

        if INTERLEAVE and DBG['do2c']:
            emit_2a_tile(0)
        else:
            for tt in range(ntile2a):
                emit_2a_tile(tt)

        NE, NSP, NA = 4, 4, 3
        off_save = ar.off
        ar.off = off_wst2
        Eb = [ar.alloc(F32, NT) for _ in range(NE)]
        ar.off = off_save
        EB = [Buf("E%d" % i) for i in range(NE)]
        SPb = [ar.alloc(BF16, NT) for _ in range(NSP)]
        SPB = [Buf("SP%d" % i) for i in range(NSP)]
        Ab = [ar.alloc(BF16, NT) for _ in range(NA)]
        AB = [Buf("A%d" % i) for i in range(NA)]
        Rb = [[ar.alloc(BF16, NT) for _ in range(2)] for _ in range(2)]
        RB = [[Buf("R%d%d" % (i, j)) for j in range(2)] for i in range(2)]
        ysb = [ar.alloc(BF16, NT) for _ in range(2)]
        YSB = [Buf("ysb0"), Buf("ysb1")]
        st_ys = [kb.stream("ys0_%d" % l), kb.stream("ys1_%d" % l)]
        kTs = (kTA, kTB)
        svs = (svA, svB)

        steps = []
        for qi in range(NTILE):
            blocks = [(4 * qi + o4, o4) for o4 in (3, 2, 1, 0)] + [(kbk, None) for kbk in range(4 * qi - 1, -1, -1)]
            nb = len(blocks)
            for k, (kbk, o4) in enumerate(blocks):
                for hd in range(2):
                    steps.append(dict(qi=qi, hd=hd, k=k, kbk=kbk, o4=o4, first=(k == 0), last=(k == nb - 1)))
        n = len(steps) if DBG['nsteps'] is None else DBG['nsteps']
        if not DBG['do2c']:
            return st_yr

        def emit_Z(s):
            stp = steps[s]
            zb = s % 2
            q0 = stp["qi"] * NT
            kT_ = kTs[stp["hd"]]
            kbk = stp["kbk"]
            diag = stp["o4"] is not None
            kb.mm(banks[zb][:, :], kT_[:, kbk * 128:(kbk + 1) * 128], qT[:, q0:q0 + NT], True, not diag,
                  [KTb[kbk // 4], QT[stp["qi"]], ZERO], [BK[zb]])
            if diag:
                kb.mm(banks[zb][:, :], ident, negmask[:, stp["o4"], :], False, True, [CBF, P2C], [BK[zb]])

        def emit_E(s):
            zb = s % 2
            kb.act(Eb[s % NE], banks[zb][:, :], AF.Exp, [BK[zb]], [EB[s % NE]])

        def emit_SP(s):
            kb.act(SPb[s % NSP], Eb[s % NE], AF.Ln, [EB[s % NE]], [SPB[s % NSP]], bias=1.0)

        def emit_R(s):
            stp = steps[s]
            if stp["last"]:
                return
            hd, k = stp["hd"], stp["k"]
            if stp["first"]:
                kb.op(POOL, "tensor_copy", [SPB[s % NSP]], [RB[hd][1]], out=Rb[hd][1], in_=SPb[s % NSP])
            else:
                kb.op(POOL, "tensor_tensor", [RB[hd][k % 2], SPB[s % NSP]], [RB[hd][(k + 1) % 2]], out=Rb[hd][(k + 1) % 2],
                      in0=Rb[hd][k % 2], in1=SPb[s % NSP], op=ALU.add)

        def emit_PA(s):
            stp = steps[s]
            pb = 2 + (s % 2)
            hd, k = stp["hd"], stp["k"]
            kb.mm(banks[pb][:, :], negU, SPb[s % NSP], True, stp["first"], [CBF, SPB[s % NSP]], [BK[pb]])
            if not stp["first"]:
                kb.mm(banks[pb][:, :], negones, Rb[hd][k % 2], False, True, [CBF, RB[hd][k % 2]], [BK[pb]])

        def emit_A(s):
            pb = 2 + (s % 2)
            kb.act(banks[pb][:, :], banks[pb][:, :], AF.Exp, [BK[pb]], [BK[pb]])
            kb.op("dve", "tensor_tensor", [BK[pb], EB[s % NE]], [AB[s % NA]], out=Ab[s % NA], in0=banks[pb][:, :], in1=Eb[s % NE], op=ALU.mult)

        def emit_AV(s):
            stp = steps[s]
            qi, hd, kbk = stp["qi"], stp["hd"], stp["kbk"]
            ob = 4
            first = stp["first"] and hd == 0
            last = stp["last"] and hd == 1
            kb.mm(banks[ob][:, :], svs[hd][:, kbk, :], Ab[s % NA], first, last, [SV[kbk // 4], ZERO, AB[s % NA]], [BK[ob]])
            if last:
                ys = qi % 2
                q0 = qi * NT
                kb.op("dve", "tensor_tensor", [BK[ob], SG[qi]], [YSB[ys]], out=ysb[ys], in0=banks[ob][:, :], in1=sgT[:, q0:q0 + NT], op=ALU.mult)
                kb.dma(ym_view(qi, 1), ysb[ys], reads=[YSB[ys]] + ([] if Y_DST is None else [Y_DST[qi // 4]]), stream=st_ys[ys])
                if y_after is not None and qi % 4 == 3:
                    y_after(qi // 4)

        pend = []
        state = dict(done=0 if INTERLEAVE else ntile2a - 1, nxt=1, rate=1)
        last_of_tile = {}
        for i_, stp_ in enumerate(steps[:n]):
            last_of_tile[stp_["qi"]] = i_

        def ensure(qi):
            while state["done"] < min(qi, ntile2a - 1):
                if not pend:
                    pend.extend(capture_tile(state["nxt"]))
                for it in pend:
                    kb.play(it)
                del pend[:]
                state["done"] = state["nxt"]
                state["nxt"] += 1

        def feed(s):
            if not INTERLEAVE:
                return
            cur = steps[min(s, n - 1)]["qi"]
            if not pend and state["nxt"] < ntile2a and state["nxt"] <= cur + 1 and state["done"] < state["nxt"]:
                pend.extend(capture_tile(state["nxt"]))
                rem = last_of_tile[cur] - s - 3
                state["rate"] = -(-len(pend) // max(1, rem))
            k = min(state["rate"], len(pend))
            for it in pend[:k]:
                kb.play(it)
            del pend[:k]
            if k and not pend:
                state["done"] = state["nxt"]
                state["nxt"] += 1

        emit_Z(0)
        emit_E(0)
        if n > 1:
            emit_Z(1)
        for s in range(0, n + 1):
            ensure(steps[min(s + 2, n - 1)]["qi"])
            if s < n:
                emit_SP(s)
                emit_R(s)
                emit_PA(s)
            if 0 <= s - 1 < n:
                emit_A(s - 1)
                emit_AV(s - 1)
            if s + 1 < n:
                emit_E(s + 1)
            if s + 2 < n:
                emit_Z(s + 2)
            feed(s)
        ensure(ntile2a - 1)
        DBG['arena_end'] = ar.off
        return st_yr + st_ys

    def hview_unfused(dram):
        hv = dram.rearrange("(r k p) t -> p r k t", p=128, k=8)
        return lambda tt: hv[:, tt // 4, :, (tt % 4) * NT:(tt % 4 + 1) * NT]

    def yview_unfused(dram):
        return lambda tile, half: dram[half * 128:(half + 1) * 128, tile * NT:(tile + 1) * NT]

    if prog == "P1_0":
        emit_silu_c()
        emit_adaln(0)
        emit_phase1(0, dr["xT"], lambda tt: pkt(dr["hTq"])[:, :, tt * NT:(tt + 1) * NT])
        final_streams += st_ho
    elif prog in ("P2_0", "P2_1"):
        l = int(prog[-1])
        final_streams += emit_phase2(l, hview_unfused(dr["hTfull"]), yview_unfused(dr["ymT"]))
    elif prog == "P3_0":
        emit_silu_c()
        emit_adaln(0)
        emit_adaln(1)
        ymv = dr["ymTfull"].rearrange("(j p) t -> p j t", p=128)
        emit_phase3(0, dr["xT"], lambda tt: ymv[:, :, tt * NT:(tt + 1) * NT], dr["x1T_out"],
                    lambda tt: pkt(dr["hTq"])[:, :, tt * NT:(tt + 1) * NT], None)
        final_streams += st_ho + st_out
    elif prog == "P3_1":
        emit_silu_c()
        emit_adaln(1)
        ymv = dr["ymTfull"].rearrange("(j p) t -> p j t", p=128)
        emit_phase3(1, dr["x1T_in"], lambda tt: ymv[:, :, tt * NT:(tt + 1) * NT], None, None, dr["outT"])
        final_streams += st_out
    elif fused:
        groups = [[0, 1, 2, 3], [4, 5, 6, 7]]
        x1T = nc.dram_tensor("x1T_scr", [D, TQ], F32).ap()
        hsrc = [[nc.dram_tensor("hsrc_%d_%d" % (l, t), [D, NT], BF16) for t in range(4)] for l in range(DEPTH)]
        hgat = [[nc.dram_tensor("hgat_%d_%d" % (l, t), [4 * D, NT], BF16) for t in range(4)] for l in range(DEPTH)]
        ysrc = [[nc.dram_tensor("ysrc_%d_%d" % (l, q), [256, TQ], BF16) for q in range(4)] for l in range(DEPTH)]
        ygat = [nc.dram_tensor("ygat_%d" % l, [4 * 1024, TQ], BF16) for l in range(DEPTH)]
        HS = [[Buf("hs") for _ in range(4)] for _ in range(DEPTH)]
        HG = [[Buf("hg") for _ in range(4)] for _ in range(DEPTH)]
        YS = [[Buf("ys") for _ in range(4)] for _ in range(DEPTH)]
        YG = [Buf("yg") for _ in range(DEPTH)]
        rank_cache = {}

        def myrank(h):
            if id(h) not in rank_cache:
                rank_cache[id(h)] = h.partition_id() % 4
            return rank_cache[id(h)]

        def h_view_dst(l):
            return lambda tt: pkt(hsrc[l][tt].ap())

        def h_after(l):
            def f(tt):
                kb.cc("AllGather", groups, hsrc[l][tt].ap().opt(), hgat[l][tt].ap().opt(), reads=[], writes=[HS[l][tt], HG[l][tt]])
            return f

        def h_view_src(l):
            def f(tt):
                hv = hgat[l][tt % 4].ap().rearrange("(r k p) t -> p r k t", p=128, k=8)
                return hv[:, tt // 4, :, :]
            return f

        def y_view_dst(l):
            return lambda tile, half: ysrc[l][tile // 4].ap()[half * 128:(half + 1) * 128, (tile % 4) * NT:(tile % 4 + 1) * NT]

        def y_after(l):
            def f(q):
                kb.cc("AllGather", groups, ysrc[l][q].ap().opt(), ygat[l].ap()[q * 1024:(q + 1) * 1024, :].opt(),
                      reads=[], writes=[YS[l][q], YG[l]])
            return f

        def y_view_src(l):
            def f(tt):
                def g(h):
                    v = ygat[l].ap().rearrange("(qj p) t -> p qj t", p=128)
                    return v[:, bass.ds(myrank(h) * 8, 8), tt * NT:(tt + 1) * NT]
                return g
            return f

        emit_silu_c()
        emit_adaln(0)
        emit_adaln(1)
        emit_phase1(0, dr["xT"], h_view_dst(0), h_after(0), HS[0])
        for l in range(DEPTH):
            kb.barrier()
            emit_phase2(l, h_view_src(l), y_view_dst(l), H_SRC=HG[l], Y_DST=YS[l], y_after=y_after(l))
            kb.barrier()
            alloc_rl("p3_%d" % l)
            if l + 1 < DEPTH:
                emit_phase3(l, dr["xT"] if l == 0 else x1T, y_view_src(l), x1T, h_view_dst(l + 1), None,
                            h_after=h_after(l + 1), H_DST=HS[l + 1], YM_SRC=[YG[l]])
            else:
                emit_phase3(l, x1T, y_view_src(l), None, None, dr["outT"], YM_SRC=[YG[l]])
        final_streams += st_out
    else:
        raise NotImplementedError(prog)

    kb.final_wait(final_streams)
    kb.replay()
    return nc, es


def _bf(a):
    return np.asarray(a, dtype=np.float32).astype(ml_dtypes.bfloat16)


def _consts():
    ident = np.eye(128, dtype=np.float32)
    jj, ss = np.meshgrid(np.arange(128), np.arange(128), indexing="ij")
    negU = np.where(jj >= ss, -1.0, 0.0).astype(np.float32)
    negones = -np.ones((128, 128), np.float32)
    ones = np.ones((128, 128), np.float32)
    perm = np.zeros((128, 128), np.float32)
    for d in range(128):
        perm[(d + 64) % 128, d] = 1.0
    cbf = _bf(np.stack([ident, negU, negones, ones, perm], axis=1))
    i = np.arange(128)[:, None, None]
    o4 = np.arange(4)[None, :, None]
    j = np.arange(NT)[None, None, :]
    negmask = _bf(np.where(j > o4 * 128 + i, 0.0, -BIG))
    half = 64
    inv = (10000.0 ** (-(np.arange(half, dtype=np.float32) / np.float32(half)))).astype(np.float32)
    pos = np.arange(S, dtype=np.float32)
    ang = (pos[None, :] * inv[:, None]).astype(np.float32)
    cos = np.cos(ang).astype(np.float32)
    sin = np.sin(ang).astype(np.float32)
    cosT = np.concatenate([cos, cos], axis=0)
    sinT = np.concatenate([-sin, sin], axis=0)
    return cbf, negmask, np.ascontiguousarray(cosT), np.ascontiguousarray(sinT)


def _retc(g):
    lg = np.log1p(-(2.0 ** (-5.0 - g)))
    m = np.arange(128)
    same = (m[:, None] // 64) == (m[None, :] // 64)
    dm = np.where(same, np.exp(np.abs(m[:, None] - m[None, :]) * lg), 0.0) * (128.0 ** -0.5)
    kdec = np.exp((63.0 - (m % 64)) * lg) * (128.0 ** -0.5)
    cdec = np.full(128, np.exp(64.0 * lg))
    c = np.arange(NT)
    qdec = np.broadcast_to(np.exp(((c % 64) + 1.0) * lg)[None, :], (128, NT))
    return np.ascontiguousarray(np.concatenate([dm, dm, dm, dm, kdec[:, None], cdec[:, None], qdec], axis=1).astype(np.float32))


def _vec(v):
    return np.ascontiguousarray(np.asarray(v, np.float32).reshape(-1, 128).T)


_NC_CACHE = {}


def _get(prog):
    if prog not in _NC_CACHE:
        _NC_CACHE[prog] = build(prog)
    return _NC_CACHE[prog][0]


def _run(prog, in_maps):
    nc = _get(prog)
    res = run_bass_kernel_spmd(nc, in_maps, core_ids=list(range(NCORES)))
    return res.results


def kernel(x, c, norm_g, w_ada, b_ada, w_in, w_out, final_g):
    x = np.asarray(x, np.float32)
    c = np.asarray(c, np.float32)
    norm_g = np.asarray(norm_g, np.float32)
    w_ada = np.ascontiguousarray(np.asarray(w_ada, np.float32))
    b_ada = np.asarray(b_ada, np.float32)
    w_in = np.asarray(w_in, np.float32)
    w_out = np.ascontiguousarray(np.asarray(w_out, np.float32))
    final_g = np.asarray(final_g, np.float32)

    cbf, negmask, cosT, sinT = _consts()
    normg_l = np.ascontiguousarray(np.stack([_vec(norm_g[l]) for l in range(DEPTH)], axis=1))
    bada_l = np.ascontiguousarray(np.stack([np.asarray(b_ada[l]).reshape(24, 128).T for l in range(DEPTH)], axis=1))
    finalg_l = _vec(final_g)

    base = []
    for core in range(NCORES):
        b, g = core // 4, core % 4
        cols = np.concatenate([
            np.arange(g * 128, (g + 1) * 128),
            512 + np.arange(g * 128, (g + 1) * 128),
            2048 + np.arange(g * 128, (g + 1) * 128),
            2560 + np.arange(g * 128, (g + 1) * 128),
            3584 + np.arange(g * 128, (g + 1) * 128),
            1024 + np.arange(g * 128, (g + 1) * 128),
            1536 + np.arange(g * 128, (g + 1) * 128),
            3072 + np.arange(g * 128, (g + 1) * 128),
        ])
        base.append(dict(
            b=b, g=g,
            cbf=cbf, negmask=negmask, cosT=cosT, sinT=sinT, retc=_retc(g),
            cvec=_vec(c[b]), normg=normg_l, finalg=finalg_l, bada=bada_l, wada=w_ada, wout=w_out,
            win=np.ascontiguousarray(w_in[:, :, cols]),
            xT=np.ascontiguousarray(x[b, g * TQ:(g + 1) * TQ, :].T),
        ))

    def pick(core, names):
        return {k: base[core][k] for k in names}

    P13 = ["cbf", "cvec", "normg", "finalg", "bada", "wada"]
    P2 = ["cbf", "win", "cosT", "sinT", "negmask", "retc"]

    def gather_h(res):
        out = []
        for core in range(NCORES):
            b = core // 4
            out.append(np.ascontiguousarray(np.concatenate([res[b * 4 + r]["hTq"] for r in range(4)], axis=0)))
        return out

    def gather_ym(res):
        out = []
        for core in range(NCORES):
            b, g = core // 4, core % 4
            full = np.concatenate([res[b * 4 + r]["ymT"] for r in range(4)], axis=0)
            out.append(np.ascontiguousarray(full[:, g * TQ:(g + 1) * TQ]))
        return out

    if MODE == "FUSED":
        names = ["cbf", "cvec", "normg", "finalg", "bada", "wada", "wout", "win", "cosT", "sinT", "negmask", "retc", "xT"]
        res = _run("FUSED", [pick(i, names) for i in range(NCORES)])
        out = np.empty((B, S, D), np.float32)
        for core in range(NCORES):
            b, g = core // 4, core % 4
            out[b, g * TQ:(g + 1) * TQ, :] = res[core]["outT"].T
        return out

    r1 = _run("P1_0", [dict(pick(i, P13 + ["xT"])) for i in range(NCORES)])
    hfull = gather_h(r1)
    r2 = _run("P2_0", [dict(pick(i, P2), hTfull=hfull[i]) for i in range(NCORES)])
    ymfull = gather_ym(r2)
    r3 = _run("P3_0", [dict(pick(i, P13 + ["xT", "wout"]), ymTfull=ymfull[i]) for i in range(NCORES)])
    hfull = gather_h(r3)
    r4 = _run("P2_1", [dict(pick(i, P2), hTfull=hfull[i]) for i in range(NCORES)])
    ymfull = gather_ym(r4)
    r5 = _run("P3_1", [dict(pick(i, P13 + ["wout"]), ymTfull=ymfull[i], x1T_in=r3[i]["x1T_out"]) for i in range(NCORES)])

    out = np.empty((B, S, D), np.float32)
    for core in range(NCORES):
        b, g = core // 4, core % 4
        out[b, g * TQ:(g + 1) * TQ, :] = r5[core]["outT"].T
    return out
```

```python
import contextlib
import numpy as np
import ml_dtypes
import concourse.bass as bass
import concourse.mybir as mybir
from concourse.bass_utils import run_bass_kernel_spmd

F32 = mybir.dt.float32
BF16 = mybir.dt.bfloat16
AF = mybir.ActivationFunctionType
ALU = mybir.AluOpType

D = 1024
B = 2
S = 8192
DEPTH = 2
NCORES = 8
TQ = S // 4
NT = 512
EPS = 1e-6
BIG = 32768.0
SAME_ENGINE_WAITS = True
INTERLEAVE = True
LEAD = 2
RATE = 2
MODE = "FUSED"
DBG = dict(ntile2a=None, do2c=True, nsteps=None, ret=True, stop=99)
POOL = "dve"


class Src:
    def __init__(self, sem, name):
        self.sem = sem
        self.count = 0
        self.name = name


class Buf:
    __slots__ = ("w", "r", "name", "excl")

    def __init__(self, name="", excl=False):
        self.w = None
        self.r = {}
        self.name = name
        self.excl = excl


class KB:
    ENGS = ("pe", "act", "dve", "pool", "sp")

    def __init__(self, nc, es):
        self.nc = nc
        self.es = es
        self.q = {n: [] for n in self.ENGS}
        self.src = {}
        for n in self.ENGS:
            self.src[n] = Src(es.enter_context(nc.semaphore("sem_" + n)), n)
        self.waited = {n: {} for n in self.ENGS}
        self.streams = []
        self.nops = 0
        self.capture = None

    def stream(self, name):
        s = Src(self.es.enter_context(self.nc.semaphore("dq_" + name)), name)
        self.streams.append(s)
        return s

    def _waits(self, eng, reads, writes):
        need = {}
        for b in reads:
            if b.w is not None:
                s, c = b.w
                if need.get(s, 0) < c:
                    need[s] = c
        for b in writes:
            if b.w is not None:
                s, c = b.w
                if need.get(s, 0) < c:
                    need[s] = c
            for s, c in b.r.items():
                if need.get(s, 0) < c:
                    need[s] = c
        out = []
        me = self.src[eng]
        wd = self.waited[eng]
        for s, c in need.items():
            if s is me and (eng == "pe" or eng == "sp" or not SAME_ENGINE_WAITS):
                continue
            if wd.get(s, 0) < c:
                wd[s] = c
                out.append((s.sem, c))
        return out

    def op(self, eng, meth, reads=(), writes=(), args=(), **kw):
        if self.capture is not None:
            self.capture.append(("op", (eng, meth, list(reads), list(writes), args), kw))
            return
        excl = [b for b in reads if b.excl]
        if excl:
            reads = [b for b in reads if not b.excl]
            writes = list(writes) + excl
        ws = self._waits(eng, reads, writes)
        me = self.src[eng]
        me.count += 1
        cnt = me.count
        sem = me.sem

        def thunk(h):
            for s, c in ws:
                h.wait_ge(s, c)
            getattr(h, meth)(*args, **kw).then_inc(sem, 1)

        self.q[eng].append(thunk)
        self.nops += 1
        for b in reads:
            if b.r.get(me, 0) < cnt:
                b.r[me] = cnt
        for b in writes:
            b.w = (me, cnt)
            b.r = {}

    def mm(self, out, lhsT, rhs, start, stop, reads, writes):
        self.op("pe", "matmul", reads, writes, args=(out,), lhsT=lhsT, rhs=rhs, start=start, stop=stop)

    def act(self, out, in_, func, reads, writes, **kw):
        self.op("act", "activation", reads, writes, out=out, in_=in_, func=func, **kw)

    def dma(self, out, in_, reads=(), writes=(), stream=None, eng="sp"):
        if self.capture is not None:
            self.capture.append(("dma", (out, in_, list(reads), list(writes), stream, eng), {}))
            return
        ws = self._waits(eng, reads, writes)
        stream.count += 16
        cnt = stream.count
        sem = stream.sem

        def thunk(h):
            for s, c in ws:
                h.wait_ge(s, c)
            src_ap = in_(h) if callable(in_) else in_
            h.dma_start(out=out, in_=src_ap).then_inc(sem, 16)

        self.q[eng].append(thunk)
        for b in reads:
            if b.r.get(stream, 0) < cnt:
                b.r[stream] = cnt
        for b in writes:
            b.w = (stream, cnt)
            b.r = {}

    def play(self, item):
        kind, a, kw = item
        assert self.capture is None
        if kind == "op":
            self.op(a[0], a[1], a[2], a[3], a[4], **kw)
        else:
            self.dma(a[0], a[1], a[2], a[3], a[4], a[5])

    def cc(self, kind, groups, in_ap, out_ap, reads=(), writes=()):
        ws = self._waits("pool", reads, writes)
        st = Src(self.es.enter_context(self.nc.semaphore("cc%d" % len(self.streams))), "cc")
        self.streams.append(st)
        st.count = 1
        sem = st.sem

        def thunk(h):
            for s, c in ws:
                h.wait_ge(s, c)
            h.collective_compute(kind, ALU.bypass, replica_groups=groups, ins=[in_ap], outs=[out_ap]).then_inc(sem, 1)

        self.q["pool"].append(thunk)
        for b in reads:
            b.r[st] = 1
        for b in writes:
            b.w = (st, 1)
            b.r = {}

    def barrier(self):
        allsrc = [self.src[n] for n in self.ENGS] + self.streams
        for eng in self.ENGS:
            ws = []
            wd = self.waited[eng]
            for s in allsrc:
                if s is self.src[eng] and eng == "sp":
                    continue
                if s.count > 0 and wd.get(s, 0) < s.count:
                    wd[s] = s.count
                    ws.append((s.sem, s.count))
            if ws:
                def thunk(h, ws=ws):
                    for s, c in ws:
                        h.wait_ge(s, c)
                self.q[eng].append(thunk)

    def final_wait(self, streams):
        ws = [(s.sem, s.count) for s in streams if s.count > 0]

        def thunk(h):
            for s, c in ws:
                h.wait_ge(s, c)
        self.q["sp"].append(thunk)

    def replay(self):
        nc = self.nc
        with nc.Block() as block:
            @block.tensor
            def _(h):
                for t in self.q["pe"]:
                    t(h)

            @block.scalar
            def _(h):
                for t in self.q["act"]:
                    t(h)

            @block.vector
            def _(h):
                for t in self.q["dve"]:
                    t(h)

            @block.gpsimd
            def _(h):
                for t in self.q["pool"]:
                    t(h)

            @block.sync
            def _(h):
                for t in self.q["sp"]:
                    t(h)


class Arena:
    def __init__(self, handle, nbytes):
        self.h = handle
        self.n = nbytes
        self.off = 0

    def alloc(self, dtype, *free):
        esz = 2 if dtype == BF16 else 4
        n = 1
        for f in free:
            n *= f
        nb = (n * esz + 31) // 32 * 32
        assert self.off + nb <= self.n, ("SBUF arena overflow", self.off, nb, self.n)
        w0 = self.off // 4
        ap = self.h[:, w0:w0 + nb // 4]
        if dtype == BF16:
            ap = ap.bitcast(BF16)
        ap = ap[:, 0:n]
        self.off += nb
        if len(free) == 2:
            ap = ap.rearrange("p (a b) -> p a b", a=free[0])
        elif len(free) == 3:
            ap = ap.rearrange("p (a b c) -> p a b c", a=free[0], b=free[1])
        return ap


def build(prog):
    fused = prog == "FUSED"
    nc = bass.Bass("TRN2", target_bir_lowering=False)
    es = contextlib.ExitStack()
    kb = KB(nc, es)

    def din(name, shape, dt):
        return nc.dram_tensor(name, list(shape), dt, kind="ExternalInput").ap()

    def dout(name, shape, dt):
        return nc.dram_tensor(name, list(shape), dt, kind="ExternalOutput").ap()

    need_p1 = {"P1_0": [0], "P3_0": [1], "FUSED": [0, 1]}.get(prog, [])
    need_p2 = {"P2_0": [0], "P2_1": [1], "FUSED": [0, 1]}.get(prog, [])
    need_p3 = {"P3_0": [0], "P3_1": [1], "FUSED": [0, 1]}.get(prog, [])
    need_ada = {"P1_0": [0], "P3_0": [0, 1], "P3_1": [1], "FUSED": [0, 1]}.get(prog, [])

    cbf_d = din("cbf", [128, 5, 128], BF16)
    dr = {}
    if need_ada:
        dr["cvec"] = din("cvec", [128, 8], F32)
        dr["normg"] = din("normg", [128, DEPTH, 8], F32)
        dr["finalg"] = din("finalg", [128, 8], F32)
        dr["bada"] = din("bada", [128, DEPTH, 24], F32)
        dr["wada"] = din("wada", [DEPTH, D, 3 * D], F32)
    if need_p3:
        dr["wout"] = din("wout", [DEPTH, D, D], F32)
    if need_p2:
        dr["win"] = din("win", [DEPTH, D, 1024], F32)
        dr["cosT"] = din("cosT", [128, S], F32)
        dr["sinT"] = din("sinT", [128, S], F32)
        dr["negmask"] = din("negmask", [128, 4, NT], BF16)
        dr["retc"] = din("retc", [128, 2 * NT + 2], F32)
    if prog in ("P1_0", "P3_0", "FUSED"):
        dr["xT"] = din("xT", [D, TQ], F32)
    if prog == "P3_1":
        dr["x1T_in"] = din("x1T_in", [D, TQ], F32)
    if prog == "P3_0":
        dr["x1T_out"] = dout("x1T_out", [D, TQ], F32)
    if prog in ("P1_0", "P3_0"):
        dr["hTq"] = dout("hTq", [D, TQ], BF16)
    if prog in ("P2_0", "P2_1"):
        dr["hTfull"] = din("hTfull", [4 * D, TQ], BF16)
        dr["ymT"] = dout("ymT", [256, S], BF16)
    if prog in ("P3_0", "P3_1"):
        dr["ymTfull"] = din("ymTfull", [4 * 256, TQ], BF16)
    if prog in ("P3_1", "FUSED"):
        dr["outT"] = dout("outT", [D, TQ], F32)

    ARENA_BYTES = 207 * 1024
    arena_h = es.enter_context(nc.sbuf_tensor("arena", [128, ARENA_BYTES // 4], F32))
    ar = Arena(arena_h, ARENA_BYTES)
    banks = [es.enter_context(nc.psum_tensor("bank%d" % i, [128, 512], F32)) for i in range(7)]
    bankT = es.enter_context(nc.psum_tensor("bankT", [128, 1024], BF16))
    BK = [Buf("bank%d" % i, excl=True) for i in range(7)]
    BKT = Buf("bankT", excl=True)

    st_const = kb.stream("const")
    cbf = ar.alloc(BF16, 5, 128)
    CBF = Buf("cbf")
    SMALL = Buf("small")
    kb.dma(cbf, cbf_d, writes=[CBF], stream=st_const)
    ident, negU, negones, ones_bf, perm = (cbf[:, i, :] for i in range(5))

    if need_ada:
        cvec = ar.alloc(F32, 8)
        normg = ar.alloc(F32, DEPTH, 8)
        finalg = ar.alloc(F32, 8)
        bada = ar.alloc(F32, DEPTH, 24)
        for a, d_ in ((cvec, dr["cvec"]), (normg, dr["normg"]), (finalg, dr["finalg"]), (bada, dr["bada"])):
            kb.dma(a, d_, writes=[SMALL], stream=st_const)
        cact = ar.alloc(F32, 8)
        ctmp = ar.alloc(F32, 8)
        mod = ar.alloc(F32, DEPTH, 24)
        gs = ar.alloc(F32, DEPTH, 8)
        MOD = Buf("mod")
        CACT = Buf("cact")
    CBF.w = (st_const, st_const.count)
    SMALL.w = (st_const, st_const.count)

    arena_mark = ar.off
    xt = XT = st_x = sq = SQ = lnv = LNV = rstd = RSTD = ntmp = NTMP = hto = HTO = st_ho = st_out = None
    wout_bf = WOUT = wst = WST = st_w = ymt = YMT = st_ym = None

    def alloc_rl(tag):
        nonlocal xt, XT, st_x, sq, SQ, lnv, LNV, rstd, RSTD, ntmp, NTMP, hto, HTO, st_ho, st_out
        nonlocal wout_bf, WOUT, wst, WST, st_w, ymt, YMT, st_ym
        ar.off = arena_mark
        xt = [ar.alloc(F32, 8, NT) for _ in range(2)]
        XT = [Buf("xt0"), Buf("xt1")]
        st_x = [kb.stream("x0" + tag), kb.stream("x1" + tag)]
        sq = ar.alloc(BF16, 8, NT)
        SQ = Buf("sq")
        lnv = ar.alloc(F32, NT)
        LNV = Buf("lnv")
        rstd = ar.alloc(F32, NT)
        RSTD = Buf("rstd")
        ntmp = [ar.alloc(F32, NT) for _ in range(2)]
        NTMP = [Buf("ntmp0"), Buf("ntmp1")]
        hto = [ar.alloc(BF16, 8, NT) for _ in range(2)]
        HTO = [Buf("hto0"), Buf("hto1")]
        st_ho = [kb.stream("ho0" + tag), kb.stream("ho1" + tag)]
        st_out = [kb.stream("out0" + tag), kb.stream("out1" + tag)]
        if need_p3:
            wout_bf = ar.alloc(BF16, 8, D)
            WOUT = Buf("wout")
            wst = [ar.alloc(F32, D) for _ in range(2)]
            WST = [Buf("wst0"), Buf("wst1")]
            st_w = [kb.stream("w0" + tag), kb.stream("w1" + tag)]
            ymt = [ar.alloc(BF16, 8, NT) for _ in range(2)]
            YMT = [Buf("ymt0"), Buf("ymt1")]
            st_ym = [kb.stream("ym0" + tag), kb.stream("ym1" + tag)]

    if need_ada:
        alloc_rl("a")
    final_streams = []

    def emit_silu_c():
        kb.act(ctmp, cvec, AF.Exp, [SMALL], [CACT], scale=-1.0)
        kb.act(ctmp, ctmp, AF.Ln, [CACT], [CACT], bias=1.0)
        kb.act(ctmp, ctmp, AF.Exp, [CACT], [CACT], scale=-1.0)
        kb.op("dve", "tensor_tensor", [CACT, SMALL], [CACT], out=cact, in0=ctmp, in1=cvec, op=ALU.mult)

    def emit_adaln(l):
        wv = dr["wada"][l].rearrange("(k p) f -> p k f", p=128)
        for grp in range(6):
            sl = grp % 2
            kb.dma(xt[sl], wv[:, :, grp * 512:(grp + 1) * 512], writes=[XT[sl]], stream=st_x[sl])
            for j in range(4):
                fb = grp * 4 + j
                for kc in range(8):
                    kb.mm(banks[0][:, fb:fb + 1], xt[sl][:, kc, j * 128:(j + 1) * 128], cact[:, kc:kc + 1],
                          kc == 0, kc == 7, [XT[sl], CACT], [BK[0]])
        kb.op("dve", "tensor_tensor", [BK[0], SMALL], [MOD], out=mod[:, l, :], in0=banks[0][:, 0:24], in1=bada[:, l, :], op=ALU.add)
        kb.op("dve", "scalar_tensor_tensor", [MOD, SMALL], [MOD], out=gs[:, l, :], in0=mod[:, l, 8:16], scalar=1.0,
              in1=normg[:, l, :], op0=ALU.add, op1=ALU.mult)

    def emit_norm_tile(sl, scale_ap, shift_ap, dst_view, final, after=None, DST=()):
        kb.act(sq, xt[sl], AF.Square, [XT[sl]], [SQ])
        for kc in range(8):
            kb.mm(banks[0][:, :], ones_bf, sq[:, kc, :], kc == 0, kc == 7, [SQ, CBF], [BK[0]])
        kb.act(lnv, banks[0][:, :], AF.Ln, [BK[0]], [LNV], scale=1.0 / D, bias=EPS)
        kb.act(rstd, lnv, AF.Exp, [LNV], [RSTD], scale=-0.5)
        if final:
            for kc in range(8):
                kb.op("dve", "scalar_tensor_tensor", [XT[sl], RSTD, MOD, SMALL], [XT[sl]], out=xt[sl][:, kc, :], in0=xt[sl][:, kc, :],
                      scalar=scale_ap[:, kc:kc + 1], in1=rstd, op0=ALU.mult, op1=ALU.mult)
            kb.dma(dst_view, xt[sl], reads=[XT[sl]], stream=st_out[sl])
        else:
            for kc in range(8):
                tm = kc % 2
                kb.op("dve", "scalar_tensor_tensor", [XT[sl], RSTD, MOD], [NTMP[tm]], out=ntmp[tm], in0=xt[sl][:, kc, :],
                      scalar=scale_ap[:, kc:kc + 1], in1=rstd, op0=ALU.mult, op1=ALU.mult)
                kb.act(hto[sl][:, kc, :], ntmp[tm], AF.Identity, [NTMP[tm], MOD], [HTO[sl]], bias=shift_ap[:, kc:kc + 1])
            kb.dma(dst_view, hto[sl], reads=[HTO[sl]] + list(DST), stream=st_ho[sl])
            if after is not None:
                after()

    def pkt(dram_ap):
        return dram_ap.rearrange("(k p) t -> p k t", p=128)

    def emit_phase1(l, x_src, h_view, h_after=None, H_DST=None):
        xv = pkt(x_src)
        for tt in range(TQ // NT):
            sl = tt % 2
            kb.dma(xt[sl], xv[:, :, tt * NT:(tt + 1) * NT], writes=[XT[sl]], stream=st_x[sl])
            emit_norm_tile(sl, gs[:, l, :], mod[:, l, 0:8], h_view(tt), final=False,
                           after=(None if h_after is None else (lambda tt=tt: h_after(tt))),
                           DST=([] if H_DST is None else [H_DST[tt]]))

    def emit_phase3(l, x_src, ym_view, x_dst, h_view, out_dst, h_after=None, H_DST=None, YM_SRC=()):
        for j in range(8):
            g_, half = j // 2, j % 2
            r0 = g_ * 128 if half == 0 else 512 + g_ * 128
            sl = j % 2
            kb.dma(wst[sl], dr["wout"][l, r0:r0 + 128, :], writes=[WST[sl]], stream=st_w[sl])
            kb.op("pool", "tensor_copy", [WST[sl]], [WOUT], out=wout_bf[:, j, :], in_=wst[sl])
        xv = pkt(x_src)
        for tt in range(TQ // NT):
            sl = tt % 2
            kb.dma(xt[sl], xv[:, :, tt * NT:(tt + 1) * NT], writes=[XT[sl]], stream=st_x[sl])
            kb.dma(ymt[sl], ym_view(tt), reads=list(YM_SRC), writes=[YMT[sl]], stream=st_ym[sl])
            for fo in range(8):
                bk = 1 + (fo % 2)
                for j in range(8):
                    kb.mm(banks[bk][:, :], wout_bf[:, j, fo * 128:(fo + 1) * 128], ymt[sl][:, j, :], j == 0, j == 7,
                          [WOUT, YMT[sl]], [BK[bk]])
                kb.op("dve", "scalar_tensor_tensor", [BK[bk], MOD, XT[sl]], [XT[sl]], out=xt[sl][:, fo, :], in0=banks[bk][:, :],
                      scalar=mod[:, l, 16 + fo:17 + fo], in1=xt[sl][:, fo, :], op0=ALU.mult, op1=ALU.add)
            if l + 1 < DEPTH:
                kb.dma(pkt(x_dst)[:, :, tt * NT:(tt + 1) * NT], xt[sl], reads=[XT[sl]], stream=st_out[sl])
                emit_norm_tile(sl, gs[:, l + 1, :], mod[:, l + 1, 0:8], h_view(tt), final=False,
                               after=(None if h_after is None else (lambda tt=tt: h_after(tt))),
                               DST=([] if H_DST is None else [H_DST[tt]]))
            else:
                emit_norm_tile(sl, finalg, None, pkt(out_dst)[:, :, tt * NT:(tt + 1) * NT], final=True)

    def emit_phase2(l, h_view, ym_view, H_SRC=None, Y_DST=None, y_after=None):
        ar.off = arena_mark
        win_bf = ar.alloc(BF16, 8, 1024)
        WIN = Buf("win")
        off_wst2 = ar.off
        wst2 = [ar.alloc(F32, 1024) for _ in range(2)]
        WST2 = [Buf("wst2_0"), Buf("wst2_1")]
        st_w2 = [kb.stream("w2_0_%d" % l), kb.stream("w2_1_%d" % l)]
        negmask = ar.alloc(BF16, 4, NT)
        retc = ar.alloc(F32, 2 * NT + 2)
        P2C = Buf("p2c")
        st_c2 = kb.stream("c2_%d" % l)
        kb.dma(negmask, dr["negmask"], writes=[P2C], stream=st_c2)
        kb.dma(retc, dr["retc"], writes=[P2C], stream=st_c2)
        dmask4 = retc[:, 0:NT]
        kdec = retc[:, NT:NT + 1]
        cdec = retc[:, NT + 1:NT + 2]
        qdec = retc[:, NT + 2:2 * NT + 2]

        qT = ar.alloc(BF16, S)
        kTA = ar.alloc(BF16, S)
        kTB = ar.alloc(BF16, S)
        sgT = ar.alloc(BF16, S)
        svA = ar.alloc(BF16, S // 128, 128)
        svB = ar.alloc(BF16, S // 128, 128)
        NTILE = S // NT
        QT = [Buf("qT%d" % i) for i in range(NTILE)]
        KTb = [Buf("kT%d" % i) for i in range(NTILE)]
        SG = [Buf("sg%d" % i) for i in range(NTILE)]
        SV = [Buf("sv%d" % i) for i in range(NTILE)]
        ZERO = Buf("zero")
        kb.op("dve", "memset", [], [ZERO], args=(kTA, 0.0))
        kb.op("dve", "memset", [], [ZERO], args=(kTB, 0.0))
        kb.op("dve", "memset", [], [ZERO], args=(svA, 0.0))
        kb.op("dve", "memset", [], [ZERO], args=(svB, 0.0))

        ht = [ar.alloc(BF16, 8, NT) for _ in range(2)]
        HT = [Buf("ht0"), Buf("ht1")]
        st_h = [kb.stream("h0_%d" % l), kb.stream("h1_%d" % l)]
        cs = [ar.alloc(F32, 2, NT) for _ in range(2)]
        CS = [Buf("cs0"), Buf("cs1")]
        st_cs = [kb.stream("cs0_%d" % l), kb.stream("cs1_%d" % l)]

        xbf = [ar.alloc(BF16, NT) for _ in range(2)]
        XBF = [Buf("xbf0"), Buf("xbf1")]
        t1_ = ar.alloc(F32, NT)
        t1 = [t1_, t1_]
        T1_ = Buf("t1")
        T1 = [T1_, T1_]
        t2_ = ar.alloc(F32, NT)
        t2 = [t2_, t2_]
        T2_ = Buf("t2")
        T2 = [T2_, T2_]
        rqT = ar.alloc(BF16, NT)
        rkT = ar.alloc(BF16, NT)
        qdT = ar.alloc(BF16, NT)
        RQ, RK, QD = Buf("rq"), Buf("rk"), Buf("qd")
        sil = ar.alloc(F32, NT)
        SIL = Buf("sil")
        vret = ar.alloc(BF16, 4, 128)
        rgs = ar.alloc(BF16, 4, 128)
        VRET = [Buf("vret%d" % i) for i in range(4)]
        RGS = [Buf("rgs%d" % i) for i in range(4)]
        rgraw = ar.alloc(F32, 4, 128)
        RGRAW = Buf("rgraw")
        kd4 = ar.alloc(BF16, 4, 128)
        KD4 = Buf("kd4")
        Sm4 = ar.alloc(BF16, 4, 128)
        SM4 = Buf("Sm4")
        PfX = [ar.alloc(F32, 9, 128) for _ in range(2)]
        PFX = [Buf("PfX0"), Buf("PfX1")]
        Pb8 = ar.alloc(BF16, 8, 128)
        PB8 = Buf("Pb8")
        stats4 = ar.alloc(F32, 4, 6)
        mv4 = ar.alloc(F32, 4, 2)
        gs4 = ar.alloc(F32, 3, 4)
        GN4 = Buf("gn4")
        on4 = ar.alloc(F32, 4, 128)
        ON4 = Buf("on4")
        ybf4 = ar.alloc(BF16, 4, 128)
        YBF4 = Buf("ybf4")
        yrT = [ar.alloc(BF16, NT) for _ in range(2)]
        YRT = [Buf("yrT0"), Buf("yrT1")]
        st_yr = [kb.stream("yr0_%d" % l), kb.stream("yr1_%d" % l)]

        for kc in range(8):
            sl = kc % 2
            kb.dma(wst2[sl], dr["win"][l, kc * 128:(kc + 1) * 128, :], writes=[WST2[sl]], stream=st_w2[sl])
            kb.op(POOL, "tensor_copy", [WST2[sl]], [WIN], out=win_bf[:, kc, :], in_=wst2[sl])

        kb.op("dve", "memset", [], [PFX[0]], args=(PfX[0][:, 0, :], 0.0))
        pst = 0


        def silu_from_psum(src_ap, SRC, tmp, TMP, dst_ap, DST):
            kb.act(tmp, src_ap, AF.Exp, [SRC], [TMP], scale=-1.0)
            kb.act(tmp, tmp, AF.Ln, [TMP], [TMP], bias=1.0)
            kb.act(tmp, tmp, AF.Exp, [TMP], [TMP], scale=-1.0)
            kb.op("dve", "tensor_tensor", [SRC, TMP], [DST], out=dst_ap, in0=src_ap, in1=tmp, op=ALU.mult)

        def emit_2a_tile(tt):
            nonlocal pst
            sl = tt % 2
            r, tq = tt // 4, (tt % 4) * NT
            c0 = tt * NT
            kb.dma(ht[sl], h_view(tt), reads=([] if H_SRC is None else [H_SRC[tt % 4]]), writes=[HT[sl]], stream=st_h[sl])
            kb.dma(cs[sl][:, 0, :], dr["cosT"][:, c0:c0 + NT], writes=[CS[sl]], stream=st_cs[sl])
            kb.dma(cs[sl][:, 1, :], dr["sinT"][:, c0:c0 + NT], writes=[CS[sl]], stream=st_cs[sl])

            def proj_fm(blk, bk):
                for kc in range(8):
                    kb.mm(banks[bk][:, :], win_bf[:, kc, blk * 128:(blk + 1) * 128], ht[sl][:, kc, :], kc == 0, kc == 7,
                          [WIN, HT[sl]], [BK[bk]])

            for which, (dstT, DST) in enumerate(((rqT, RQ), (rkT, RK))):
                bk = 5 + which
                pk = 6 - which
                proj_fm(which, bk)
                kb.act(xbf[which], banks[bk][:, :], AF.Identity, [BK[bk]], [XBF[which]])
                kb.mm(banks[pk][:, :], perm, xbf[which], True, True, [XBF[which], CBF], [BK[pk]])
                kb.op("dve", "tensor_tensor", [BK[bk], CS[sl]], [T1[which]], out=t1[which], in0=banks[bk][:, :], in1=cs[sl][:, 0, :], op=ALU.mult)
                kb.op("dve", "tensor_tensor", [BK[pk], CS[sl]], [T2[which]], out=t2[which], in0=banks[pk][:, :], in1=cs[sl][:, 1, :], op=ALU.mult)
                kb.op(POOL, "tensor_tensor", [T1[which], T2[which]], [DST], out=dstT, in0=t1[which], in1=t2[which], op=ALU.add)
                if which == 0:
                    kb.op("dve", "tensor_tensor", [T1[0], T2[0]], [T1[0]], out=t1[0], in0=t1[0], in1=t2[0], op=ALU.add)
                    kb.op(POOL, "tensor_tensor", [T1[0], P2C], [QD], out=qdT, in0=t1[0], in1=qdec, op=ALU.mult)
            proj_fm(2, 5)
            kb.act(qT[:, c0:c0 + NT], banks[5][:, :], AF.Identity, [BK[5]], [QT[tt]])
            proj_fm(3, 6)
            kb.act(kTA[0:64, c0:c0 + NT], banks[6][0:64, :], AF.Identity, [BK[6], ZERO], [KTb[tt]], scale=0.125)
            kb.act(kTB[64:128, c0:c0 + NT], banks[6][64:128, :], AF.Identity, [BK[6], ZERO], [KTb[tt]], scale=0.125)
            proj_fm(4, 5)
            silu_from_psum(banks[5][:, :], BK[5], sil, SIL, sgT[:, c0:c0 + NT], SG[tt])

            for st in range(4):
                gt = tt * 4 + st
                bt = 5 + (st % 2)
                for kc in range(8):
                    kb.mm(banks[bt][:, 0:384], ht[sl][:, kc, st * 128:(st + 1) * 128], win_bf[:, kc, 640:1024], kc == 0, kc == 7,
                          [WIN, HT[sl]], [BK[bt]])
                kb.op("dve", "tensor_copy", [BK[bt]], [VRET[st]], out=vret[:, st, :], in_=banks[bt][:, 0:128])
                kb.op("dve", "tensor_copy", [BK[bt]], [RGRAW], out=rgraw[:, st, :], in_=banks[bt][:, 128:256])
                kb.op("dve", "tensor_copy", [BK[bt], ZERO], [SV[tt]], out=svA[:, gt, 0:64], in_=banks[bt][:, 256:320])
                kb.op("dve", "tensor_copy", [BK[bt], ZERO], [SV[tt]], out=svB[:, gt, 64:128], in_=banks[bt][:, 320:384])
            kb.act(sil, rgraw, AF.Exp, [RGRAW], [SIL], scale=-1.0)
            kb.act(sil, sil, AF.Ln, [SIL], [SIL], bias=1.0)
            kb.act(sil, sil, AF.Exp, [SIL], [SIL], scale=-1.0)
            kb.op("dve", "tensor_tensor", [RGRAW, SIL], [RGS[0]], out=rgs, in0=rgraw, in1=sil, op=ALU.mult)
            for st in range(4):
                kb.op("pe", "transpose", [RK, CBF], [BKT], args=(bankT[:, st * 128:(st + 1) * 128], rkT[:, st * 128:(st + 1) * 128], ident))
            kb.op("dve", "tensor_scalar", [BKT, P2C], [KD4], out=kd4, in0=bankT[:, 0:512], scalar1=kdec, scalar2=None, op0=ALU.mult)
            for st in range(4):
                kb.mm(banks[5][:, st * 128:(st + 1) * 128], rkT[:, st * 128:(st + 1) * 128], rqT[:, st * 128:(st + 1) * 128], True, True,
                      [RK, RQ], [BK[5]])
            kb.op("dve", "tensor_tensor", [BK[5], P2C], [SM4], out=Sm4, in0=banks[5][:, :], in1=dmask4, op=ALU.mult)
            for st in range(4):
                kb.mm(banks[5][:, st * 128:(st + 1) * 128], kd4[0:64, st, :], vret[0:64, st, :], True, True, [KD4, VRET[st]], [BK[5]])
                kb.mm(banks[6][:, st * 128:(st + 1) * 128], kd4[64:128, st, :], vret[64:128, st, :], True, True, [KD4, VRET[st]], [BK[6]])
            pa = pst
            pbn = 1 - pst
            for c in range(8):
                bkv = 5 + (c % 2)
                src_kv = banks[bkv][:, (c // 2) * 128:(c // 2 + 1) * 128]
                if c < 7:
                    kb.op("dve", "scalar_tensor_tensor", [PFX[pa], BK[bkv], P2C], [PFX[pa]], out=PfX[pa][:, c + 1, :], in0=PfX[pa][:, c, :],
                          scalar=cdec, in1=src_kv, op0=ALU.mult, op1=ALU.add)
                else:
                    kb.op("dve", "scalar_tensor_tensor", [PFX[pa], BK[bkv], P2C], [PFX[pbn]], out=PfX[pbn][:, 0, :], in0=PfX[pa][:, c, :],
                          scalar=cdec, in1=src_kv, op0=ALU.mult, op1=ALU.add)
            kb.op("dve", "tensor_copy", [PFX[pa]], [PB8], out=Pb8, in_=PfX[pa][:, 0:8, :])
            pst = pbn
            for st in range(4):
                oc = slice(st * 128, (st + 1) * 128)
                kb.mm(banks[5][:, oc], Sm4[:, st, :], vret[:, st, :], True, False, [SM4, VRET[st]], [BK[5]])
                kb.mm(banks[5][0:64, oc], qdT[:, st * 128:st * 128 + 64], Pb8[:, 2 * st, :], False, True, [QD, PB8], [BK[5]])
                kb.mm(banks[5][64:128, oc], qdT[:, st * 128 + 64:(st + 1) * 128], Pb8[:, 2 * st + 1, :], False, True, [QD, PB8], [BK[5]])
            for st in range(4):
                kb.op("dve", "bn_stats", [BK[5]], [GN4], out=stats4[:, st, :], in_=banks[5][:, st * 128:(st + 1) * 128])
            for st in range(4):
                kb.op("dve", "bn_aggr", [GN4], [GN4], out=mv4[:, st, :], in_=stats4[:, st, :])
            kb.act(gs4[:, 0, :], mv4[:, :, 1], AF.Ln, [GN4], [GN4], bias=EPS)
            kb.act(gs4[:, 1, :], gs4[:, 0, :], AF.Exp, [GN4], [GN4], scale=-0.5)
            kb.op("dve", "scalar_tensor_tensor", [GN4], [GN4], out=gs4[:, 2, :], in0=mv4[:, :, 0], scalar=-1.0, in1=gs4[:, 1, :],
                  op0=ALU.mult, op1=ALU.mult)
            for st in range(4):
                kb.act(on4[:, st, :], banks[5][:, st * 128:(st + 1) * 128], AF.Identity, [BK[5], GN4], [ON4],
                       bias=gs4[:, 2, st:st + 1], scale=gs4[:, 1, st:st + 1])
            kb.op("dve", "tensor_tensor", [ON4, RGS[0]], [YBF4], out=ybf4, in0=on4, in1=rgs, op=ALU.mult)
            for st in range(4):
                kb.op("pe", "transpose", [YBF4, CBF], [BKT], args=(bankT[:, 512 + st * 128:512 + (st + 1) * 128], ybf4[:, st, :], ident))
            kb.op("dve", "tensor_copy", [BKT], [YRT[sl]], out=yrT[sl], in_=bankT[:, 512:1024])
            kb.dma(ym_view(tt, 0), yrT[sl], reads=[YRT[sl]] + ([] if Y_DST is None else [Y_DST[tt // 4]]), stream=st_yr[sl])

        ntile2a = NTILE if DBG['ntile2a'] is None else DBG['ntile2a']

        def capture_tile(tt):
            kb.capture = []
            emit_2a_tile(tt)
            cap = kb.capture
            kb.capture = None
            return cap

        if INTERLEAVE and DBG['do2c']:
            for tt in range(min(LEAD, ntile2a)):
                emit_2a_tile(tt)
        else:
            for tt in range(ntile2a):
                emit_2a_tile(tt)

        NE, NSP, NA = 4, 4, 3
        off_save = ar.off
        ar.off = off_wst2
        Eb = [ar.alloc(F32, NT) for _ in range(NE)]
        ar.off = off_save
        EB = [Buf("E%d" % i) for i in range(NE)]
        SPb = [ar.alloc(BF16, NT) for _ in range(NSP)]
        SPB = [Buf("SP%d" % i) for i in range(NSP)]
        Ab = [ar.alloc(BF16, NT) for _ in range(NA)]
        AB = [Buf("A%d" % i) for i in range(NA)]
        Rb = [[ar.alloc(BF16, NT) for _ in range(2)] for _ in range(2)]
        RB = [[Buf("R%d%d" % (i, j)) for j in range(2)] for i in range(2)]
        ysb = [ar.alloc(BF16, NT) for _ in range(2)]
        YSB = [Buf("ysb0"), Buf("ysb1")]
        st_ys = [kb.stream("ys0_%d" % l), kb.stream("ys1_%d" % l)]
        kTs = (kTA, kTB)
        svs = (svA, svB)

        steps = []
        for qi in range(NTILE):
            blocks = [(4 * qi + o4, o4) for o4 in (3, 2, 1, 0)] + [(kbk, None) for kbk in range(4 * qi - 1, -1, -1)]
            nb = len(blocks)
            for k, (kbk, o4) in enumerate(blocks):
                for hd in range(2):
                    steps.append(dict(qi=qi, hd=hd, k=k, kbk=kbk, o4=o4, first=(k == 0), last=(k == nb - 1)))
        n = len(steps) if DBG['nsteps'] is None else DBG['nsteps']
        if not DBG['do2c']:
            return st_yr

        def emit_Z(s):
            stp = steps[s]
            zb = s % 2
            q0 = stp["qi"] * NT
            kT_ = kTs[stp["hd"]]
            kbk = stp["kbk"]
            diag = stp["o4"] is not None
            kb.mm(banks[zb][:, :], kT_[:, kbk * 128:(kbk + 1) * 128], qT[:, q0:q0 + NT], True, not diag,
                  [KTb[kbk // 4], QT[stp["qi"]], ZERO], [BK[zb]])
            if diag:
                kb.mm(banks[zb][:, :], ident, negmask[:, stp["o4"], :], False, True, [CBF, P2C], [BK[zb]])

        def emit_E(s):
            zb = s % 2
            kb.act(Eb[s % NE], banks[zb][:, :], AF.Exp, [BK[zb]], [EB[s % NE]])

        def emit_SP(s):
            kb.act(SPb[s % NSP], Eb[s % NE], AF.Ln, [EB[s % NE]], [SPB[s % NSP]], bias=1.0)

        def emit_R(s):
            stp = steps[s]
            if stp["last"]:
                return
            hd, k = stp["hd"], stp["k"]
            if stp["first"]:
                kb.op(POOL, "tensor_copy", [SPB[s % NSP]], [RB[hd][1]], out=Rb[hd][1], in_=SPb[s % NSP])
            else:
                kb.op(POOL, "tensor_tensor", [RB[hd][k % 2], SPB[s % NSP]], [RB[hd][(k + 1) % 2]], out=Rb[hd][(k + 1) % 2],
                      in0=Rb[hd][k % 2], in1=SPb[s % NSP], op=ALU.add)

        def emit_PA(s):
            stp = steps[s]
            pb = 2 + (s % 2)
            hd, k = stp["hd"], stp["k"]
            kb.mm(banks[pb][:, :], negU, SPb[s % NSP], True, stp["first"], [CBF, SPB[s % NSP]], [BK[pb]])
            if not stp["first"]:
                kb.mm(banks[pb][:, :], negones, Rb[hd][k % 2], False, True, [CBF, RB[hd][k % 2]], [BK[pb]])

        def emit_A(s):
            pb = 2 + (s % 2)
            kb.act(banks[pb][:, :], banks[pb][:, :], AF.Exp, [BK[pb]], [BK[pb]])
            kb.op("dve", "tensor_tensor", [BK[pb], EB[s % NE]], [AB[s % NA]], out=Ab[s % NA], in0=banks[pb][:, :], in1=Eb[s % NE], op=ALU.mult)

        def emit_AV(s):
            stp = steps[s]
            qi, hd, kbk = stp["qi"], stp["hd"], stp["kbk"]
            ob = 4
            first = stp["first"] and hd == 0
            last = stp["last"] and hd == 1
            kb.mm(banks[ob][:, :], svs[hd][:, kbk, :], Ab[s % NA], first, last, [SV[kbk // 4], ZERO, AB[s % NA]], [BK[ob]])
            if last:
                ys = qi % 2
                q0 = qi * NT
                kb.op("dve", "tensor_tensor", [BK[ob], SG[qi]], [YSB[ys]], out=ysb[ys], in0=banks[ob][:, :], in1=sgT[:, q0:q0 + NT], op=ALU.mult)
                kb.dma(ym_view(qi, 1), ysb[ys], reads=[YSB[ys]] + ([] if Y_DST is None else [Y_DST[qi // 4]]), stream=st_ys[ys])
                if y_after is not None and qi % 4 == 3:
                    y_after(qi // 4)

        pend = []
        state = dict(done=(min(LEAD, ntile2a) - 1) if INTERLEAVE else ntile2a - 1, nxt=min(LEAD, ntile2a), rate=RATE)
        last_of_tile = {}
        first_of_tile = {}
        for i_, stp_ in enumerate(steps[:n]):
            first_of_tile.setdefault(stp_["qi"], i_)
        for i_, stp_ in enumerate(steps[:n]):
            last_of_tile[stp_["qi"]] = i_

        def ensure(qi):
            while state["done"] < min(qi, ntile2a - 1):
                if not pend:
                    pend.extend(capture_tile(state["nxt"]))
                for it in pend:
                    kb.play(it)
                del pend[:]
                state["done"] = state["nxt"]
                state["nxt"] += 1

        def feed(s):
            if not INTERLEAVE:
                return
            if not pend and state["nxt"] < ntile2a:
                pend.extend(capture_tile(state["nxt"]))
            rate = state["rate"]
            k = min(rate, len(pend))
            for it in pend[:k]:
                kb.play(it)
            del pend[:k]
            if k and not pend:
                state["done"] = state["nxt"]
                state["nxt"] += 1

        emit_Z(0)
        emit_E(0)
        if n > 1:
            emit_Z(1)
        for s in range(0, n + 1):
            ensure(steps[min(s + 2, n - 1)]["qi"])
            if s < n:
                emit_SP(s)
                emit_R(s)
                emit_PA(s)
            if 0 <= s - 1 < n:
                emit_A(s - 1)
                emit_AV(s - 1)
            if s + 1 < n:
                emit_E(s + 1)
            if s + 2 < n:
                emit_Z(s + 2)
            feed(s)
        ensure(ntile2a - 1)
        DBG['arena_end'] = ar.off
        return st_yr + st_ys

    def hview_unfused(dram):
        hv = dram.rearrange("(r k p) t -> p r k t", p=128, k=8)
        return lambda tt: hv[:, tt // 4, :, (tt % 4) * NT:(tt % 4 + 1) * NT]

    def yview_unfused(dram):
        return lambda tile, half: dram[half * 128:(half + 1) * 128, tile * NT:(tile + 1) * NT]

    if prog == "P1_0":
        emit_silu_c()
        emit_adaln(0)
        emit_phase1(0, dr["xT"], lambda tt: pkt(dr["hTq"])[:, :, tt * NT:(tt + 1) * NT])
        final_streams += st_ho
    elif prog in ("P2_0", "P2_1"):
        l = int(prog[-1])
        final_streams += emit_phase2(l, hview_unfused(dr["hTfull"]), yview_unfused(dr["ymT"]))
    elif prog == "P3_0":
        emit_silu_c()
        emit_adaln(0)
        emit_adaln(1)
        ymv = dr["ymTfull"].rearrange("(j p) t -> p j t", p=128)
        emit_phase3(0, dr["xT"], lambda tt: ymv[:, :, tt * NT:(tt + 1) * NT], dr["x1T_out"],
                    lambda tt: pkt(dr["hTq"])[:, :, tt * NT:(tt + 1) * NT], None)
        final_streams += st_ho + st_out
    elif prog == "P3_1":
        emit_silu_c()
        emit_adaln(1)
        ymv = dr["ymTfull"].rearrange("(j p) t -> p j t", p=128)
        emit_phase3(1, dr["x1T_in"], lambda tt: ymv[:, :, tt * NT:(tt + 1) * NT], None, None, dr["outT"])
        final_streams += st_out
    elif fused:
        groups = [[0, 1, 2, 3], [4, 5, 6, 7]]
        x1T = nc.dram_tensor("x1T_scr", [D, TQ], F32).ap()
        hsrc = [[nc.dram_tensor("hsrc_%d_%d" % (l, t), [D, NT], BF16) for t in range(4)] for l in range(DEPTH)]
        hgat = [[nc.dram_tensor("hgat_%d_%d" % (l, t), [4 * D, NT], BF16) for t in range(4)] for l in range(DEPTH)]
        ysrc = [[nc.dram_tensor("ysrc_%d_%d" % (l, q), [256, TQ], BF16) for q in range(4)] for l in range(DEPTH)]
        ygat = [nc.dram_tensor("ygat_%d" % l, [4 * 1024, TQ], BF16) for l in range(DEPTH)]
        HS = [[Buf("hs") for _ in range(4)] for _ in range(DEPTH)]
        HG = [[Buf("hg") for _ in range(4)] for _ in range(DEPTH)]
        YS = [[Buf("ys") for _ in range(4)] for _ in range(DEPTH)]
        YG = [Buf("yg") for _ in range(DEPTH)]
        rank_cache = {}

        def myrank(h):
            if id(h) not in rank_cache:
                rank_cache[id(h)] = h.partition_id() % 4
            return rank_cache[id(h)]

        def h_view_dst(l):
            return lambda tt: pkt(hsrc[l][tt].ap())

        def h_after(l):
            def f(tt):
                kb.cc("AllGather", groups, hsrc[l][tt].ap().opt(), hgat[l][tt].ap().opt(), reads=[], writes=[HS[l][tt], HG[l][tt]])
            return f

        def h_view_src(l):
            def f(tt):
                hv = hgat[l][tt % 4].ap().rearrange("(r k p) t -> p r k t", p=128, k=8)
                return hv[:, tt // 4, :, :]
            return f

        def y_view_dst(l):
            return lambda tile, half: ysrc[l][tile // 4].ap()[half * 128:(half + 1) * 128, (tile % 4) * NT:(tile % 4 + 1) * NT]

        def y_after(l):
            def f(q):
                kb.cc("AllGather", groups, ysrc[l][q].ap().opt(), ygat[l].ap()[q * 1024:(q + 1) * 1024, :].opt(),
                      reads=[], writes=[YS[l][q], YG[l]])
            return f

        def y_view_src(l):
            def f(tt):
                def g(h):
                    v = ygat[l].ap().rearrange("(qj p) t -> p qj t", p=128)
                    return v[:, bass.ds(myrank(h) * 8, 8), tt * NT:(tt + 1) * NT]
                return g
            return f

        emit_silu_c()
        emit_adaln(0)
        emit_adaln(1)
        emit_phase1(0, dr["xT"], h_view_dst(0), h_after(0), HS[0])
        for l in range(DEPTH):
            kb.barrier()
            emit_phase2(l, h_view_src(l), y_view_dst(l), H_SRC=HG[l], Y_DST=YS[l], y_after=y_after(l))
            kb.barrier()
            alloc_rl("p3_%d" % l)
            if l + 1 < DEPTH:
                emit_phase3(l, dr["xT"] if l == 0 else x1T, y_view_src(l), x1T, h_view_dst(l + 1), None,
                            h_after=h_after(l + 1), H_DST=HS[l + 1], YM_SRC=[YG[l]])
            else:
                emit_phase3(l, x1T, y_view_src(l), None, None, dr["outT"], YM_SRC=[YG[l]])
        final_streams += st_out
    else:
        raise NotImplementedError(prog)

    kb.final_wait(final_streams)
    kb.replay()
    return nc, es


def _bf(a):
    return np.asarray(a, dtype=np.float32).astype(ml_dtypes.bfloat16)


def _consts():
    ident = np.eye(128, dtype=np.float32)
    jj, ss = np.meshgrid(np.arange(128), np.arange(128), indexing="ij")
    negU = np.where(jj >= ss, -1.0, 0.0).astype(np.float32)
    negones = -np.ones((128, 128), np.float32)
    ones = np.ones((128, 128), np.float32)
    perm = np.zeros((128, 128), np.float32)
    for d in range(128):
        perm[(d + 64) % 128, d] = 1.0
    cbf = _bf(np.stack([ident, negU, negones, ones, perm], axis=1))
    i = np.arange(128)[:, None, None]
    o4 = np.arange(4)[None, :, None]
    j = np.arange(NT)[None, None, :]
    negmask = _bf(np.where(j > o4 * 128 + i, 0.0, -BIG))
    half = 64
    inv = (10000.0 ** (-(np.arange(half, dtype=np.float32) / np.float32(half)))).astype(np.float32)
    pos = np.arange(S, dtype=np.float32)
    ang = (pos[None, :] * inv[:, None]).astype(np.float32)
    cos = np.cos(ang).astype(np.float32)
    sin = np.sin(ang).astype(np.float32)
    cosT = np.concatenate([cos, cos], axis=0)
    sinT = np.concatenate([-sin, sin], axis=0)
    return cbf, negmask, np.ascontiguousarray(cosT), np.ascontiguousarray(sinT)


def _retc(g):
    lg = np.log1p(-(2.0 ** (-5.0 - g)))
    m = np.arange(128)
    same = (m[:, None] // 64) == (m[None, :] // 64)
    dm = np.where(same, np.exp(np.abs(m[:, None] - m[None, :]) * lg), 0.0) * (128.0 ** -0.5)
    kdec = np.exp((63.0 - (m % 64)) * lg) * (128.0 ** -0.5)
    cdec = np.full(128, np.exp(64.0 * lg))
    c = np.arange(NT)
    qdec = np.broadcast_to(np.exp(((c % 64) + 1.0) * lg)[None, :], (128, NT))
    return np.ascontiguousarray(np.concatenate([dm, dm, dm, dm, kdec[:, None], cdec[:, None], qdec], axis=1).astype(np.float32))


def _vec(v):
    return np.ascontiguousarray(np.asarray(v, np.float32).reshape(-1, 128).T)


_NC_CACHE = {}


def _get(prog):
    if prog not in _NC_CACHE:
        _NC_CACHE[prog] = build(prog)
    return _NC_CACHE[prog][0]


def _run(prog, in_maps):
    nc = _get(prog)
    res = run_bass_kernel_spmd(nc, in_maps, core_ids=list(range(NCORES)))
    return res.results


def kernel(x, c, norm_g, w_ada, b_ada, w_in, w_out, final_g):
    x = np.asarray(x, np.float32)
    c = np.asarray(c, np.float32)
    norm_g = np.asarray(norm_g, np.float32)
    w_ada = np.ascontiguousarray(np.asarray(w_ada, np.float32))
    b_ada = np.asarray(b_ada, np.float32)
    w_in = np.asarray(w_in, np.float32)
    w_out = np.ascontiguousarray(np.asarray(w_out, np.float32))
    final_g = np.asarray(final_g, np.float32)

    cbf, negmask, cosT, sinT = _consts()
    normg_l = np.ascontiguousarray(np.stack([_vec(norm_g[l]) for l in range(DEPTH)], axis=1))
    bada_l = np.ascontiguousarray(np.stack([np.asarray(b_ada[l]).reshape(24, 128).T for l in range(DEPTH)], axis=1))
    finalg_l = _vec(final_g)

    base = []
    for core in range(NCORES):
        b, g = core // 4, core % 4
        cols = np.concatenate([
            np.arange(g * 128, (g + 1) * 128),
            512 + np.arange(g * 128, (g + 1) * 128),
            2048 + np.arange(g * 128, (g + 1) * 128),
            2560 + np.arange(g * 128, (g + 1) * 128),
            3584 + np.arange(g * 128, (g + 1) * 128),
            1024 + np.arange(g * 128, (g + 1) * 128),
            1536 + np.arange(g * 128, (g + 1) * 128),
            3072 + np.arange(g * 128, (g + 1) * 128),
        ])
        base.append(dict(
            b=b, g=g,
            cbf=cbf, negmask=negmask, cosT=cosT, sinT=sinT, retc=_retc(g),
            cvec=_vec(c[b]), normg=normg_l, finalg=finalg_l, bada=bada_l, wada=w_ada, wout=w_out,
            win=np.ascontiguousarray(w_in[:, :, cols]),
            xT=np.ascontiguousarray(x[b, g * TQ:(g + 1) * TQ, :].T),
        ))

    def pick(core, names):
        return {k: base[core][k] for k in names}

    P13 = ["cbf", "cvec", "normg", "finalg", "bada", "wada"]
    P2 = ["cbf", "win", "cosT", "sinT", "negmask", "retc"]

    def gather_h(res):
        out = []
        for core in range(NCORES):
            b = core // 4
            out.append(np.ascontiguousarray(np.concatenate([res[b * 4 + r]["hTq"] for r in range(4)], axis=0)))
        return out

    def gather_ym(res):
        out = []
        for core in range(NCORES):
            b, g = core // 4, core % 4
            full = np.concatenate([res[b * 4 + r]["ymT"] for r in range(4)], axis=0)
            out.append(np.ascontiguousarray(full[:, g * TQ:(g + 1) * TQ]))
        return out

    if MODE == "FUSED":
        names = ["cbf", "cvec", "normg", "finalg", "bada", "wada", "wout", "win", "cosT", "sinT", "negmask", "retc", "xT"]
        res = _run("FUSED", [pick(i, names) for i in range(NCORES)])
        out = np.empty((B, S, D), np.float32)
        for core in range(NCORES):
            b, g = core // 4, core % 4
            out[b, g * TQ:(g + 1) * TQ, :] = res[core]["outT"].T
        return out

    r1 = _run("P1_0", [dict(pick(i, P13 + ["xT"])) for i in range(NCORES)])
    hfull = gather_h(r1)
    r2 = _run("P2_0", [dict(pick(i, P2), hTfull=hfull[i]) for i in range(NCORES)])
    ymfull = gather_ym(r2)
    r3 = _run("P3_0", [dict(pick(i, P13 + ["xT", "wout"]), ymTfull=ymfull[i]) for i in range(NCORES)])
    hfull = gather_h(r3)
    r4 = _run("P2_1", [dict(pick(i, P2), hTfull=hfull[i]) for i in range(NCORES)])
    ymfull = gather_ym(r4)
    r5 = _run("P3_1", [dict(pick(i, P13 + ["wout"]), ymTfull=ymfull[i], x1T_in=r3[i]["x1T_out"]) for i in range(NCORES)])

    out = np.empty((B, S, D), np.float32)
    for core in range(NCORES):
        b, g = core // 4, core % 4
        out[b, g * TQ:(g + 1) * TQ, :] = r5[core]["outT"].T
    return out
```

```python
import contextlib
import numpy as np
import ml_dtypes
import concourse.bass as bass
import concourse.mybir as mybir
from concourse.bass_utils import run_bass_kernel_spmd

F32 = mybir.dt.float32
BF16 = mybir.dt.bfloat16
AF = mybir.ActivationFunctionType
ALU = mybir.AluOpType

D = 1024
B = 2
S = 8192
DEPTH = 2
NCORES = 8
TQ = S // 4
NT = 512
EPS = 1e-6
BIG = 32768.0
SAME_ENGINE_WAITS = True
INTERLEAVE = True
LEAD = 2
RATE = 2
MODE = "FUSED"
DBG = dict(ntile2a=None, do2c=True, nsteps=None, ret=True, stop=99)
POOL = "dve"


class Src:
    def __init__(self, sem, name):
        self.sem = sem
        self.count = 0
        self.name = name


class Buf:
    __slots__ = ("w", "r", "name", "excl")

    def __init__(self, name="", excl=False):
        self.w = None
        self.r = {}
        self.name = name
        self.excl = excl


class KB:
    ENGS = ("pe", "act", "dve", "pool", "sp")

    def __init__(self, nc, es):
        self.nc = nc
        self.es = es
        self.q = {n: [] for n in self.ENGS}
        self.src = {}
        for n in self.ENGS:
            self.src[n] = Src(es.enter_context(nc.semaphore("sem_" + n)), n)
        self.waited = {n: {} for n in self.ENGS}
        self.streams = []
        self.nops = 0
        self.capture = None

    def stream(self, name):
        s = Src(self.es.enter_context(self.nc.semaphore("dq_" + name)), name)
        self.streams.append(s)
        return s

    def _waits(self, eng, reads, writes):
        need = {}
        for b in reads:
            if b.w is not None:
                s, c = b.w
                if need.get(s, 0) < c:
                    need[s] = c
        for b in writes:
            if b.w is not None:
                s, c = b.w
                if need.get(s, 0) < c:
                    need[s] = c
            for s, c in b.r.items():
                if need.get(s, 0) < c:
                    need[s] = c
        out = []
        me = self.src[eng]
        wd = self.waited[eng]
        for s, c in need.items():
            if s is me and (eng == "pe" or eng == "sp" or not SAME_ENGINE_WAITS):
                continue
            if wd.get(s, 0) < c:
                wd[s] = c
                out.append((s.sem, c))
        return out

    def op(self, eng, meth, reads=(), writes=(), args=(), **kw):
        if self.capture is not None:
            self.capture.append(("op", (eng, meth, list(reads), list(writes), args), kw))
            return
        excl = [b for b in reads if b.excl]
        if excl:
            reads = [b for b in reads if not b.excl]
            writes = list(writes) + excl
        ws = self._waits(eng, reads, writes)
        me = self.src[eng]
        me.count += 1
        cnt = me.count
        sem = me.sem

        def thunk(h):
            for s, c in ws:
                h.wait_ge(s, c)
            getattr(h, meth)(*args, **kw).then_inc(sem, 1)

        self.q[eng].append(thunk)
        self.nops += 1
        for b in reads:
            if b.r.get(me, 0) < cnt:
                b.r[me] = cnt
        for b in writes:
            b.w = (me, cnt)
            b.r = {}

    def mm(self, out, lhsT, rhs, start, stop, reads, writes):
        self.op("pe", "matmul", reads, writes, args=(out,), lhsT=lhsT, rhs=rhs, start=start, stop=stop)

    def act(self, out, in_, func, reads, writes, **kw):
        self.op("act", "activation", reads, writes, out=out, in_=in_, func=func, **kw)

    def dma(self, out, in_, reads=(), writes=(), stream=None, eng="sp"):
        if self.capture is not None:
            self.capture.append(("dma", (out, in_, list(reads), list(writes), stream, eng), {}))
            return
        ws = self._waits(eng, reads, writes)
        stream.count += 16
        cnt = stream.count
        sem = stream.sem

        def thunk(h):
            for s, c in ws:
                h.wait_ge(s, c)
            src_ap = in_(h) if callable(in_) else in_
            h.dma_start(out=out, in_=src_ap).then_inc(sem, 16)

        self.q[eng].append(thunk)
        for b in reads:
            if b.r.get(stream, 0) < cnt:
                b.r[stream] = cnt
        for b in writes:
            b.w = (stream, cnt)
            b.r = {}

    def play(self, item):
        kind, a, kw = item
        assert self.capture is None
        if kind == "op":
            self.op(a[0], a[1], a[2], a[3], a[4], **kw)
        else:
            self.dma(a[0], a[1], a[2], a[3], a[4], a[5])

    def cc(self, kind, groups, in_ap, out_ap, reads=(), writes=()):
        ws = self._waits("pool", reads, writes)
        st = Src(self.es.enter_context(self.nc.semaphore("cc%d" % len(self.streams))), "cc")
        self.streams.append(st)
        st.count = 1
        sem = st.sem

        def thunk(h):
            for s, c in ws:
                h.wait_ge(s, c)
            h.collective_compute(kind, ALU.bypass, replica_groups=groups, ins=[in_ap], outs=[out_ap]).then_inc(sem, 1)

        self.q["pool"].append(thunk)
        for b in reads:
            b.r[st] = 1
        for b in writes:
            b.w = (st, 1)
            b.r = {}

    def barrier(self):
        allsrc = [self.src[n] for n in self.ENGS] + self.streams
        for eng in self.ENGS:
            ws = []
            wd = self.waited[eng]
            for s in allsrc:
                if s is self.src[eng] and eng == "sp":
                    continue
                if s.count > 0 and wd.get(s, 0) < s.count:
                    wd[s] = s.count
                    ws.append((s.sem, s.count))
            if ws:
                def thunk(h, ws=ws):
                    for s, c in ws:
                        h.wait_ge(s, c)
                self.q[eng].append(thunk)

    def final_wait(self, streams):
        ws = [(s.sem, s.count) for s in streams if s.count > 0]

        def thunk(h):
            for s, c in ws:
                h.wait_ge(s, c)
        self.q["sp"].append(thunk)

    def replay(self):
        nc = self.nc
        with nc.Block() as block:
            @block.tensor
            def _(h):
                for t in self.q["pe"]:
                    t(h)

            @block.scalar
            def _(h):
                for t in self.q["act"]:
                    t(h)

            @block.vector
            def _(h):
                for t in self.q["dve"]:
                    t(h)

            @block.gpsimd
            def _(h):
                for t in self.q["pool"]:
                    t(h)

            @block.sync
            def _(h):
                for t in self.q["sp"]:
                    t(h)


class Arena:
    def __init__(self, handle, nbytes):
        self.h = handle
        self.n = nbytes
        self.off = 0

    def alloc(self, dtype, *free):
        esz = 2 if dtype == BF16 else 4
        n = 1
        for f in free:
            n *= f
        nb = (n * esz + 31) // 32 * 32
        assert self.off + nb <= self.n, ("SBUF arena overflow", self.off, nb, self.n)
        w0 = self.off // 4
        ap = self.h[:, w0:w0 + nb // 4]
        if dtype == BF16:
            ap = ap.bitcast(BF16)
        ap = ap[:, 0:n]
        self.off += nb
        if len(free) == 2:
            ap = ap.rearrange("p (a b) -> p a b", a=free[0])
        elif len(free) == 3:
            ap = ap.rearrange("p (a b c) -> p a b c", a=free[0], b=free[1])
        return ap


def build(prog):
    fused = prog == "FUSED"
    nc = bass.Bass("TRN2", target_bir_lowering=False)
    es = contextlib.ExitStack()
    kb = KB(nc, es)

    def din(name, shape, dt):
        return nc.dram_tensor(name, list(shape), dt, kind="ExternalInput").ap()

    def dout(name, shape, dt):
        return nc.dram_tensor(name, list(shape), dt, kind="ExternalOutput").ap()

    need_p1 = {"P1_0": [0], "P3_0": [1], "FUSED": [0, 1]}.get(prog, [])
    need_p2 = {"P2_0": [0], "P2_1": [1], "FUSED": [0, 1]}.get(prog, [])
    need_p3 = {"P3_0": [0], "P3_1": [1], "FUSED": [0, 1]}.get(prog, [])
    need_ada = {"P1_0": [0], "P3_0": [0, 1], "P3_1": [1], "FUSED": [0, 1]}.get(prog, [])

    cbf_d = din("cbf", [128, 5, 128], BF16)
    dr = {}
    if need_ada:
        dr["cvec"] = din("cvec", [128, 8], F32)
        dr["normg"] = din("normg", [128, DEPTH, 8], F32)
        dr["finalg"] = din("finalg", [128, 8], F32)
        dr["bada"] = din("bada", [128, DEPTH, 24], F32)
        dr["wada"] = din("wada", [DEPTH, D, 3 * D], F32)
    if need_p3:
        dr["wout"] = din("wout", [DEPTH, D, D], F32)
    if need_p2:
        dr["win"] = din("win", [DEPTH, D, 1024], F32)
        dr["cosT"] = din("cosT", [128, S], F32)
        dr["sinT"] = din("sinT", [128, S], F32)
        dr["negmask"] = din("negmask", [128, 4, NT], BF16)
        dr["retc"] = din("retc", [128, 2 * NT + 2], F32)
    if prog in ("P1_0", "P3_0", "FUSED"):
        dr["xT"] = din("xT", [D, TQ], F32)
    if prog == "P3_1":
        dr["x1T_in"] = din("x1T_in", [D, TQ], F32)
    if prog == "P3_0":
        dr["x1T_out"] = dout("x1T_out", [D, TQ], F32)
    if prog in ("P1_0", "P3_0"):
        dr["hTq"] = dout("hTq", [D, TQ], BF16)
    if prog in ("P2_0", "P2_1"):
        dr["hTfull"] = din("hTfull", [4 * D, TQ], BF16)
        dr["ymT"] = dout("ymT", [256, S], BF16)
    if prog in ("P3_0", "P3_1"):
        dr["ymTfull"] = din("ymTfull", [4 * 256, TQ], BF16)
    if prog in ("P3_1", "FUSED"):
        dr["outT"] = dout("outT", [D, TQ], F32)

    ARENA_BYTES = 207 * 1024
    arena_h = es.enter_context(nc.sbuf_tensor("arena", [128, ARENA_BYTES // 4], F32))
    ar = Arena(arena_h, ARENA_BYTES)
    banks = [es.enter_context(nc.psum_tensor("bank%d" % i, [128, 512], F32)) for i in range(7)]
    bankT = es.enter_context(nc.psum_tensor("bankT", [128, 1024], BF16))
    BK = [Buf("bank%d" % i, excl=True) for i in range(7)]
    BKT = Buf("bankT", excl=True)

    st_const = kb.stream("const")
    cbf = ar.alloc(BF16, 5, 128)
    CBF = Buf("cbf")
    SMALL = Buf("small")
    kb.dma(cbf, cbf_d, writes=[CBF], stream=st_const)
    ident, negU, negones, ones_bf, perm = (cbf[:, i, :] for i in range(5))

    if need_ada:
        cvec = ar.alloc(F32, 8)
        normg = ar.alloc(F32, DEPTH, 8)
        finalg = ar.alloc(F32, 8)
        bada = ar.alloc(F32, DEPTH, 24)
        for a, d_ in ((cvec, dr["cvec"]), (normg, dr["normg"]), (finalg, dr["finalg"]), (bada, dr["bada"])):
            kb.dma(a, d_, writes=[SMALL], stream=st_const)
        cact = ar.alloc(F32, 8)
        ctmp = ar.alloc(F32, 8)
        mod = ar.alloc(F32, DEPTH, 24)
        gs = ar.alloc(F32, DEPTH, 8)
        MOD = Buf("mod")
        CACT = Buf("cact")
    CBF.w = (st_const, st_const.count)
    SMALL.w = (st_const, st_const.count)

    arena_mark = ar.off
    xt = XT = st_x = sq = SQ = lnv = LNV = rstd = RSTD = ntmp = NTMP = hto = HTO = st_ho = st_out = None
    wout_bf = WOUT = wst = WST = st_w = ymt = YMT = st_ym = None

    def alloc_rl(tag):
        nonlocal xt, XT, st_x, sq, SQ, lnv, LNV, rstd, RSTD, ntmp, NTMP, hto, HTO, st_ho, st_out
        nonlocal wout_bf, WOUT, wst, WST, st_w, ymt, YMT, st_ym
        ar.off = arena_mark
        xt = [ar.alloc(F32, 8, NT) for _ in range(2)]
        XT = [Buf("xt0"), Buf("xt1")]
        st_x = [kb.stream("x0" + tag), kb.stream("x1" + tag)]
        sq = ar.alloc(BF16, 8, NT)
        SQ = Buf("sq")
        lnv = ar.alloc(F32, NT)
        LNV = Buf("lnv")
        rstd = ar.alloc(F32, NT)
        RSTD = Buf("rstd")
        ntmp = [ar.alloc(F32, NT) for _ in range(2)]
        NTMP = [Buf("ntmp0"), Buf("ntmp1")]
        hto = [ar.alloc(BF16, 8, NT) for _ in range(2)]
        HTO = [Buf("hto0"), Buf("hto1")]
        st_ho = [kb.stream("ho0" + tag), kb.stream("ho1" + tag)]
        st_out = [kb.stream("out0" + tag), kb.stream("out1" + tag)]
        if need_p3:
            wout_bf = ar.alloc(BF16, 8, D)
            WOUT = Buf("wout")
            wst = [ar.alloc(F32, D) for _ in range(2)]
            WST = [Buf("wst0"), Buf("wst1")]
            st_w = [kb.stream("w0" + tag), kb.stream("w1" + tag)]
            ymt = [ar.alloc(BF16, 8, NT) for _ in range(2)]
            YMT = [Buf("ymt0"), Buf("ymt1")]
            st_ym = [kb.stream("ym0" + tag), kb.stream("ym1" + tag)]

    if need_ada:
        alloc_rl("a")
    final_streams = []

    def emit_silu_c():
        kb.act(ctmp, cvec, AF.Exp, [SMALL], [CACT], scale=-1.0)
        kb.act(ctmp, ctmp, AF.Ln, [CACT], [CACT], bias=1.0)
        kb.act(ctmp, ctmp, AF.Exp, [CACT], [CACT], scale=-1.0)
        kb.op("dve", "tensor_tensor", [CACT, SMALL], [CACT], out=cact, in0=ctmp, in1=cvec, op=ALU.mult)

    def emit_adaln(l):
        wv = dr["wada"][l].rearrange("(k p) f -> p k f", p=128)
        for grp in range(6):
            sl = grp % 2
            kb.dma(xt[sl], wv[:, :, grp * 512:(grp + 1) * 512], writes=[XT[sl]], stream=st_x[sl])
            for j in range(4):
                fb = grp * 4 + j
                for kc in range(8):
                    kb.mm(banks[0][:, fb:fb + 1], xt[sl][:, kc, j * 128:(j + 1) * 128], cact[:, kc:kc + 1],
                          kc == 0, kc == 7, [XT[sl], CACT], [BK[0]])
        kb.op("dve", "tensor_tensor", [BK[0], SMALL], [MOD], out=mod[:, l, :], in0=banks[0][:, 0:24], in1=bada[:, l, :], op=ALU.add)
        kb.op("dve", "scalar_tensor_tensor", [MOD, SMALL], [MOD], out=gs[:, l, :], in0=mod[:, l, 8:16], scalar=1.0,
              in1=normg[:, l, :], op0=ALU.add, op1=ALU.mult)

    def emit_norm_tile(sl, scale_ap, shift_ap, dst_view, final, after=None, DST=()):
        kb.act(sq, xt[sl], AF.Square, [XT[sl]], [SQ])
        for kc in range(8):
            kb.mm(banks[0][:, :], ones_bf, sq[:, kc, :], kc == 0, kc == 7, [SQ, CBF], [BK[0]])
        kb.act(lnv, banks[0][:, :], AF.Ln, [BK[0]], [LNV], scale=1.0 / D, bias=EPS)
        kb.act(rstd, lnv, AF.Exp, [LNV], [RSTD], scale=-0.5)
        if final:
            for kc in range(8):
                kb.op("dve", "scalar_tensor_tensor", [XT[sl], RSTD, MOD, SMALL], [XT[sl]], out=xt[sl][:, kc, :], in0=xt[sl][:, kc, :],
                      scalar=scale_ap[:, kc:kc + 1], in1=rstd, op0=ALU.mult, op1=ALU.mult)
            kb.dma(dst_view, xt[sl], reads=[XT[sl]], stream=st_out[sl])
        else:
            for kc in range(8):
                tm = kc % 2
                kb.op("dve", "scalar_tensor_tensor", [XT[sl], RSTD, MOD], [NTMP[tm]], out=ntmp[tm], in0=xt[sl][:, kc, :],
                      scalar=scale_ap[:, kc:kc + 1], in1=rstd, op0=ALU.mult, op1=ALU.mult)
                kb.act(hto[sl][:, kc, :], ntmp[tm], AF.Identity, [NTMP[tm], MOD], [HTO[sl]], bias=shift_ap[:, kc:kc + 1])
            kb.dma(dst_view, hto[sl], reads=[HTO[sl]] + list(DST), stream=st_ho[sl])
            if after is not None:
                after()

    def pkt(dram_ap):
        return dram_ap.rearrange("(k p) t -> p k t", p=128)

    def emit_phase1(l, x_src, h_view, h_after=None, H_DST=None):
        xv = pkt(x_src)
        for tt in range(TQ // NT):
            sl = tt % 2
            kb.dma(xt[sl], xv[:, :, tt * NT:(tt + 1) * NT], writes=[XT[sl]], stream=st_x[sl])
            emit_norm_tile(sl, gs[:, l, :], mod[:, l, 0:8], h_view(tt), final=False,
                           after=(None if h_after is None else (lambda tt=tt: h_after(tt))),
                           DST=([] if H_DST is None else [H_DST[tt]]))

    def emit_phase3(l, x_src, ym_view, x_dst, h_view, out_dst, h_after=None, H_DST=None, YM_SRC=()):
        for j in range(8):
            g_, half = j // 2, j % 2
            r0 = g_ * 128 if half == 0 else 512 + g_ * 128
            sl = j % 2
            kb.dma(wst[sl], dr["wout"][l, r0:r0 + 128, :], writes=[WST[sl]], stream=st_w[sl])
            kb.op("pool", "tensor_copy", [WST[sl]], [WOUT], out=wout_bf[:, j, :], in_=wst[sl])
        xv = pkt(x_src)
        for tt in range(TQ // NT):
            sl = tt % 2
            kb.dma(xt[sl], xv[:, :, tt * NT:(tt + 1) * NT], writes=[XT[sl]], stream=st_x[sl])
            kb.dma(ymt[sl], ym_view(tt), reads=list(YM_SRC), writes=[YMT[sl]], stream=st_ym[sl])
            for fo in range(8):
                bk = 1 + (fo % 2)
                for j in range(8):
                    kb.mm(banks[bk][:, :], wout_bf[:, j, fo * 128:(fo + 1) * 128], ymt[sl][:, j, :], j == 0, j == 7,
                          [WOUT, YMT[sl]], [BK[bk]])
                kb.op("dve", "scalar_tensor_tensor", [BK[bk], MOD, XT[sl]], [XT[sl]], out=xt[sl][:, fo, :], in0=banks[bk][:, :],
                      scalar=mod[:, l, 16 + fo:17 + fo], in1=xt[sl][:, fo, :], op0=ALU.mult, op1=ALU.add)
            if l + 1 < DEPTH:
                kb.dma(pkt(x_dst)[:, :, tt * NT:(tt + 1) * NT], xt[sl], reads=[XT[sl]], stream=st_out[sl])
                emit_norm_tile(sl, gs[:, l + 1, :], mod[:, l + 1, 0:8], h_view(tt), final=False,
                               after=(None if h_after is None else (lambda tt=tt: h_after(tt))),
                               DST=([] if H_DST is None else [H_DST[tt]]))
            else:
                emit_norm_tile(sl, finalg, None, pkt(out_dst)[:, :, tt * NT:(tt + 1) * NT], final=True)

    def emit_phase2(l, h_view, ym_view, H_SRC=None, Y_DST=None, y_after=None):
        ar.off = arena_mark
        win_bf = ar.alloc(BF16, 8, 1024)
        WIN = Buf("win")
        off_wst2 = ar.off
        wst2 = [ar.alloc(F32, 1024) for _ in range(2)]
        WST2 = [Buf("wst2_0"), Buf("wst2_1")]
        st_w2 = [kb.stream("w2_0_%d" % l), kb.stream("w2_1_%d" % l)]
        negmask = ar.alloc(BF16, 4, NT)
        retc = ar.alloc(F32, 2 * NT + 2)
        P2C = Buf("p2c")
        st_c2 = kb.stream("c2_%d" % l)
        kb.dma(negmask, dr["negmask"], writes=[P2C], stream=st_c2)
        kb.dma(retc, dr["retc"], writes=[P2C], stream=st_c2)
        dmask4 = retc[:, 0:NT]
        kdec = retc[:, NT:NT + 1]
        cdec = retc[:, NT + 1:NT + 2]
        qdec = retc[:, NT + 2:2 * NT + 2]

        qT = ar.alloc(BF16, S)
        kTA = ar.alloc(BF16, S)
        kTB = ar.alloc(BF16, S)
        sgT = ar.alloc(BF16, S)
        svA = ar.alloc(BF16, S // 128, 128)
        svB = ar.alloc(BF16, S // 128, 128)
        NTILE = S // NT
        QT = [Buf("qT%d" % i) for i in range(NTILE)]
        KTb = [Buf("kT%d" % i) for i in range(NTILE)]
        SG = [Buf("sg%d" % i) for i in range(NTILE)]
        SV = [Buf("sv%d" % i) for i in range(NTILE)]
        ZERO = Buf("zero")
        kb.op("dve", "memset", [], [ZERO], args=(kTA, 0.0))
        kb.op("dve", "memset", [], [ZERO], args=(kTB, 0.0))
        kb.op("dve", "memset", [], [ZERO], args=(svA, 0.0))
        kb.op("dve", "memset", [], [ZERO], args=(svB, 0.0))

        ht = [ar.alloc(BF16, 8, NT) for _ in range(2)]
        HT = [Buf("ht0"), Buf("ht1")]
        st_h = [kb.stream("h0_%d" % l), kb.stream("h1_%d" % l)]
        cs = [ar.alloc(F32, 2, NT) for _ in range(2)]
        CS = [Buf("cs0"), Buf("cs1")]
        st_cs = [kb.stream("cs0_%d" % l), kb.stream("cs1_%d" % l)]

        xbf = [ar.alloc(BF16, NT) for _ in range(2)]
        XBF = [Buf("xbf0"), Buf("xbf1")]
        t1_ = ar.alloc(F32, NT)
        t1 = [t1_, t1_]
        T1_ = Buf("t1")
        T1 = [T1_, T1_]
        t2_ = ar.alloc(F32, NT)
        t2 = [t2_, t2_]
        T2_ = Buf("t2")
        T2 = [T2_, T2_]
        rqT = ar.alloc(BF16, NT)
        rkT = ar.alloc(BF16, NT)
        qdT = ar.alloc(BF16, NT)
        RQ, RK, QD = Buf("rq"), Buf("rk"), Buf("qd")
        sil = ar.alloc(F32, NT)
        SIL = Buf("sil")
        vret = ar.alloc(BF16, 4, 128)
        rgs = ar.alloc(BF16, 4, 128)
        VRET = [Buf("vret%d" % i) for i in range(4)]
        RGS = [Buf("rgs%d" % i) for i in range(4)]
        rgraw = ar.alloc(F32, 4, 128)
        RGRAW = Buf("rgraw")
        kd4 = ar.alloc(BF16, 4, 128)
        KD4 = Buf("kd4")
        Sm4 = ar.alloc(BF16, 4, 128)
        SM4 = Buf("Sm4")
        PfX = [ar.alloc(F32, 9, 128) for _ in range(2)]
        PFX = [Buf("PfX0"), Buf("PfX1")]
        Pb8 = ar.alloc(BF16, 8, 128)
        PB8 = Buf("Pb8")
        stats4 = ar.alloc(F32, 4, 6)
        mv4 = ar.alloc(F32, 4, 2)
        gs4 = ar.alloc(F32, 3, 4)
        GN4 = Buf("gn4")
        on4 = ar.alloc(F32, 4, 128)
        ON4 = Buf("on4")
        ybf4 = ar.alloc(BF16, 4, 128)
        YBF4 = Buf("ybf4")
        yrT = [ar.alloc(BF16, NT) for _ in range(2)]
        YRT = [Buf("yrT0"), Buf("yrT1")]
        st_yr = [kb.stream("yr0_%d" % l), kb.stream("yr1_%d" % l)]

        for kc in range(8):
            sl = kc % 2
            kb.dma(wst2[sl], dr["win"][l, kc * 128:(kc + 1) * 128, :], writes=[WST2[sl]], stream=st_w2[sl])
            kb.op(POOL, "tensor_copy", [WST2[sl]], [WIN], out=win_bf[:, kc, :], in_=wst2[sl])

        kb.op("dve", "memset", [], [PFX[0]], args=(PfX[0][:, 0, :], 0.0))
        pst = 0


        def silu_from_psum(src_ap, SRC, tmp, TMP, dst_ap, DST):
            kb.act(tmp, src_ap, AF.Exp, [SRC], [TMP], scale=-1.0)
            kb.act(tmp, tmp, AF.Ln, [TMP], [TMP], bias=1.0)
            kb.act(tmp, tmp, AF.Exp, [TMP], [TMP], scale=-1.0)
            kb.op("dve", "tensor_tensor", [SRC, TMP], [DST], out=dst_ap, in0=src_ap, in1=tmp, op=ALU.mult)

        def emit_2a_tile(tt):
            nonlocal pst
            sl = tt % 2
            r, tq = tt // 4, (tt % 4) * NT
            c0 = tt * NT
            kb.dma(ht[sl], h_view(tt), reads=([] if H_SRC is None else [H_SRC[tt % 4]]), writes=[HT[sl]], stream=st_h[sl])
            kb.dma(cs[sl][:, 0, :], dr["cosT"][:, c0:c0 + NT], writes=[CS[sl]], stream=st_cs[sl])
            kb.dma(cs[sl][:, 1, :], dr["sinT"][:, c0:c0 + NT], writes=[CS[sl]], stream=st_cs[sl])

            def proj_fm(blk, bk):
                for kc in range(8):
                    kb.mm(banks[bk][:, :], win_bf[:, kc, blk * 128:(blk + 1) * 128], ht[sl][:, kc, :], kc == 0, kc == 7,
                          [WIN, HT[sl]], [BK[bk]])

            for which, (dstT, DST) in enumerate(((rqT, RQ), (rkT, RK))):
                bk = 5 + which
                pk = 6 - which
                proj_fm(which, bk)
                kb.op("dve", "tensor_copy", [BK[bk]], [XBF[which]], out=xbf[which], in_=banks[bk][:, :])
                kb.mm(banks[pk][:, :], perm, xbf[which], True, True, [XBF[which], CBF], [BK[pk]])
                kb.op("dve", "tensor_tensor", [BK[bk], CS[sl]], [T1[which]], out=t1[which], in0=banks[bk][:, :], in1=cs[sl][:, 0, :], op=ALU.mult)
                kb.op("dve", "tensor_tensor", [BK[pk], CS[sl]], [T2[which]], out=t2[which], in0=banks[pk][:, :], in1=cs[sl][:, 1, :], op=ALU.mult)
                kb.op(POOL, "tensor_tensor", [T1[which], T2[which]], [DST], out=dstT, in0=t1[which], in1=t2[which], op=ALU.add)
                if which == 0:
                    kb.op("dve", "tensor_tensor", [T1[0], T2[0]], [T1[0]], out=t1[0], in0=t1[0], in1=t2[0], op=ALU.add)
                    kb.op(POOL, "tensor_tensor", [T1[0], P2C], [QD], out=qdT, in0=t1[0], in1=qdec, op=ALU.mult)
            proj_fm(2, 5)
            kb.op("dve", "tensor_copy", [BK[5]], [QT[tt]], out=qT[:, c0:c0 + NT], in_=banks[5][:, :])
            proj_fm(3, 6)
            kb.op("dve", "tensor_scalar", [BK[6], ZERO], [KTb[tt]], out=kTA[0:64, c0:c0 + NT], in0=banks[6][0:64, :], scalar1=0.125, scalar2=None, op0=ALU.mult)
            kb.op("dve", "tensor_scalar", [BK[6], ZERO], [KTb[tt]], out=kTB[64:128, c0:c0 + NT], in0=banks[6][64:128, :], scalar1=0.125, scalar2=None, op0=ALU.mult)
            proj_fm(4, 5)
            silu_from_psum(banks[5][:, :], BK[5], sil, SIL, sgT[:, c0:c0 + NT], SG[tt])

            for st in range(4):
                gt = tt * 4 + st
                bt = 5 + (st % 2)
                for kc in range(8):
                    kb.mm(banks[bt][:, 0:384], ht[sl][:, kc, st * 128:(st + 1) * 128], win_bf[:, kc, 640:1024], kc == 0, kc == 7,
                          [WIN, HT[sl]], [BK[bt]])
                kb.op("dve", "tensor_copy", [BK[bt]], [VRET[st]], out=vret[:, st, :], in_=banks[bt][:, 0:128])
                kb.op("dve", "tensor_copy", [BK[bt]], [RGRAW], out=rgraw[:, st, :], in_=banks[bt][:, 128:256])
                kb.op("dve", "tensor_copy", [BK[bt], ZERO], [SV[tt]], out=svA[:, gt, 0:64], in_=banks[bt][:, 256:320])
                kb.op("dve", "tensor_copy", [BK[bt], ZERO], [SV[tt]], out=svB[:, gt, 64:128], in_=banks[bt][:, 320:384])
            kb.act(sil, rgraw, AF.Exp, [RGRAW], [SIL], scale=-1.0)
            kb.act(sil, sil, AF.Ln, [SIL], [SIL], bias=1.0)
            kb.act(sil, sil, AF.Exp, [SIL], [SIL], scale=-1.0)
            kb.op("dve", "tensor_tensor", [RGRAW, SIL], [RGS[0]], out=rgs, in0=rgraw, in1=sil, op=ALU.mult)
            for st in range(4):
                kb.op("pe", "transpose", [RK, CBF], [BKT], args=(bankT[:, st * 128:(st + 1) * 128], rkT[:, st * 128:(st + 1) * 128], ident))
            kb.op("dve", "tensor_scalar", [BKT, P2C], [KD4], out=kd4, in0=bankT[:, 0:512], scalar1=kdec, scalar2=None, op0=ALU.mult)
            for st in range(4):
                kb.mm(banks[5][:, st * 128:(st + 1) * 128], rkT[:, st * 128:(st + 1) * 128], rqT[:, st * 128:(st + 1) * 128], True, True,
                      [RK, RQ], [BK[5]])
            kb.op("dve", "tensor_tensor", [BK[5], P2C], [SM4], out=Sm4, in0=banks[5][:, :], in1=dmask4, op=ALU.mult)
            for st in range(4):
                kb.mm(banks[5][:, st * 128:(st + 1) * 128], kd4[0:64, st, :], vret[0:64, st, :], True, True, [KD4, VRET[st]], [BK[5]])
                kb.mm(banks[6][:, st * 128:(st + 1) * 128], kd4[64:128, st, :], vret[64:128, st, :], True, True, [KD4, VRET[st]], [BK[6]])
            pa = pst
            pbn = 1 - pst
            for c in range(8):
                bkv = 5 + (c % 2)
                src_kv = banks[bkv][:, (c // 2) * 128:(c // 2 + 1) * 128]
                if c < 7:
                    kb.op("dve", "scalar_tensor_tensor", [PFX[pa], BK[bkv], P2C], [PFX[pa]], out=PfX[pa][:, c + 1, :], in0=PfX[pa][:, c, :],
                          scalar=cdec, in1=src_kv, op0=ALU.mult, op1=ALU.add)
                else:
                    kb.op("dve", "scalar_tensor_tensor", [PFX[pa], BK[bkv], P2C], [PFX[pbn]], out=PfX[pbn][:, 0, :], in0=PfX[pa][:, c, :],
                          scalar=cdec, in1=src_kv, op0=ALU.mult, op1=ALU.add)
            kb.op("dve", "tensor_copy", [PFX[pa]], [PB8], out=Pb8, in_=PfX[pa][:, 0:8, :])
            pst = pbn
            for st in range(4):
                oc = slice(st * 128, (st + 1) * 128)
                kb.mm(banks[5][:, oc], Sm4[:, st, :], vret[:, st, :], True, False, [SM4, VRET[st]], [BK[5]])
                kb.mm(banks[5][0:64, oc], qdT[:, st * 128:st * 128 + 64], Pb8[:, 2 * st, :], False, True, [QD, PB8], [BK[5]])
                kb.mm(banks[5][64:128, oc], qdT[:, st * 128 + 64:(st + 1) * 128], Pb8[:, 2 * st + 1, :], False, True, [QD, PB8], [BK[5]])
            for st in range(4):
                kb.op("dve", "bn_stats", [BK[5]], [GN4], out=stats4[:, st, :], in_=banks[5][:, st * 128:(st + 1) * 128])
            for st in range(4):
                kb.op("dve", "bn_aggr", [GN4], [GN4], out=mv4[:, st, :], in_=stats4[:, st, :])
            kb.act(gs4[:, 0, :], mv4[:, :, 1], AF.Ln, [GN4], [GN4], bias=EPS)
            kb.act(gs4[:, 1, :], gs4[:, 0, :], AF.Exp, [GN4], [GN4], scale=-0.5)
            kb.op("dve", "scalar_tensor_tensor", [GN4], [GN4], out=gs4[:, 2, :], in0=mv4[:, :, 0], scalar=-1.0, in1=gs4[:, 1, :],
                  op0=ALU.mult, op1=ALU.mult)
            for st in range(4):
                kb.op("dve", "tensor_scalar", [BK[5], GN4], [ON4], out=on4[:, st, :], in0=banks[5][:, st * 128:(st + 1) * 128],
                      scalar1=gs4[:, 1, st:st + 1], scalar2=gs4[:, 2, st:st + 1], op0=ALU.mult, op1=ALU.add)
            kb.op("dve", "tensor_tensor", [ON4, RGS[0]], [YBF4], out=ybf4, in0=on4, in1=rgs, op=ALU.mult)
            for st in range(4):
                kb.op("pe", "transpose", [YBF4, CBF], [BKT], args=(bankT[:, 512 + st * 128:512 + (st + 1) * 128], ybf4[:, st, :], ident))
            kb.op("dve", "tensor_copy", [BKT], [YRT[sl]], out=yrT[sl], in_=bankT[:, 512:1024])
            kb.dma(ym_view(tt, 0), yrT[sl], reads=[YRT[sl]] + ([] if Y_DST is None else [Y_DST[tt // 4]]), stream=st_yr[sl])

        ntile2a = NTILE if DBG['ntile2a'] is None else DBG['ntile2a']

        def capture_tile(tt):
            kb.capture = []
            emit_2a_tile(tt)
            cap = kb.capture
            kb.capture = None
            return cap

        if INTERLEAVE and DBG['do2c']:
            for tt in range(min(LEAD, ntile2a)):
                emit_2a_tile(tt)
        else:
            for tt in range(ntile2a):
                emit_2a_tile(tt)

        NE, NSP, NA = 4, 4, 3
        off_save = ar.off
        ar.off = off_wst2
        Eb = [ar.alloc(F32, NT) for _ in range(NE)]
        ar.off = off_save
        EB = [Buf("E%d" % i) for i in range(NE)]
        SPb = [ar.alloc(BF16, NT) for _ in range(NSP)]
        SPB = [Buf("SP%d" % i) for i in range(NSP)]
        Ab = [ar.alloc(BF16, NT) for _ in range(NA)]
        AB = [Buf("A%d" % i) for i in range(NA)]
        Rb = [[ar.alloc(BF16, NT) for _ in range(2)] for _ in range(2)]
        RB = [[Buf("R%d%d" % (i, j)) for j in range(2)] for i in range(2)]
        ysb = [ar.alloc(BF16, NT) for _ in range(2)]
        YSB = [Buf("ysb0"), Buf("ysb1")]
        st_ys = [kb.stream("ys0_%d" % l), kb.stream("ys1_%d" % l)]
        kTs = (kTA, kTB)
        svs = (svA, svB)

        steps = []
        for qi in range(NTILE):
            blocks = [(4 * qi + o4, o4) for o4 in (3, 2, 1, 0)] + [(kbk, None) for kbk in range(4 * qi - 1, -1, -1)]
            nb = len(blocks)
            for k, (kbk, o4) in enumerate(blocks):
                for hd in range(2):
                    steps.append(dict(qi=qi, hd=hd, k=k, kbk=kbk, o4=o4, first=(k == 0), last=(k == nb - 1)))
        n = len(steps) if DBG['nsteps'] is None else DBG['nsteps']
        if not DBG['do2c']:
            return st_yr

        def emit_Z(s):
            stp = steps[s]
            zb = s % 2
            q0 = stp["qi"] * NT
            kT_ = kTs[stp["hd"]]
            kbk = stp["kbk"]
            diag = stp["o4"] is not None
            kb.mm(banks[zb][:, :], kT_[:, kbk * 128:(kbk + 1) * 128], qT[:, q0:q0 + NT], True, not diag,
                  [KTb[kbk // 4], QT[stp["qi"]], ZERO], [BK[zb]])
            if diag:
                kb.mm(banks[zb][:, :], ident, negmask[:, stp["o4"], :], False, True, [CBF, P2C], [BK[zb]])

        def emit_E(s):
            zb = s % 2
            kb.act(Eb[s % NE], banks[zb][:, :], AF.Exp, [BK[zb]], [EB[s % NE]])

        def emit_SP(s):
            kb.act(SPb[s % NSP], Eb[s % NE], AF.Ln, [EB[s % NE]], [SPB[s % NSP]], bias=1.0)

        def emit_R(s):
            stp = steps[s]
            if stp["last"]:
                return
            hd, k = stp["hd"], stp["k"]
            if stp["first"]:
                kb.op(POOL, "tensor_copy", [SPB[s % NSP]], [RB[hd][1]], out=Rb[hd][1], in_=SPb[s % NSP])
            else:
                kb.op(POOL, "tensor_tensor", [RB[hd][k % 2], SPB[s % NSP]], [RB[hd][(k + 1) % 2]], out=Rb[hd][(k + 1) % 2],
                      in0=Rb[hd][k % 2], in1=SPb[s % NSP], op=ALU.add)

        def emit_PA(s):
            stp = steps[s]
            pb = 2 + (s % 2)
            hd, k = stp["hd"], stp["k"]
            kb.mm(banks[pb][:, :], negU, SPb[s % NSP], True, stp["first"], [CBF, SPB[s % NSP]], [BK[pb]])
            if not stp["first"]:
                kb.mm(banks[pb][:, :], negones, Rb[hd][k % 2], False, True, [CBF, RB[hd][k % 2]], [BK[pb]])

        def emit_A(s):
            pb = 2 + (s % 2)
            kb.act(banks[pb][:, :], banks[pb][:, :], AF.Exp, [BK[pb]], [BK[pb]])
            kb.op("dve", "tensor_tensor", [BK[pb], EB[s % NE]], [AB[s % NA]], out=Ab[s % NA], in0=banks[pb][:, :], in1=Eb[s % NE], op=ALU.mult)

        def emit_AV(s):
            stp = steps[s]
            qi, hd, kbk = stp["qi"], stp["hd"], stp["kbk"]
            ob = 4
            first = stp["first"] and hd == 0
            last = stp["last"] and hd == 1
            kb.mm(banks[ob][:, :], svs[hd][:, kbk, :], Ab[s % NA], first, last, [SV[kbk // 4], ZERO, AB[s % NA]], [BK[ob]])
            if last:
                ys = qi % 2
                q0 = qi * NT
                kb.op("dve", "tensor_tensor", [BK[ob], SG[qi]], [YSB[ys]], out=ysb[ys], in0=banks[ob][:, :], in1=sgT[:, q0:q0 + NT], op=ALU.mult)
                kb.dma(ym_view(qi, 1), ysb[ys], reads=[YSB[ys]] + ([] if Y_DST is None else [Y_DST[qi // 4]]), stream=st_ys[ys])
                if y_after is not None and qi % 4 == 3:
                    y_after(qi // 4)

        pend = []
        state = dict(done=(min(LEAD, ntile2a) - 1) if INTERLEAVE else ntile2a - 1, nxt=min(LEAD, ntile2a), rate=RATE)
        last_of_tile = {}
        first_of_tile = {}
        for i_, stp_ in enumerate(steps[:n]):
            first_of_tile.setdefault(stp_["qi"], i_)
        for i_, stp_ in enumerate(steps[:n]):
            last_of_tile[stp_["qi"]] = i_

        def ensure(qi):
            while state["done"] < min(qi, ntile2a - 1):
                if not pend:
                    pend.extend(capture_tile(state["nxt"]))
                for it in pend:
                    kb.play(it)
                del pend[:]
                state["done"] = state["nxt"]
                state["nxt"] += 1

        def feed(s):
            if not INTERLEAVE:
                return
            if not pend and state["nxt"] < ntile2a:
                pend.extend(capture_tile(state["nxt"]))
            rate = state["rate"]
            k = min(rate, len(pend))
            for it in pend[:k]:
                kb.play(it)
            del pend[:k]
            if k and not pend:
                state["done"] = state["nxt"]
                state["nxt"] += 1

        emit_Z(0)
        emit_E(0)
        if n > 1:
            emit_Z(1)
        for s in range(0, n + 1):
            ensure(steps[min(s + 2, n - 1)]["qi"])
            if s < n:
                emit_SP(s)
                emit_R(s)
                emit_PA(s)
            if 0 <= s - 1 < n:
                emit_A(s - 1)
                emit_AV(s - 1)
            if s + 1 < n:
                emit_E(s + 1)
            if s + 2 < n:
                emit_Z(s + 2)
            feed(s)
        ensure(ntile2a - 1)
        DBG['arena_end'] = ar.off
        return st_yr + st_ys

    def hview_unfused(dram):
        hv = dram.rearrange("(r k p) t -> p r k t", p=128, k=8)
        return lambda tt: hv[:, tt // 4, :, (tt % 4) * NT:(tt % 4 + 1) * NT]

    def yview_unfused(dram):
        return lambda tile, half: dram[half * 128:(half + 1) * 128, tile * NT:(tile + 1) * NT]

    if prog == "P1_0":
        emit_silu_c()
        emit_adaln(0)
        emit_phase1(0, dr["xT"], lambda tt: pkt(dr["hTq"])[:, :, tt * NT:(tt + 1) * NT])
        final_streams += st_ho
    elif prog in ("P2_0", "P2_1"):
        l = int(prog[-1])
        final_streams += emit_phase2(l, hview_unfused(dr["hTfull"]), yview_unfused(dr["ymT"]))
    elif prog == "P3_0":
        emit_silu_c()
        emit_adaln(0)
        emit_adaln(1)
        ymv = dr["ymTfull"].rearrange("(j p) t -> p j t", p=128)
        emit_phase3(0, dr["xT"], lambda tt: ymv[:, :, tt * NT:(tt + 1) * NT], dr["x1T_out"],
                    lambda tt: pkt(dr["hTq"])[:, :, tt * NT:(tt + 1) * NT], None)
        final_streams += st_ho + st_out
    elif prog == "P3_1":
        emit_silu_c()
        emit_adaln(1)
        ymv = dr["ymTfull"].rearrange("(j p) t -> p j t", p=128)
        emit_phase3(1, dr["x1T_in"], lambda tt: ymv[:, :, tt * NT:(tt + 1) * NT], None, None, dr["outT"])
        final_streams += st_out
    elif fused:
        groups = [[0, 1, 2, 3], [4, 5, 6, 7]]
        x1T = nc.dram_tensor("x1T_scr", [D, TQ], F32).ap()
        hsrc = [[nc.dram_tensor("hsrc_%d_%d" % (l, t), [D, NT], BF16) for t in range(4)] for l in range(DEPTH)]
        hgat = [[nc.dram_tensor("hgat_%d_%d" % (l, t), [4 * D, NT], BF16) for t in range(4)] for l in range(DEPTH)]
        ysrc = [[nc.dram_tensor("ysrc_%d_%d" % (l, q), [256, TQ], BF16) for q in range(4)] for l in range(DEPTH)]
        ygat = [nc.dram_tensor("ygat_%d" % l, [4 * 1024, TQ], BF16) for l in range(DEPTH)]
        HS = [[Buf("hs") for _ in range(4)] for _ in range(DEPTH)]
        HG = [[Buf("hg") for _ in range(4)] for _ in range(DEPTH)]
        YS = [[Buf("ys") for _ in range(4)] for _ in range(DEPTH)]
        YG = [Buf("yg") for _ in range(DEPTH)]
        rank_cache = {}

        def myrank(h):
            if id(h) not in rank_cache:
                rank_cache[id(h)] = h.partition_id() % 4
            return rank_cache[id(h)]

        def h_view_dst(l):
            return lambda tt: pkt(hsrc[l][tt].ap())

        def h_after(l):
            def f(tt):
                kb.cc("AllGather", groups, hsrc[l][tt].ap().opt(), hgat[l][tt].ap().opt(), reads=[], writes=[HS[l][tt], HG[l][tt]])
            return f

        def h_view_src(l):
            def f(tt):
                hv = hgat[l][tt % 4].ap().rearrange("(r k p) t -> p r k t", p=128, k=8)
                return hv[:, tt // 4, :, :]
            return f

        def y_view_dst(l):
            return lambda tile, half: ysrc[l][tile // 4].ap()[half * 128:(half + 1) * 128, (tile % 4) * NT:(tile % 4 + 1) * NT]

        def y_after(l):
            def f(q):
                kb.cc("AllGather", groups, ysrc[l][q].ap().opt(), ygat[l].ap()[q * 1024:(q + 1) * 1024, :].opt(),
                      reads=[], writes=[YS[l][q], YG[l]])
            return f

        def y_view_src(l):
            def f(tt):
                def g(h):
                    v = ygat[l].ap().rearrange("(qj p) t -> p qj t", p=128)
                    return v[:, bass.ds(myrank(h) * 8, 8), tt * NT:(tt + 1) * NT]
                return g
            return f

        emit_silu_c()
        emit_adaln(0)
        emit_adaln(1)
        emit_phase1(0, dr["xT"], h_view_dst(0), h_after(0), HS[0])
        for l in range(DEPTH):
            kb.barrier()
            emit_phase2(l, h_view_src(l), y_view_dst(l), H_SRC=HG[l], Y_DST=YS[l], y_after=y_after(l))
            kb.barrier()
            alloc_rl("p3_%d" % l)
            if l + 1 < DEPTH:
                emit_phase3(l, dr["xT"] if l == 0 else x1T, y_view_src(l), x1T, h_view_dst(l + 1), None,
                            h_after=h_after(l + 1), H_DST=HS[l + 1], YM_SRC=[YG[l]])
            else:
                emit_phase3(l, x1T, y_view_src(l), None, None, dr["outT"], YM_SRC=[YG[l]])
        final_streams += st_out
    else:
        raise NotImplementedError(prog)

    kb.final_wait(final_streams)
    kb.replay()
    return nc, es


def _bf(a):
    return np.asarray(a, dtype=np.float32).astype(ml_dtypes.bfloat16)


def _consts():
    ident = np.eye(128, dtype=np.float32)
    jj, ss = np.meshgrid(np.arange(128), np.arange(128), indexing="ij")
    negU = np.where(jj >= ss, -1.0, 0.0).astype(np.float32)
    negones = -np.ones((128, 128), np.float32)
    ones = np.ones((128, 128), np.float32)
    perm = np.zeros((128, 128), np.float32)
    for d in range(128):
        perm[(d + 64) % 128, d] = 1.0
    cbf = _bf(np.stack([ident, negU, negones, ones, perm], axis=1))
    i = np.arange(128)[:, None, None]
    o4 = np.arange(4)[None, :, None]
    j = np.arange(NT)[None, None, :]
    negmask = _bf(np.where(j > o4 * 128 + i, 0.0, -BIG))
    half = 64
    inv = (10000.0 ** (-(np.arange(half, dtype=np.float32) / np.float32(half)))).astype(np.float32)
    pos = np.arange(S, dtype=np.float32)
    ang = (pos[None, :] * inv[:, None]).astype(np.float32)
    cos = np.cos(ang).astype(np.float32)
    sin = np.sin(ang).astype(np.float32)
    cosT = np.concatenate([cos, cos], axis=0)
    sinT = np.concatenate([-sin, sin], axis=0)
    return cbf, negmask, np.ascontiguousarray(cosT), np.ascontiguousarray(sinT)


def _retc(g):
    lg = np.log1p(-(2.0 ** (-5.0 - g)))
    m = np.arange(128)
    same = (m[:, None] // 64) == (m[None, :] // 64)
    dm = np.where(same, np.exp(np.abs(m[:, None] - m[None, :]) * lg), 0.0) * (128.0 ** -0.5)
    kdec = np.exp((63.0 - (m % 64)) * lg) * (128.0 ** -0.5)
    cdec = np.full(128, np.exp(64.0 * lg))
    c = np.arange(NT)
    qdec = np.broadcast_to(np.exp(((c % 64) + 1.0) * lg)[None, :], (128, NT))
    return np.ascontiguousarray(np.concatenate([dm, dm, dm, dm, kdec[:, None], cdec[:, None], qdec], axis=1).astype(np.float32))


def _vec(v):
    return np.ascontiguousarray(np.asarray(v, np.float32).reshape(-1, 128).T)


_NC_CACHE = {}


def _get(prog):
    if prog not in _NC_CACHE:
        _NC_CACHE[prog] = build(prog)
    return _NC_CACHE[prog][0]


def _run(prog, in_maps):
    nc = _get(prog)
    res = run_bass_kernel_spmd(nc, in_maps, core_ids=list(range(NCORES)))
    return res.results


def kernel(x, c, norm_g, w_ada, b_ada, w_in, w_out, final_g):
    x = np.asarray(x, np.float32)
    c = np.asarray(c, np.float32)
    norm_g = np.asarray(norm_g, np.float32)
    w_ada = np.ascontiguousarray(np.asarray(w_ada, np.float32))
    b_ada = np.asarray(b_ada, np.float32)
    w_in = np.asarray(w_in, np.float32)
    w_out = np.ascontiguousarray(np.asarray(w_out, np.float32))
    final_g = np.asarray(final_g, np.float32)

    cbf, negmask, cosT, sinT = _consts()
    normg_l = np.ascontiguousarray(np.stack([_vec(norm_g[l]) for l in range(DEPTH)], axis=1))
    bada_l = np.ascontiguousarray(np.stack([np.asarray(b_ada[l]).reshape(24, 128).T for l in range(DEPTH)], axis=1))
    finalg_l = _vec(final_g)

    base = []
    for core in range(NCORES):
        b, g = core // 4, core % 4
        cols = np.concatenate([
            np.arange(g * 128, (g + 1) * 128),
            512 + np.arange(g * 128, (g + 1) * 128),
            2048 + np.arange(g * 128, (g + 1) * 128),
            2560 + np.arange(g * 128, (g + 1) * 128),
            3584 + np.arange(g * 128, (g + 1) * 128),
            1024 + np.arange(g * 128, (g + 1) * 128),
            1536 + np.arange(g * 128, (g + 1) * 128),
            3072 + np.arange(g * 128, (g + 1) * 128),
        ])
        base.append(dict(
            b=b, g=g,
            cbf=cbf, negmask=negmask, cosT=cosT, sinT=sinT, retc=_retc(g),
            cvec=_vec(c[b]), normg=normg_l, finalg=finalg_l, bada=bada_l, wada=w_ada, wout=w_out,
            win=np.ascontiguousarray(w_in[:, :, cols]),
            xT=np.ascontiguousarray(x[b, g * TQ:(g + 1) * TQ, :].T),
        ))

    def pick(core, names):
        return {k: base[core][k] for k in names}

    P13 = ["cbf", "cvec", "normg", "finalg", "bada", "wada"]
    P2 = ["cbf", "win", "cosT", "sinT", "negmask", "retc"]

    def gather_h(res):
        out = []
        for core in range(NCORES):
            b = core // 4
            out.append(np.ascontiguousarray(np.concatenate([res[b * 4 + r]["hTq"] for r in range(4)], axis=0)))
        return out

    def gather_ym(res):
        out = []
        for core in range(NCORES):
            b, g = core // 4, core % 4
            full = np.concatenate([res[b * 4 + r]["ymT"] for r in range(4)], axis=0)
            out.append(np.ascontiguousarray(full[:, g * TQ:(g + 1) * TQ]))
        return out

    if MODE == "FUSED":
        names = ["cbf", "cvec", "normg", "finalg", "bada", "wada", "wout", "win", "cosT", "sinT", "negmask", "retc", "xT"]
        res = _run("FUSED", [pick(i, names) for i in range(NCORES)])
        out = np.empty((B, S, D), np.float32)
        for core in range(NCORES):
            b, g = core // 4, core % 4
            out[b, g * TQ:(g + 1) * TQ, :] = res[core]["outT"].T
        return out

    r1 = _run("P1_0", [dict(pick(i, P13 + ["xT"])) for i in range(NCORES)])
    hfull = gather_h(r1)
    r2 = _run("P2_0", [dict(pick(i, P2), hTfull=hfull[i]) for i in range(NCORES)])
    ymfull = gather_ym(r2)
    r3 = _run("P3_0", [dict(pick(i, P13 + ["xT", "wout"]), ymTfull=ymfull[i]) for i in range(NCORES)])
    hfull = gather_h(r3)
    r4 = _run("P2_1", [dict(pick(i, P2), hTfull=hfull[i]) for i in range(NCORES)])
    ymfull = gather_ym(r4)
    r5 = _run("P3_1", [dict(pick(i, P13 + ["wout"]), ymTfull=ymfull[i], x1T_in=r3[i]["x1T_out"]) for i in range(NCORES)])

    out = np.empty((B, S, D), np.float32)
    for core in range(NCORES):
        b, g = core // 4, core % 4
        out[b, g * TQ:(g + 1) * TQ, :] = r5[core]["outT"].T
    return out
```
